# Optimizing a Trainium2 kernel written in Bass

```python
import math
import jax, jax.numpy as jnp
from jax import lax
import numpy as np

D_MODEL = 1024
BATCH = 4
SEQ = 4096
DEPTH = 2
DEC_BATCH = 128
DEC_SEQ = 4
PAST_LEN = 2048
PAGE_SIZE = 128

CONV_W = 4
EPS = 1e-6
D_LRU = D_MODEL // 2
LRU_BLOCKS = 8
LRU_BLOCK = D_LRU // LRU_BLOCKS
LRU_C = 8.0
HEAD_DIM = 64
D_ATTN = D_MODEL // 2
N_ATTN_HEADS = D_ATTN // HEAD_DIM
DILATIONS = ((128, 1), (512, 4), (2048, 16))
MAX_WINDOW = 2048
ATTN_BLOCK = 128
D_SSD = D_MODEL
SSD_HEAD_DIM = 64
SSD_HEADS = D_SSD // SSD_HEAD_DIM
SSD_GROUPS = 2
SSD_STATE = 128
SSD_CHUNK = 128
D_XBC = D_SSD + 2 * SSD_GROUPS * SSD_STATE
D_MIX = D_LRU + D_ATTN + D_SSD
IN_SPLITS = (D_LRU, D_LRU, D_ATTN, D_ATTN, D_ATTN, D_ATTN, D_SSD, D_XBC, SSD_HEADS)
D_IN = sum(IN_SPLITS)

kernel_name = 'hybrid_lru_dilated_attn_ssd_step'


def rmsnorm(x, g):
    xf = x.astype(jnp.float32)
    y = xf * lax.rsqrt(jnp.mean(xf * xf, axis=-1, keepdims=True) + EPS)
    return (y * g.astype(jnp.float32)).astype(x.dtype)


def causal_conv(x, buf, w, b):
    L = x.shape[1]
    xp = jnp.concatenate([buf.astype(x.dtype), x], axis=1)
    y = b.astype(x.dtype) + xp[:, 0:L] * w[0]
    for tap in range(1, CONV_W):
        y = y + xp[:, tap:tap + L] * w[tap]
    return y, xp[:, L:]


def _lin_comb(left, right):
    a1, b1 = left
    a2, b2 = right
    return a1 * a2, a2 * b1 + b2


def rg_lru(x, h0, w_a, b_a, w_x, b_x, lam):
    f32 = jnp.float32
    Bn, L, _ = x.shape
    xf = x.astype(f32)
    xb = xf.reshape(Bn, L, LRU_BLOCKS, LRU_BLOCK)
    r = jax.nn.sigmoid(jnp.einsum('blki,kij->blkj', xb, w_a.astype(f32)).reshape(Bn, L, D_LRU) + b_a.astype(f32))
    i = jax.nn.sigmoid(jnp.einsum('blki,kij->blkj', xb, w_x.astype(f32)).reshape(Bn, L, D_LRU) + b_x.astype(f32))
    log_a = -LRU_C * r * jax.nn.softplus(-lam.astype(f32))
    a = jnp.exp(log_a)
    bt = jnp.sqrt(-jnp.expm1(2.0 * log_a)) * (i * xf)
    bt = bt.at[:, 0].add(a[:, 0] * h0.astype(f32))
    _, hs = lax.associative_scan(_lin_comb, (a, bt), axis=1)
    return hs.astype(x.dtype), hs[:, -1]


def alibi_slopes():
    return jnp.exp2(-8.0 * jnp.arange(1, N_ATTN_HEADS + 1, dtype=jnp.float32) / N_ATTN_HEADS)


def band_attn_prompt(q, k, v, slopes, window, dil):
    f32 = jnp.float32
    Bn, S, H, Dh = q.shape
    span = window // dil
    unit = dil * ATTN_BLOCK
    Sp = -(-S // unit) * unit
    M = Sp // dil
    nb = M // ATTN_BLOCK

    def residues(t):
        t = jnp.pad(t, ((0, 0), (0, Sp - S), (0, 0), (0, 0)))
        t = t.reshape(Bn, M, dil, H, Dh).transpose(0, 2, 1, 3, 4)
        return t.reshape(Bn, dil, nb, ATTN_BLOCK, H, Dh)

    def with_prev(t):
        prev = jnp.pad(t, ((0, 0), (0, 0), (1, 0), (0, 0), (0, 0), (0, 0)))[:, :, :-1]
        return jnp.concatenate([prev, t], axis=3)

    qb = residues(q)
    kk = with_prev(residues(k))
    vv = with_prev(residues(v))
    s = jnp.einsum('brnqhd,brnkhd->brnhqk', qb, kk, preferred_element_type=f32) * (Dh ** -0.5)
    iq = jnp.arange(ATTN_BLOCK)[:, None]
    ik = jnp.arange(2 * ATTN_BLOCK)[None, :]
    j = ATTN_BLOCK + iq - ik
    mk = jnp.arange(nb)[:, None, None] * ATTN_BLOCK + ik - ATTN_BLOCK
    valid = (j >= 0) & (j <= span) & (mk >= 0)
    bias = -slopes[:, None, None] * (j * dil).astype(f32)
    s = jnp.where(valid[None, None, :, None], s + bias[None, None, None], -jnp.inf)
    m = jnp.max(s, axis=-1)
    p = jnp.exp(s - m[..., None])
    l = jnp.sum(p, axis=-1)
    m = jnp.moveaxis(m, -1, -2)
    l = jnp.moveaxis(l, -1, -2)
    o = jnp.einsum('brnhqk,brnkhd->brnqhd', p.astype(v.dtype), vv, preferred_element_type=f32) / l[..., None]

    def merge(t):
        t = t.reshape((Bn, dil, M) + t.shape[4:])
        t = jnp.moveaxis(t, 1, 2).reshape((Bn, Sp) + t.shape[3:])
        return t[:, :S]

    return merge(o), merge(m), merge(l)


def band_attn_sample(q, k_all, v_all, W, slopes, window, dil):
    f32 = jnp.float32
    L = q.shape[1]
    Dh = q.shape[-1]
    span = window // dil
    jd = jnp.arange(span + 1) * dil
    idx = W + jnp.arange(L)[:, None] - jd[None, :]
    valid = idx >= 0
    idx = jnp.maximum(idx, 0)
    kg = k_all[:, idx]
    vg = v_all[:, idx]
    s = jnp.einsum('blhd,bljhd->blhj', q, kg, preferred_element_type=f32) * (Dh ** -0.5)
    s = s - slopes[:, None] * jd.astype(f32)
    s = jnp.where(valid[None, :, None, :], s, -jnp.inf)
    m = jnp.max(s, axis=-1)
    p = jnp.exp(s - m[..., None])
    l = jnp.sum(p, axis=-1)
    o = jnp.einsum('blhj,bljhd->blhd', p.astype(vg.dtype), vg, preferred_element_type=f32) / l[..., None]
    return o, m, l


def combine_dilations(parts):
    m_all = parts[0][1]
    for _, m, _ in parts[1:]:
        m_all = jnp.maximum(m_all, m)
    o0, m0, l0 = parts[0]
    w0 = l0 * jnp.exp(m0 - m_all)
    num = w0[..., None] * o0
    den = w0
    for o, m, l in parts[1:]:
        wgt = l * jnp.exp(m - m_all)
        num = num + wgt[..., None] * o
        den = den + wgt
    return num / den[..., None]


def ssd_chunked(x, dt, A, Bm, Cm, h0):
    Bn, L, H, P = x.shape
    Q = min(SSD_CHUNK, L)
    nc = -(-L // Q)
    pad = nc * Q - L
    rep = H // SSD_GROUPS
    Bh = jnp.repeat(Bm, rep, axis=2)
    Ch = jnp.repeat(Cm, rep, axis=2)

    def padl(t):
        return jnp.pad(t, ((0, 0), (0, pad)) + ((0, 0),) * (t.ndim - 2))

    x = padl(x).reshape(Bn, nc, Q, H, P)
    dt = padl(dt).reshape(Bn, nc, Q, H)
    Bh = padl(Bh).reshape(Bn, nc, Q, H, SSD_STATE)
    Ch = padl(Ch).reshape(Bn, nc, Q, H, SSD_STATE)
    cum = jnp.cumsum(dt * A, axis=2)
    seg = cum[:, :, :, None, :] - cum[:, :, None, :, :]
    causal = jnp.tril(jnp.ones((Q, Q), dtype=bool))
    decay = jnp.exp(jnp.where(causal[None, None, :, :, None], seg, -jnp.inf))
    xdt = x * dt[..., None]
    scores = jnp.einsum('bcihn,bcjhn->bcijh', Ch, Bh) * decay
    y_diag = jnp.einsum('bcijh,bcjhp->bcihp', scores, xdt)
    to_end = jnp.exp(cum[:, :, -1:, :] - cum)
    chunk_states = jnp.einsum('bcjhn,bcjh,bcjhp->bchpn', Bh, to_end, xdt)
    chunk_decay = jnp.exp(cum[:, :, -1, :])

    def step(hc, inp):
        dec, st = inp
        return dec[:, :, None, None] * hc + st, hc

    h_last, h_starts = lax.scan(step, h0, (jnp.moveaxis(chunk_decay, 1, 0), jnp.moveaxis(chunk_states, 1, 0)))
    h_starts = jnp.moveaxis(h_starts, 0, 1)
    y_off = jnp.einsum('bcihn,bchpn->bcihp', Ch, h_starts) * jnp.exp(cum)[..., None]
    y = (y_diag + y_off).reshape(Bn, nc * Q, H, P)[:, :L]
    return y, h_last


def layer(x, lru_conv0, lru_h0, k_buf, v_buf, ssd_conv0, ssd_h0, params):
    (pre_g, post_g, w_in, lru_conv_w, lru_conv_b, lru_w_a, lru_b_a, lru_w_x, lru_b_x, lru_lambda,
     ssd_conv_w, ssd_conv_b, ssd_dt_bias, ssd_a_log, ssd_d, ssd_norm_g, w_out) = params
    f32 = jnp.float32
    Bn, L, _ = x.shape
    h = rmsnorm(x, pre_g)
    u = h @ w_in
    offs = np.cumsum(IN_SPLITS)[:-1].tolist()
    x_lru, g_lru, q, k, v, g_attn, z, xbc, dt_raw = jnp.split(u, offs, axis=-1)

    xc, lru_conv_new = causal_conv(x_lru, lru_conv0, lru_conv_w, lru_conv_b)
    hs, lru_h_new = rg_lru(xc, lru_h0, lru_w_a, lru_b_a, lru_w_x, lru_b_x, lru_lambda)
    y_lru = hs * jax.nn.silu(g_lru)

    qh = q.reshape(Bn, L, N_ATTN_HEADS, HEAD_DIM)
    kh = k.reshape(Bn, L, N_ATTN_HEADS, HEAD_DIM)
    vh = v.reshape(Bn, L, N_ATTN_HEADS, HEAD_DIM)
    slopes = alibi_slopes()
    if k_buf is None:
        parts = [band_attn_prompt(qh, kh, vh, slopes, wd, dl) for (wd, dl) in DILATIONS]
        win = min(MAX_WINDOW, L)
        k_new = kh[:, L - win:]
        v_new = vh[:, L - win:]
    else:
        W = k_buf.shape[1]
        k_all = jnp.concatenate([k_buf.astype(kh.dtype), kh], axis=1)
        v_all = jnp.concatenate([v_buf.astype(vh.dtype), vh], axis=1)
        parts = [band_attn_sample(qh, k_all, v_all, W, slopes, wd, dl) for (wd, dl) in DILATIONS]
        k_new = kh
        v_new = vh
    o = combine_dilations(parts)
    y_attn = o.reshape(Bn, L, D_ATTN).astype(x.dtype) * jax.nn.silu(g_attn)

    xbc_c, ssd_conv_new = causal_conv(xbc, ssd_conv0, ssd_conv_w, ssd_conv_b)
    xbc_c = jax.nn.silu(xbc_c).astype(f32)
    xs, bm, cm = jnp.split(xbc_c, [D_SSD, D_SSD + SSD_GROUPS * SSD_STATE], axis=-1)
    xs = xs.reshape(Bn, L, SSD_HEADS, SSD_HEAD_DIM)
    bm = bm.reshape(Bn, L, SSD_GROUPS, SSD_STATE)
    cm = cm.reshape(Bn, L, SSD_GROUPS, SSD_STATE)
    dt = jax.nn.softplus(dt_raw.astype(f32) + ssd_dt_bias.astype(f32))
    A = -jnp.exp(ssd_a_log.astype(f32))
    ys, ssd_h_new = ssd_chunked(xs, dt, A, bm, cm, ssd_h0.astype(f32))
    ys = ys + ssd_d.astype(f32)[:, None] * xs
    yg = (ys.reshape(Bn, L, D_SSD) * jax.nn.silu(z.astype(f32))).reshape(Bn, L, SSD_GROUPS, D_SSD // SSD_GROUPS)
    yg = yg * lax.rsqrt(jnp.mean(yg * yg, axis=-1, keepdims=True) + EPS)
    y_ssd = (yg.reshape(Bn, L, D_SSD) * ssd_norm_g.astype(f32)).astype(x.dtype)

    mix = jnp.concatenate([y_lru, y_attn, y_ssd], axis=-1) @ w_out
    out = x + rmsnorm(mix, post_g)
    return out, (lru_conv_new, lru_h_new, k_new, v_new, ssd_conv_new, ssd_h_new)


def setup_inputs(seed: int = 0) -> dict:
    key = jax.random.key(seed)
    ks = list(jax.random.split(key, 32))
    f32 = jnp.float32
    kv_win = min(MAX_WINDOW, PAST_LEN)

    def nrm(i, shape, scale):
        return scale * jax.random.normal(ks[i], shape, f32)

    a8 = jax.random.uniform(ks[20], (DEPTH, D_LRU), f32, 0.9, 0.999)
    sig = a8 ** (1.0 / LRU_C)
    dt0 = jnp.exp(jax.random.uniform(ks[21], (DEPTH, SSD_HEADS), f32, math.log(1e-3), math.log(1e-1)))
    return {
        'x_prompt': nrm(0, (BATCH, SEQ, D_MODEL), 1.0),
        'x_sample': nrm(1, (DEC_BATCH, DEC_SEQ, D_MODEL), 1.0),
        'state_lru_conv': nrm(2, (DEPTH, DEC_BATCH, CONV_W - 1, D_LRU), 1.0),
        'state_lru_h': nrm(3, (DEPTH, DEC_BATCH, D_LRU), 0.5),
        'cache_attn_k': nrm(4, (DEPTH, DEC_BATCH, kv_win, N_ATTN_HEADS, HEAD_DIM), 1.0),
        'cache_attn_v': nrm(5, (DEPTH, DEC_BATCH, kv_win, N_ATTN_HEADS, HEAD_DIM), 1.0),
        'state_ssd_conv': nrm(6, (DEPTH, DEC_BATCH, CONV_W - 1, D_XBC), 1.0),
        'state_ssd_h': nrm(7, (DEPTH, DEC_BATCH, SSD_HEADS, SSD_HEAD_DIM, SSD_STATE), 0.1),
        'pre_norm_g': 1.0 + nrm(8, (DEPTH, D_MODEL), 0.05),
        'post_norm_g': 1.0 + nrm(9, (DEPTH, D_MODEL), 0.05),
        'w_in': nrm(10, (DEPTH, D_MODEL, D_IN), D_MODEL ** -0.5),
        'lru_conv_w': nrm(11, (DEPTH, CONV_W, D_LRU), CONV_W ** -0.5),
        'lru_conv_b': nrm(12, (DEPTH, D_LRU), 0.01),
        'lru_w_a': nrm(13, (DEPTH, LRU_BLOCKS, LRU_BLOCK, LRU_BLOCK), LRU_BLOCK ** -0.5),
        'lru_b_a': nrm(14, (DEPTH, D_LRU), 0.01),
        'lru_w_x': nrm(15, (DEPTH, LRU_BLOCKS, LRU_BLOCK, LRU_BLOCK), LRU_BLOCK ** -0.5),
        'lru_b_x': nrm(16, (DEPTH, D_LRU), 0.01),
        'lru_lambda': jnp.log(sig) - jnp.log1p(-sig),
        'ssd_conv_w': nrm(17, (DEPTH, CONV_W, D_XBC), CONV_W ** -0.5),
        'ssd_conv_b': nrm(18, (DEPTH, D_XBC), 0.01),
        'ssd_dt_bias': dt0 + jnp.log(-jnp.expm1(-dt0)),
        'ssd_a_log': jnp.log(jax.random.uniform(ks[22], (DEPTH, SSD_HEADS), f32, 1.0, 16.0)),
        'ssd_d': 1.0 + nrm(23, (DEPTH, SSD_HEADS), 0.1),
        'ssd_norm_g': 1.0 + nrm(24, (DEPTH, D_SSD), 0.05),
        'w_out': nrm(25, (DEPTH, D_MIX, D_MODEL), D_MIX ** -0.5),
    }


def reference(x_prompt, x_sample, state_lru_conv, state_lru_h, cache_attn_k, cache_attn_v, state_ssd_conv, state_ssd_h,
              pre_norm_g, post_norm_g, w_in, lru_conv_w, lru_conv_b, lru_w_a, lru_b_a, lru_w_x, lru_b_x, lru_lambda,
              ssd_conv_w, ssd_conv_b, ssd_dt_bias, ssd_a_log, ssd_d, ssd_norm_g, w_out):
    f32 = jnp.float32
    yp = x_prompt
    ys = x_sample
    prompt_states = []
    sample_states = []
    for li in range(DEPTH):
        params = (pre_norm_g[li], post_norm_g[li], w_in[li], lru_conv_w[li], lru_conv_b[li], lru_w_a[li], lru_b_a[li],
                  lru_w_x[li], lru_b_x[li], lru_lambda[li], ssd_conv_w[li], ssd_conv_b[li], ssd_dt_bias[li],
                  ssd_a_log[li], ssd_d[li], ssd_norm_g[li], w_out[li])
        bp = yp.shape[0]
        yp, st_p = layer(yp,
                         jnp.zeros((bp, CONV_W - 1, D_LRU), yp.dtype),
                         jnp.zeros((bp, D_LRU), f32),
                         None, None,
                         jnp.zeros((bp, CONV_W - 1, D_XBC), yp.dtype),
                         jnp.zeros((bp, SSD_HEADS, SSD_HEAD_DIM, SSD_STATE), f32),
                         params)
        ys, st_s = layer(ys, state_lru_conv[li], state_lru_h[li], cache_attn_k[li], cache_attn_v[li],
                         state_ssd_conv[li], state_ssd_h[li], params)
        prompt_states.append(st_p)
        sample_states.append(st_s)
    lru_conv_p, lru_h_p, k_p, v_p, ssd_conv_p, ssd_h_p = [jnp.stack(c) for c in zip(*prompt_states)]
    lru_conv_s, lru_h_s, k_s, v_s, ssd_conv_s, ssd_h_s = [jnp.stack(c) for c in zip(*sample_states)]
    return (yp, ys, lru_conv_p, lru_conv_s, lru_h_p, lru_h_s, k_p, k_s, v_p, v_s, ssd_conv_p, ssd_conv_s, ssd_h_p, ssd_h_s)
```

```python
import contextlib
import numpy as np
import ml_dtypes
import concourse.bass as bass
import concourse.mybir as mybir
from concourse.bass_utils import run_bass_kernel_spmd

F32 = mybir.dt.float32
BF16 = mybir.dt.bfloat16
AF = mybir.ActivationFunctionType
ALU = mybir.AluOpType
AX = mybir.AxisListType

D = 1024
DIN = 5648
EPS = 1e-6
SELF_SYNC = True


class _Op:
    __slots__ = ("q", "fn", "deps", "sig", "val", "dma")


class Prog:
    QUEUES = ("pe", "act", "dve", "pool", "sp")

    def __init__(self, nc):
        self.nc = nc
        self.ops = []
        self.lastw = {}
        self.rd = {}
        self.dtot = {}

    def add(self, q, fn, R=(), W=(), dma=None):
        op = _Op()
        op.q, op.fn, op.dma, op.sig, op.val = q, fn, dma, False, 0
        deps = set()
        W = list(W) + [k for k in R if k[0] == "ps" and k not in W]
        for k in R:
            w = self.lastw.get(k)
            if w is not None:
                deps.add(w)
        for k in W:
            w = self.lastw.get(k)
            if w is not None:
                deps.add(w)
            for r in self.rd.get(k, ()):
                deps.add(r)
        for k in W:
            self.lastw[k] = op
            self.rd[k] = []
        for k in R:
            self.rd.setdefault(k, []).append(op)
        deps.discard(op)
        op.deps = {d: (self.dtot[d.dma] if d.dma is not None else None) for d in deps}
        for d in deps:
            d.sig = True
        if dma is not None:
            self.dtot[dma] = self.dtot.get(dma, 0) + 16
        self.ops.append(op)
        return op

    def barrier(self):
        last = []
        seen = set()
        for op in reversed(self.ops):
            key = op.dma if op.dma is not None else op.q
            if key in seen:
                continue
            seen.add(key)
            last.append(op)
        for q in self.QUEUES:
            op = _Op()
            op.q, op.dma, op.sig, op.val = q, None, False, 0
            op.fn = lambda e: e.nop()
            op.deps = {d: (self.dtot[d.dma] if d.dma is not None else None) for d in last}
            for d in last:
                d.sig = True
            self.ops.append(op)
        self.lastw = {}
        self.rd = {}

    def emit(self):
        nc = self.nc
        cnt = {q: 0 for q in self.QUEUES}
        dcnt = {}
        for op in self.ops:
            if op.dma is not None:
                dcnt[op.dma] = dcnt.get(op.dma, 0) + 16
                op.val = dcnt[op.dma]
            elif op.sig:
                cnt[op.q] += 1
                op.val = cnt[op.q]
        with contextlib.ExitStack() as st:
            esem = {q: st.enter_context(nc.semaphore("e_" + q)) for q in self.QUEUES}
            dsem = {k: st.enter_context(nc.semaphore("d_%d" % i)) for i, k in enumerate(dcnt)}
            block = st.enter_context(nc.Block())
            ops = self.ops

            def run(q, eng):
                waited = {}
                for op in ops:
                    if op.q != q:
                        continue
                    for d, dv in op.deps.items():
                        if d.dma is not None:
                            sem = dsem[d.dma]
                            val = dv
                        else:
                            if d.q == q and (q == "pe" or not SELF_SYNC):
                                continue
                            sem = esem[d.q]
                            val = d.val
                        if waited.get(sem, 0) >= val:
                            continue
                        eng.wait_ge(sem, val)
                        waited[sem] = val
                    ins = op.fn(eng)
                    if op.dma is not None:
                        ins.then_inc(dsem[op.dma], 16)
                    elif op.sig:
                        ins.then_inc(esem[q], 1)
                if q == "sp":
                    for k, v in dcnt.items():
                        if waited.get(dsem[k], 0) < v:
                            eng.wait_ge(dsem[k], v)
                    for qq in self.QUEUES:
                        if qq != "sp" and cnt[qq] > 0:
                            eng.wait_ge(esem[qq], cnt[qq])

            @block.tensor
            def _(e):
                run("pe", e)

            @block.scalar
            def _(e):
                run("act", e)

            @block.vector
            def _(e):
                run("dve", e)

            @block.gpsimd
            def _(e):
                run("pool", e)

            @block.sync
            def _(e):
                run("sp", e)


class Ctx:
    pass


STOP = None


def build(nc, SEQ, NB, do_sample=True):
    P = Prog(nc)

    class _Stop(Exception):
        pass

    def chk(tag):
        if STOP == tag:
            raise _Stop()

    try:
        _build_body(nc, P, SEQ, NB, chk)
    except _Stop:
        pass
    P.emit()
    return nc


def _build_body(nc, P, SEQ, NB, chk):
    T = 256
    NT = SEQ // T
    NBLK = SEQ // 128
    KWIN = min(2048, SEQ)
    NTOK = NB * 4

    def din(name, shape, dt=F32):
        return nc.dram_tensor(name, list(shape), dt, kind="ExternalInput").ap()

    def dout(name, shape, dt=F32):
        return nc.dram_tensor(name, list(shape), dt, kind="ExternalOutput").ap()

    def dscr(name, shape, dt=F32):
        return nc.dram_tensor(name, list(shape), dt, kind="Internal").ap()

    xp = din("xp", [SEQ, D])
    w_in = din("w_in", [2, D, DIN])
    w_out = din("w_out", [2, 2048, D])
    pp_d = din("pp", [128, 2, 104])
    pbc_d = din("pbc", [2, 128, 2096])
    preg_d = din("preg", [128, 2, 8])
    wbd_d = din("wbd", [128, 2, 2, 4, 128])
    c_ident = din("c_ident", [128, 128])
    c_tri = din("c_tri", [128, 128])
    c_negmask = din("c_negmask", [128, 128], BF16)
    c_mtab = din("c_mtab", [128, 8 * 17 * 128], BF16)
    c_sel = din("c_sel", [128, 64])

    yp = dout("yp", [SEQ, D])
    lru_conv_p = dout("lru_conv_p", [2, 128, 4, 3])
    lru_h_p = dout("lru_h_p", [2, 128, 4])
    k_p = dout("k_p", [2, KWIN, 512])
    v_p = dout("v_p", [2, KWIN, 512])
    ssd_conv_p = dout("ssd_conv_p", [2, 128, 12, 3])
    ssd_h_p = dout("ssd_h_p", [2, 1024, 128])

    xs_d = din("xs", [NTOK, D])
    s_lc = din("s_lc", [2, 128, 4, NB, 3])
    s_lh = din("s_lh", [2, 128, 4, NB])
    s_sc = din("s_sc", [2, 128, 12, NB, 3])
    s_sh = din("s_sh", [2, NB, 1024, 128])
    ck = din("ck", [2, NB, 2048, 512])
    cv = din("cv", [2, NB, 2048, 512])
    c_ms = din("c_ms", [128, 7 * 32], BF16)
    c_mn = din("c_mn", [NTOK, NB * 32], BF16)
    c_E = din("c_E", [16, 8, 128])
    pdt_d = din("pdt", [16, 2])
    pps_d = din("pps", [128, 2, 32])
    ys_o = dout("ys", [NTOK, D])
    lru_conv_s = dout("lru_conv_s", [2, 128, 4, NB, 3])
    lru_h_s = dout("lru_h_s", [2, 128, 4, NB])
    k_s = dout("k_s", [2, NTOK, 512])
    v_s = dout("v_s", [2, NTOK, 512])
    ssd_conv_s = dout("ssd_conv_s", [2, 128, 12, NB, 3])
    ssd_h_s = dout("ssd_h_s", [2, NB, 1024, 128])

    wib = dscr("wib", [2, D, DIN], BF16)
    wob = dscr("wob", [2, 2048, D], BF16)
    out1 = dscr("out1", [SEQ, D])
    resid = dscr("resid", [SEQ, D])

    LIMIT = 229376
    state = {"off": 16640, "n": 0}

    def sb(shape, dt=F32, name=None):
        nbytes = int(np.prod(shape[1:])) * (4 if dt == F32 else 2)
        nbytes = (nbytes + 63) // 64 * 64
        off = state["off"]
        state["off"] += nbytes
        assert state["off"] <= LIMIT, ("SBUF overflow", state["off"])
        state["n"] += 1
        return nc.alloc_sbuf_tensor_at(name or ("t%d" % state["n"]), list(shape), dt, offset=off).ap()

    def mark():
        return state["off"]

    def release(m):
        state["off"] = m

    psb = [nc.alloc_psum_tensor("psb%d" % i, [128, 512], F32).ap() for i in range(8)]

    def MM(out, lhsT, rhs, start, stop, R, W):
        P.add("pe", lambda e: e.matmul(out, lhsT=lhsT, rhs=rhs, start=start, stop=stop,
                                       skip_group_check=True), R, W)

    def TR(out, in_, ident, R, W):
        P.add("pe", lambda e: e.transpose(out, in_, ident), R, W)

    def ACT(out, in_, func, R, W, scale=1.0, bias=None, accum=None):
        def f(e):
            kw = {}
            if bias is not None:
                kw["bias"] = bias
            if accum is not None:
                kw["accum_out"] = accum
            return e.activation(out=out, in_=in_, func=func, scale=scale, **kw)
        P.add("act", f, R, W)

    def TS(q, out, in0, s1, s2, op0, op1, R, W, accum=None):
        def f(e):
            if s2 is None:
                return e.tensor_scalar(out=out, in0=in0, scalar1=s1, scalar2=None, op0=op0)
            if accum is not None:
                return e.tensor_scalar(out=out, in0=in0, scalar1=s1, scalar2=s2, op0=op0, op1=op1,
                                       accum_out=accum)
            return e.tensor_scalar(out=out, in0=in0, scalar1=s1, scalar2=s2, op0=op0, op1=op1)
        P.add(q, f, R, W)

    def TT(q, out, in0, in1, op, R, W):
        P.add(q, lambda e: e.tensor_tensor(out=out, in0=in0, in1=in1, op=op), R, W)

    def STT(q, out, in0, scalar, in1, op0, op1, R, W):
        P.add(q, lambda e: e.scalar_tensor_tensor(out=out, in0=in0, scalar=scalar, in1=in1,
                                                  op0=op0, op1=op1), R, W)

    def CP(q, out, in_, R, W):
        if q == "act":
            P.add("act", lambda e: e.activation(out=out, in_=in_, func=AF.Copy), R, W)
        else:
            P.add(q, lambda e: e.tensor_copy(out=out, in_=in_), R, W)

    def MEMSET(q, ap, val, W):
        P.add(q, lambda e: e.memset(ap, val), (), W)

    def SCAN(out, d0, d1, init, R, W):
        P.add("dve", lambda e: e.tensor_tensor_scan(out=out, data0=d0, data1=d1, initial=init,
                                                    op0=ALU.mult, op1=ALU.add), R, W)

    def RED(out, in_, R, W):
        P.add("dve", lambda e: e.tensor_reduce(out=out, in_=in_, axis=AX.X, op=ALU.add), R, W)

    def RECIP(out, in_, R, W):
        P.add("dve", lambda e: e.reciprocal(out=out, in_=in_), R, W)

    def DMA(out, in_, key, R, W, q="sp", slow=False):
        if slow:
            P.add(q, lambda e: e.dma_start(out=out, in_=in_, allow_slow_non_contiguous=True), R, W, dma=key)
        else:
            P.add(q, lambda e: e.dma_start(out=out, in_=in_), R, W, dma=key)

    ident_f = sb([128, 128])
    ident_b = sb([128, 128], BF16)
    tri_f = sb([128, 128])
    ones_f = sb([128, 128])
    negmask = sb([128, 4, 128], BF16)
    sel65 = sb([128, 64])
    pp = sb([128, 2, 104])
    preg = sb([128, 2, 8])
    wbd = sb([128, 2, 2, 4, 128], BF16)
    c8 = sb([128, 2, 4])
    Aneg = sb([128, 2, 16])
    eps_t = sb([128, 1])

    def lcw(l, c, tap): return pp[:, l, c * 4 + tap: c * 4 + tap + 1]
    def lcb(l, c): return pp[:, l, 16 + c: 17 + c]
    def lba(l, c): return pp[:, l, 20 + c: 21 + c]
    def lbx(l, c): return pp[:, l, 24 + c: 25 + c]
    def lam(l): return pp[:, l, 28:32]
    def scw(l, c, tap): return pp[:, l, 32 + c * 4 + tap: 32 + c * 4 + tap + 1]
    def scb(l, c): return pp[:, l, 80 + c: 81 + c]
    pbc_box = [None]
    def postg(l): return pbc_box[0][:, 0:1024]
    def normg(l): return pbc_box[0][:, 1024:2048]
    def dtb(l): return pbc_box[0][:, 2048:2064]
    def dsk(l): return pbc_box[0][:, 2080:2096]

    m0 = mark()
    stg = sb([128, 2, 2, 4, 128])
    DMA(ident_f, c_ident, "c0", (), [("ident_f",)])
    DMA(tri_f, c_tri, "c1", (), [("tri",)])
    DMA(sel65, c_sel, "c2", (), [("sel",)])
    DMA(pp, pp_d, "c3", (), [("pp",)])
    DMA(preg, preg_d, "c5", (), [("preg",)])
    DMA(stg, wbd_d, "c6", (), [("stg",)])
    for i in range(4):
        DMA(negmask[:, i, :], c_negmask, "c7", (), [("negmask",)])
    CP("dve", ident_b, ident_f, [("ident_f",)], [("ident_b",)])
    CP("dve", wbd, stg, [("stg",)], [("wbd",)])
    MEMSET("pool", ones_f, 1.0, [("ones",)])
    MEMSET("pool", eps_t, EPS, [("eps",)])
    tmp4 = sb([128, 2, 4])
    alg = sb([128, 2, 16])
    for l in range(2):
        DMA(alg[:, l, :], pbc_d[l, :, 2064:2080], "c4", (), [("alg", l)])
    for l in range(2):
        ACT(tmp4[:, l, :], lam(l), AF.Exp, [("pp",)], [("tmp4", l)], scale=-1.0)
        ACT(tmp4[:, l, :], tmp4[:, l, :], AF.Ln, [("tmp4", l), ("ones",)], [("tmp4b", l)], bias=ones_f[:, 0:1])
        TS("dve", c8[:, l, :], tmp4[:, l, :], -8.0, None, ALU.mult, None, [("tmp4b", l)], [("c8", l)])
        ACT(Aneg[:, l, :], alg[:, l, :], AF.Exp, [("alg", l)], [("Aneg0", l)])
        TS("dve", Aneg[:, l, :], Aneg[:, l, :], -1.0, None, ALU.mult, None, [("Aneg0", l)], [("Aneg", l)])
    P.barrier()
    release(m0)
    chk("setup")

    m0 = mark()
    CW = 2824
    wst = [sb([128, CW]) for _ in range(3)]
    wsb = [sb([128, CW], BF16) for _ in range(3)]
    it = 0
    for l in range(2):
        for kc in range(8):
            for cb in range(2):
                s = it % 3
                DMA(wst[s], w_in[l, kc * 128:(kc + 1) * 128, cb * CW:(cb + 1) * CW], ("wst", s), (), [("wst", s)])
                ACT(wsb[s], wst[s], AF.Copy, [("wst", s)], [("wsb", s)], scale=preg[:, l, kc:kc + 1])
                DMA(wib[l, kc * 128:(kc + 1) * 128, cb * CW:(cb + 1) * CW], wsb[s], ("wsbo", s), [("wsb", s)], [("wib",)],
                    q="pool")
                it += 1
    for l in range(2):
        for rc in range(8):
            s = it % 3
            DMA(wst[s][:, 0:2048].rearrange("p (a n) -> p a n", a=2),
                w_out[l, rc * 256:(rc + 1) * 256, :].rearrange("(a p) n -> p a n", p=128), ("wst", s), (), [("wst", s)])
            CP("act", wsb[s][:, 0:2048], wst[s][:, 0:2048], [("wst", s)], [("wsb", s)])
            DMA(wob[l, rc * 256:(rc + 1) * 256, :].rearrange("(a p) n -> p a n", p=128),
                wsb[s][:, 0:2048].rearrange("p (a n) -> p a n", a=2), ("wsbo", s), [("wsb", s)], [("wob",)], q="pool")
            it += 1
    P.barrier()
    release(m0)
    chk("prologue")

    def front_alloc():
        c = Ctx()
        c.xt = sb([128, 2, D])
        c.hb = sb([128, 2, D], BF16)
        c.hT = sb([128, 8, T], BF16)
        c.ssq = sb([128, 2])
        c.rstd = sb([128, 2])
        return c

    def front(c, src, t0, tag):
        for j in range(2):
            DMA(c.xt[:, j, :], src[t0 + j * 128: t0 + (j + 1) * 128, :], ("xt", tag), (), [("xt", j)])
        for j in range(2):
            ACT(c.hb[:, j, :], c.xt[:, j, :], AF.Square, [("xt", j)], [("hb", j), ("ssq", j)], accum=c.ssq[:, j:j + 1])
        ACT(c.rstd, c.ssq, AF.Sqrt, [("ssq", 0), ("ssq", 1)], [("rstd0",)], scale=1.0 / D, bias=eps_t[:, 0:1])
        RECIP(c.rstd, c.rstd, [("rstd0",)], [("rstd",)])
        for j in range(2):
            ACT(c.hb[:, j, :], c.xt[:, j, :], AF.Copy, [("xt", j), ("rstd",)], [("hb", j)], scale=c.rstd[:, j:j + 1])
        for kc in range(8):
            bank = 6 + (kc % 2)
            pt = psb[bank].bitcast(BF16)[:, 0:T]
            for j in range(2):
                TR(pt[:, j * 128:(j + 1) * 128], c.hb[:, j, kc * 128:(kc + 1) * 128], ident_b,
                   [("hb", j)], [("ps", bank)])
            CP("dve" if kc % 2 == 0 else "act", c.hT[:, kc, :], pt, [("ps", bank)], [("hT", kc)])

    HT_ALL = [("hT", kc) for kc in range(8)]

    def load_w(wt, l, col0, ncols, key):
        DMA(wt[:, :, 0:ncols], wib[l].rearrange("(kc p) n -> p kc n", p=128)[:, :, col0:col0 + ncols],
            key, (), [key])

    def proj_fm(wt, wkey, c, hT, ncols_t, col, M, out_ps, pskey):
        for kc in range(8):
            MM(out_ps, wt[:, kc, col:col + M], hT[:, kc, 0:ncols_t], kc == 0, kc == 7,
               [wkey, ("hT", kc)], [pskey])

    def proj_tm(wt, wkey, hT, j, col, N, out_ps, pskey):
        for kc in range(8):
            MM(out_ps, hT[:, kc, j * 128:(j + 1) * 128], wt[:, kc, col:col + N], kc == 0, kc == 7,
               [wkey, ("hT", kc)], [pskey])

    xs_res = sb([NTOK, D])
    E_t = sb([16, 8, 128])
    pdt = sb([16, 2])
    pps = sb([128, 2, 32])
    Aexp = sb([128, 2, 8])
    DMA(xs_res, xs_d, "s0", (), [("xs_res",)])
    DMA(E_t, c_E, "s1", (), [("E",)])
    DMA(pdt, pdt_d, "s2", (), [("pdt",)])
    DMA(pps, pps_d, "s3", (), [("pps",)])
    for l in range(2):
        ACT(Aexp[:, l, :], pps[:, l, 8:16], AF.Exp, [("pps",)], [("Aexp0", l)])
        TS("dve", Aexp[:, l, :], Aexp[:, l, :], -1.0, None, ALU.mult, None, [("Aexp0", l)], [("Aexp", l)])
    P.barrier()
    m_layer = mark()

    for l in range(2):
        src = xp if l == 0 else resid
        release(m_layer)
        KT = sb([128, 4, SEQ], BF16)
        V1 = sb([128, NBLK, 8, 66], BF16)
        mtab = sb([128, 8, 17, 128], BF16)
        DMA(mtab.rearrange("p h d q -> p (h d q)"), c_mtab, "c8", (), [("mtab",)])
        MEMSET("pool", V1[:, :, :, 64:65], 1.0, [("V1ones",)])
        fc = front_alloc()
        wA = [sb([128, 8, 512], BF16) for _ in range(3)]
        woL = sb([128, 4, 512], BF16)
        woA = sb([64, 8, 512], BF16)
        xl = sb([128, 4, T + 3])
        sg = sb([128, 4, T])
        xc = sb([128, T])
        xcb = sb([128, T], BF16)
        gr = sb([128, T])
        gi = sb([128, T])
        ga = sb([128, T])
        gb = sb([128, T])
        hs = sb([128, 4, T])
        hprev = sb([128, 4])
        mixL = sb([128, 4, T], BF16)
        mixA = sb([64, 8, T], BF16)
        QT = sb([128, 4, T], BF16)
        sga = sb([64, 8, T])
        Pt = [sb([128, T], BF16) for _ in range(4)]
        Oacc = sb([65, T])
        rl = sb([64, T])
        t1 = sb([64, T])
        o1 = fc.xt
        MEMSET("pool", xl[:, :, 0:3], 0.0, [("xlh", c) for c in range(4)])
        MEMSET("pool", hprev, 0.0, [("hprev", c) for c in range(4)])
        P.barrier()

        for ti in range(NT):
            t0 = ti * T
            in_win = t0 >= SEQ - KWIN
            front(fc, src, t0, "A")
            chk("A_front")
            load_w(wA[0], l, 0, 512, ("wA", 0))
            load_w(wA[1], l, 512, 512, ("wA", 1))
            load_w(wA[2], l, 1024, 512, ("wA", 2))
            for c in range(8):
                bank = c % 2
                ps = psb[bank][:, 0:T]
                wi = c // 4
                proj_fm(wA[wi], ("wA", wi), c, fc.hT, T, (c % 4) * 128, 128, ps, ("ps", bank))
                if c < 4:
                    CP("act", xl[:, c, 3:3 + T], ps, [("ps", bank)], [("xl", c)])
                else:
                    ACT(sg[:, c - 4, :], ps, AF.Silu, [("ps", bank)], [("sg", c - 4)])
            load_w(wA[0], l, 1536, 512, ("wA", 0))
            load_w(wA[1], l, 2048, 512, ("wA", 1))
            chk("A_lruproj")
            for c in range(4):
                TS("dve", xc, xl[:, c, 0:T], lcw(l, c, 0), lcb(l, c), ALU.mult, ALU.add,
                   [("xl", c), ("xlh", c)], [("xc",)])
                for tap in range(1, 4):
                    STT("dve", xc, xl[:, c, tap:tap + T], lcw(l, c, tap), xc, ALU.mult, ALU.add,
                        [("xl", c), ("xlh", c), ("xc",)], [("xc",)])
                CP("pool", xcb, xc, [("xc",)], [("xcb",)])
                CP("pool", xl[:, c, 0:3], xl[:, c, T:T + 3], [("xl", c), ("xlh", c)], [("xlh", c)])
                bank = 2 + (c % 2) * 2
                MM(psb[bank][:, 0:T], wbd[:, l, 0, c, :], xcb, True, True, [("xcb",)], [("ps", bank)])
                MM(psb[bank + 1][:, 0:T], wbd[:, l, 1, c, :], xcb, True, True, [("xcb",)], [("ps", bank + 1)])
                ACT(gr, psb[bank][:, 0:T], AF.Sigmoid, [("ps", bank)], [("gr",)], bias=lba(l, c))
                ACT(gi, psb[bank + 1][:, 0:T], AF.Sigmoid, [("ps", bank + 1)], [("gi",)], bias=lbx(l, c))
                ACT(ga, gr, AF.Exp, [("gr",)], [("ga",)], scale=c8[:, l, c:c + 1])
                TT("pool", gb, ga, ga, ALU.mult, [("ga",)], [("gb",)])
                TS("pool", gb, gb, -1.0, 1.0, ALU.mult, ALU.add, [("gb",)], [("gb",)])
                ACT(gb, gb, AF.Sqrt, [("gb",)], [("gb",)])
                TT("dve", gi, gi, xc, ALU.mult, [("gi",), ("xc",)], [("gi",)])
                TT("dve", gb, gb, gi, ALU.mult, [("gb",), ("gi",)], [("gb",)])
                SCAN(hs[:, c, :], ga, gb, hprev[:, c:c + 1], [("ga",), ("gb",), ("hprev", c)], [("hs", c)])
                CP("pool", hprev[:, c:c + 1], hs[:, c, T - 1:T], [("hs", c)], [("hprev", c)])
                TT("dve", mixL[:, c, :], hs[:, c, :], sg[:, c, :], ALU.mult, [("hs", c), ("sg", c)], [("mixL", c)])
            chk("A_lru")
            for c in range(4):
                bank = c % 2
                ps = psb[bank][:, 0:T]
                proj_fm(wA[2], ("wA", 2), c, fc.hT, T, c * 128, 128, ps, ("ps", bank))
                CP("act", QT[:, c, :], ps, [("ps", bank)], [("QT", c)])
            chk("A_q")
            load_w(wA[2], l, 2560, 512, ("wA", 2))
            for c in range(4):
                bank = c % 2
                ps = psb[bank][:, 0:T]
                proj_fm(wA[0], ("wA", 0), c, fc.hT, T, c * 128, 128, ps, ("ps", bank))
                CP("dve", KT[:, c, t0:t0 + T], ps, [("ps", bank)], [("KT", ti)])
            chk("A_k")
            if in_win:
                for j in range(2):
                    bank = 2 + j
                    proj_tm(wA[0], ("wA", 0), fc.hT, j, 0, 512, psb[bank], ("ps", bank))
                    CP("act", o1[:, j, 0:512], psb[bank], [("ps", bank)], [("xt", j)])
                    r0 = t0 + j * 128 - (SEQ - KWIN)
                    DMA(k_p[l, r0:r0 + 128, :], o1[:, j, 0:512], ("kst", j), [("xt", j)], [("k_p",)])
            chk("A_ktok")
            for j in range(2):
                bank = 4 + j
                blk = ti * 2 + j
                proj_tm(wA[1], ("wA", 1), fc.hT, j, 0, 512, psb[bank], ("ps", bank))
                chk("A_v1")
                for hh_ in range(8):
                    CP("dve", V1[:, blk, hh_, 0:64], psb[bank][:, hh_ * 64:(hh_ + 1) * 64], [("ps", bank)], [("V1", blk)])
                chk("A_v2")
                if in_win:
                    CP("act", o1[:, j, 512:1024], psb[bank], [("ps", bank), ("V1", blk)], [("xt", j)])
                    chk("A_v3")
                    r0 = t0 + j * 128 - (SEQ - KWIN)
                    DMA(v_p[l, r0:r0 + 128, :], o1[:, j, 512:1024], ("vst", j), [("xt", j)], [("v_p",)])
            chk("A_v")
            for h in range(8):
                bank = h % 2
                ps = psb[bank][0:64, 0:T]
                proj_fm(wA[2], ("wA", 2), h, fc.hT, T, h * 64, 64, ps, ("ps", bank))
                ACT(sga[:, h, :], ps, AF.Silu, [("ps", bank)], [("sga", h)])
            chk("A_attnproj")
            b0 = ti * 2
            units = []
            for h in range(8):
                kbs = [b0, b0 + 1] + list(range(b0 - 1, max(-1, b0 - 17), -1))
                for kb in kbs:
                    units.append((h, kb, kb == kbs[0], kb == kbs[-1]))
            pend = []

            def att_front(un, h, kb):
                pr, pb = h // 2, (h % 2) * 64
                d0 = b0 - kb
                if d0 < 0:
                    c0_, c1_, dl = 128, 256, 0
                elif d0 == 16:
                    c0_, c1_, dl = 0, 128, 16
                else:
                    c0_, c1_, dl = 0, 256, d0
                n = c1_ - c0_
                sbank = un % 4
                sps = psb[sbank][:, 0:n]
                MM(sps, KT[pb:pb + 64, pr, kb * 128:(kb + 1) * 128], QT[pb:pb + 64, pr, c0_:c1_], True, True,
                   [("KT", kb // 2), ("QT", pr)], [("ps", sbank)])
                pt = Pt[un % 4]
                ACT(pt[:, 0:n], sps, AF.Exp, [("ps", sbank)], [("Pt", un % 4)], scale=0.125)
                TT("dve" if un % 3 == 2 else "pool", pt[:, 0:n], pt[:, 0:n],
                   mtab[:, h, dl:dl + n // 128, :].rearrange("p a q -> p (a q)"),
                   ALU.mult, [("Pt", un % 4)], [("Pt", un % 4)])
                return c0_, c1_, n

            def att_pv(un, h, kb, first, last, c0_, c1_, n):
                obank = 4 + (h % 2)
                ops = psb[obank][0:65, 0:T]
                MM(ops[:, c0_:c1_], V1[:, kb, h, 0:65], Pt[un % 4][:, 0:n], first, last,
                   [("Pt", un % 4), ("V1", kb)], [("ps", obank)])
                if last:
                    pend.append([3, lambda: att_finish(h)])

            def att_finish(h):
                obank = 4 + (h % 2)
                ops = psb[obank][0:65, 0:T]
                CP("dve", Oacc, ops, [("ps", obank)], [("Oacc",)])
                lbank = 6 + (h % 2)
                MM(psb[lbank][0:64, 0:T], sel65[0:65, :], Oacc, True, True, [("Oacc",)], [("ps", lbank)])
                RECIP(rl, psb[lbank][0:64, 0:T], [("ps", lbank)], [("rl",)])
                TT("dve", t1, Oacc[0:64, :], rl, ALU.mult, [("Oacc",), ("rl",)], [("t1",)])
                TT("dve", mixA[:, h, :], t1, sga[:, h, :], ALU.mult, [("t1",), ("sga", h)], [("mixA", h)])

            def tick():
                for p_ in pend:
                    p_[0] -= 1
                while pend and pend[0][0] <= 0:
                    pend.pop(0)[1]()

            for un, (h, kb, first, last) in enumerate(units):
                c0_, c1_, n = att_front(un, h, kb)
                pend.append([3, (lambda un=un, h=h, kb=kb, first=first, last=last, c0_=c0_, c1_=c1_, n=n:
                                 att_pv(un, h, kb, first, last, c0_, c1_, n))])
                tick()
            while pend:
                tick()
            chk("A_attn")
            for half in range(2):
                hsl = slice(half * 512, (half + 1) * 512)
                DMA(woL, wob[l, 0:512, hsl].rearrange("(c p) n -> p c n", p=128), ("woL",), (), [("woL",)])
                DMA(woA, wob[l, 512:1024, hsl].rearrange("(h p) n -> p h n", p=64), ("woA",), (), [("woA",)])
                for j in range(2):
                    bank = 2 * j + half
                    ps = psb[bank]
                    for c in range(4):
                        MM(ps, mixL[:, c, j * 128:(j + 1) * 128], woL[:, c, :], c == 0, False,
                           [("mixL", c), ("woL",)], [("ps", bank)])
                    for h in range(8):
                        MM(ps, mixA[:, h, j * 128:(j + 1) * 128], woA[:, h, :], False, h == 7,
                           [("mixA", h), ("woA",)], [("ps", bank)])
                    CP("act" if j == 0 else "dve", o1[:, j, hsl], ps, [("ps", bank)], [("xt", j)])
            for j in range(2):
                DMA(out1[t0 + j * 128: t0 + (j + 1) * 128, :], o1[:, j, :], ("o1", j), [("xt", j)], [("out1", ti)])
        DMA(lru_conv_p[l], xl[:, :, 0:3], ("fin", 0), [("xlh", c) for c in range(4)], [("lcp",)])
        DMA(lru_h_p[l], hprev, ("fin", 1), [("hprev", c) for c in range(4)], [("lhp",)])
        P.barrier()
        chk("A%d" % l)

        release(m_layer)
        fc = front_alloc()
        pbc = sb([128, 2096])
        pbc_box[0] = pbc
        DMA(pbc, pbc_d[l], ("pbc",), (), [("pbc",)])
        wB = [sb([128, 8, 512], BF16) for _ in range(3)]
        wdt = sb([128, 8, 16], BF16)
        woS = sb([128, 8, D], BF16)
        xb = sb([128, 12, T + 3])
        xa2 = [sb([128, T]), sb([128, T])]
        xsT = sb([128, 8, T])
        BT = sb([128, 2, T], BF16)
        CT = sb([128, 2, T], BF16)
        Btok = sb([128, 2, 128], BF16)
        sz = sb([128, 2, D])
        dtt = sb([128, 2, 16])
        dtA = sb([128, 16])
        ncum = sb([128, 16])
        ecum = sb([128, 16])
        wend = sb([128, 16])
        dec = sb([128, 16])
        xs_f = sb([128, D])
        xs_b = sb([128, D], BF16)
        rhsb = sb([128, 16, 128])
        decay = sb([128, 16, 128])
        GT = sb([128, 2, 128])
        MT = sb([128, 16, 128], BF16)
        hS = sb([128, D])
        hSb = sb([128, D], BF16)
        xw = sb([128, D], BF16)
        ya = sb([128, D])
        yb = sb([128, D])
        yt = sb([128, D], BF16)
        gss = sb([128, 2])
        mixS = sb([128, 8, T], BF16)
        o1b = sb([128, 2, D])
        yo = sb([128, 2, D])
        ss2 = sb([128, 2])

        DMA(woS, wob[l, 1024:2048, :].rearrange("(c p) n -> p c n", p=128), ("woS",), (), [("woS",)])
        MEMSET("pool", xb[:, :, 0:3], 0.0, [("xbh", c) for c in range(12)])
        MEMSET("pool", hS, 0.0, [("hS",)])
        P.barrier()

        for ti in range(NT):
            t0 = ti * T
            front(fc, src, t0, "B")
            for j in range(2):
                DMA(o1b[:, j, :], out1[t0 + j * 128: t0 + (j + 1) * 128, :], ("o1b", j), [("out1", ti)], [("o1b", j)])
            load_w(wB[0], l, 3072, 512, ("wB", 0))
            load_w(wB[1], l, 3584, 512, ("wB", 1))
            load_w(wB[2], l, 4096, 512, ("wB", 2))
            load_w(wdt, l, 5632, 16, ("wdt",))
            for j in range(2):
                for half in range(2):
                    bank = 2 * j + half
                    proj_tm(wB[half], ("wB", half), fc.hT, j, 0, 512, psb[bank], ("ps", bank))
                    ACT(sz[:, j, half * 512:(half + 1) * 512], psb[bank], AF.Silu, [("ps", bank)], [("sz", j)])
            load_w(wB[0], l, 4608, 512, ("wB", 0))
            load_w(wB[1], l, 5120, 512, ("wB", 1))
            for c in range(12):
                bank = 4 + c % 2
                ps = psb[bank][:, 0:T]
                wi = (2, 0, 1)[c // 4]
                proj_fm(wB[wi], ("wB", wi), c, fc.hT, T, (c % 4) * 128, 128, ps, ("ps", bank))
                CP("act" if c % 2 == 0 else "dve", xb[:, c, 3:3 + T], ps, [("ps", bank)], [("xb", c)])
            for j in range(2):
                bank = 6 + j
                proj_tm(wdt, ("wdt",), fc.hT, j, 0, 16, psb[bank][:, 0:16], ("ps", bank))
                TT("dve", dtt[:, j, :], psb[bank][:, 0:16], dtb(l), ALU.add, [("ps", bank), ("pbc",)], [("dtt", j)])
                ACT(dtt[:, j, :], dtt[:, j, :], AF.Exp, [("dtt", j)], [("dtt", j)])
                ACT(dtt[:, j, :], dtt[:, j, :], AF.Ln, [("dtt", j)], [("dtt", j)], bias=ones_f[:, 0:1])
            for c in range(12):
                q = "dve"
                xa = xa2[c % 2]
                xak = ("xa", c % 2)
                TS(q, xa, xb[:, c, 0:T], scw(l, c, 0), scb(l, c), ALU.mult, ALU.add,
                   [("xb", c), ("xbh", c)], [xak])
                for tap in range(1, 4):
                    STT(q, xa, xb[:, c, tap:tap + T], scw(l, c, tap), xa, ALU.mult, ALU.add,
                        [("xb", c), ("xbh", c), xak], [xak])
                CP("pool", xb[:, c, 0:3], xb[:, c, T:T + 3], [("xb", c), ("xbh", c)], [("xbh", c)])
                if c < 8:
                    ACT(xsT[:, c, :], xa, AF.Silu, [xak], [("xsT", c)])
                elif c < 10:
                    ACT(BT[:, c - 8, :], xa, AF.Silu, [xak], [("BT", c - 8)])
                else:
                    ACT(CT[:, c - 10, :], xa, AF.Silu, [xak], [("CT", c - 10)])
            for j in range(2):
                js = slice(j * 128, (j + 1) * 128)
                for half in range(2):
                    bank = half
                    for cc in range(4):
                        c = half * 4 + cc
                        TR(psb[bank][:, cc * 128:(cc + 1) * 128], xsT[:, c, js], ident_f, [("xsT", c)], [("ps", bank)])
                    CP("act", xs_f[:, half * 512:(half + 1) * 512], psb[bank], [("ps", bank)], [("xs_f", half)])
                    CP("dve", xs_b[:, half * 512:(half + 1) * 512], psb[bank], [("ps", bank)], [("xs_b", half)])
                bank = 2
                ptb = psb[bank].bitcast(BF16)[:, 0:256]
                for g in range(2):
                    TR(ptb[:, g * 128:(g + 1) * 128], BT[:, g, js], ident_b, [("BT", g)], [("ps", bank)])
                CP("act", Btok.rearrange("p g n -> p (g n)"), ptb, [("ps", bank)], [("Btok",)])
                TT("dve", dtA, dtt[:, j, :], Aneg[:, l, :], ALU.mult, [("dtt", j)], [("dtA",)])
                bank = 3
                MM(psb[bank][:, 0:16], tri_f, dtA, True, True, [("dtA",)], [("ps", bank)])
                MM(psb[bank][:, 16:32], ones_f, dtA, True, True, [("dtA",)], [("ps", bank)])
                TS("dve", ncum, psb[bank][:, 0:16], -1.0, None, ALU.mult, None, [("ps", bank)], [("ncum",)])
                ACT(ecum, psb[bank][:, 0:16], AF.Exp, [("ps", bank)], [("ecum",)])
                ACT(dec, psb[bank][:, 16:32], AF.Exp, [("ps", bank)], [("dec",)])
                TT("dve", wend, psb[bank][:, 16:32], ncum, ALU.add, [("ps", bank), ("ncum",)], [("wend",)])
                ACT(wend, wend, AF.Exp, [("wend",)], [("wend",)])
                TT("dve", wend, wend, dtt[:, j, :], ALU.mult, [("wend",), ("dtt", j)], [("wend",)])
                TT("pool", rhsb, tri_f.unsqueeze(1).to_broadcast([128, 16, 128]),
                   dtA.unsqueeze(2).to_broadcast([128, 16, 128]), ALU.mult, [("dtA",)], [("rhsb",)])
                bank = 2
                for g in range(2):
                    MM(psb[bank][:, 256 + g * 128:256 + (g + 1) * 128], BT[:, g, js], CT[:, g, js], True, True,
                       [("BT", g), ("CT", g)], [("ps", bank)])
                CP("act", GT.rearrange("p g n -> p (g n)"), psb[bank][:, 256:512], [("ps", bank)], [("GT",)])
                for q4 in range(4):
                    bank = 4 + q4
                    MM(psb[bank], ones_f, rhsb[:, q4 * 4:(q4 + 1) * 4, :].rearrange("p h i -> p (h i)"), True, False,
                       [("rhsb",)], [("ps", bank)])
                    MM(psb[bank], ident_b, negmask.rearrange("p a i -> p (a i)"), False, True, (), [("ps", bank)])
                    for hh in range(4):
                        h = q4 * 4 + hh
                        ACT(decay[:, h, :], psb[bank][:, hh * 128:(hh + 1) * 128], AF.Exp, [("ps", bank), ("ncum",)],
                            [("decay", h)], bias=ncum[:, h:h + 1])
                        STT("dve", MT[:, h, :], decay[:, h, :], dtt[:, j, h:h + 1], GT[:, h // 8, :], ALU.mult, ALU.mult,
                            [("decay", h), ("dtt", j), ("GT",)], [("MT", h)])
                CP("pool", hSb, hS, [("hS",)], [("hSb",)])
                for h in range(16):
                    bank = h // 8
                    MM(psb[bank][:, (h % 8) * 64:(h % 8 + 1) * 64], MT[:, h, :], xs_b[:, h * 64:(h + 1) * 64], True, True,
                       [("MT", h), ("xs_b", h // 8)], [("ps", bank)])
                for g in range(2):
                    bank = 2 + g
                    MM(psb[bank], CT[:, g, js], hSb[:, g * 512:(g + 1) * 512], True, True, [("CT", g), ("hSb",)],
                       [("ps", bank)])
                for g in range(2):
                    gs = slice(g * 512, (g + 1) * 512)
                    TT("dve", ya[:, gs].rearrange("p (h d) -> p h d", h=8), psb[2 + g].rearrange("p (h d) -> p h d", h=8),
                       ecum[:, g * 8:(g + 1) * 8].unsqueeze(2).to_broadcast([128, 8, 64]), ALU.mult,
                       [("ps", 2 + g), ("ecum",)], [("ya", g)])
                    TT("dve", ya[:, gs], ya[:, gs], psb[g], ALU.add, [("ya", g), ("ps", g)], [("ya", g)])
                    TT("pool", yb[:, gs].rearrange("p (h d) -> p h d", h=8), xs_f[:, gs].rearrange("p (h d) -> p h d", h=8),
                       dsk(l)[:, g * 8:(g + 1) * 8].unsqueeze(2).to_broadcast([128, 8, 64]), ALU.mult,
                       [("xs_f", g)], [("yb", g)])
                    TT("dve", ya[:, gs], ya[:, gs], yb[:, gs], ALU.add, [("ya", g), ("yb", g)], [("ya", g)])
                    TT("dve", ya[:, gs], ya[:, gs], sz[:, j, gs], ALU.mult, [("ya", g), ("sz", j)], [("ya", g)])
                    ACT(yb[:, gs], ya[:, gs], AF.Square, [("ya", g), ("yb", g)], [("yb", g), ("gss", g)],
                        accum=gss[:, g:g + 1])
                ACT(gss, gss, AF.Sqrt, [("gss", 0), ("gss", 1)], [("gss2",)], scale=1.0 / 512, bias=eps_t[:, 0:1])
                RECIP(gss, gss, [("gss2",)], [("gss3",)])
                for g in range(2):
                    gs = slice(g * 512, (g + 1) * 512)
                    STT("dve", yt[:, gs], ya[:, gs], gss[:, g:g + 1], normg(l)[:, gs], ALU.mult, ALU.mult,
                        [("ya", g), ("gss3",)], [("yt", g)])
                TT("pool", xw.rearrange("p (h d) -> p h d", h=16), xs_f.rearrange("p (h d) -> p h d", h=16),
                   wend.unsqueeze(2).to_broadcast([128, 16, 64]), ALU.mult, [("xs_f", 0), ("xs_f", 1), ("wend",)], [("xw",)])
                for g in range(2):
                    bank = 4 + g
                    MM(psb[bank], Btok[:, g, :], xw[:, g * 512:(g + 1) * 512], True, True, [("Btok",), ("xw",)], [("ps", bank)])
                TT("pool", hS.rearrange("p (h d) -> p h d", h=16), hS.rearrange("p (h d) -> p h d", h=16),
                   dec.unsqueeze(2).to_broadcast([128, 16, 64]), ALU.mult, [("hS",), ("hSb",), ("dec",)], [("hS",)])
                for g in range(2):
                    gs = slice(g * 512, (g + 1) * 512)
                    TT("dve", hS[:, gs], hS[:, gs], psb[4 + g], ALU.add, [("hS",), ("ps", 4 + g)], [("hS",)])
                for half in range(2):
                    bank = 6 + half
                    ptb = psb[bank].bitcast(BF16)
                    for cc in range(4):
                        c = half * 4 + cc
                        TR(ptb[:, cc * 128:(cc + 1) * 128], yt[:, c * 128:(c + 1) * 128], ident_b, [("yt", c // 4)],
                           [("ps", bank)])
                    CP("act", mixS[:, half * 4:(half + 1) * 4, js], ptb[:, 0:512].rearrange("p (c t) -> p c t", c=4),
                       [("ps", bank)], [("mixS", j)])
            for j in range(2):
                for half in range(2):
                    bank = 2 * j + half
                    hsl = slice(half * 512, (half + 1) * 512)
                    for c in range(8):
                        MM(psb[bank], mixS[:, c, j * 128:(j + 1) * 128], woS[:, c, hsl], c == 0, c == 7,
                           [("mixS", j), ("woS",)], [("ps", bank)])
                    TT("dve", o1b[:, j, hsl], o1b[:, j, hsl], psb[bank], ALU.add, [("o1b", j), ("ps", bank)], [("o1b", j)])
                ACT(yo[:, j, :], o1b[:, j, :], AF.Square, [("o1b", j), ("yo", j)], [("yo", j), ("ss2", j)],
                    accum=ss2[:, j:j + 1])
            ACT(ss2, ss2, AF.Sqrt, [("ss2", 0), ("ss2", 1)], [("ss2b",)], scale=1.0 / D, bias=eps_t[:, 0:1])
            RECIP(ss2, ss2, [("ss2b",)], [("ss2c",)])
            for j in range(2):
                STT("dve", yo[:, j, :], o1b[:, j, :], ss2[:, j:j + 1], postg(l), ALU.mult, ALU.mult,
                    [("o1b", j), ("ss2c",), ("yo", j)], [("yo", j)])
                TT("pool", yo[:, j, :], yo[:, j, :], fc.xt[:, j, :], ALU.add, [("yo", j), ("xt", j)], [("yo", j)])
                dst = resid if l == 0 else yp
                DMA(dst[t0 + j * 128: t0 + (j + 1) * 128, :], yo[:, j, :], ("yo", j), [("yo", j)], [("dst",)])
        DMA(ssd_conv_p[l], xb[:, :, 0:3], ("fin", 2), [("xbh", c) for c in range(12)], [("scp",)])
        for half in range(2):
            for cc in range(4):
                c = half * 4 + cc
                bank = c % 4
                TR(psb[bank][:, 0:128], hS[:, c * 128:(c + 1) * 128], ident_f, [("hS",)], [("ps", bank)])
                CP("act", ya[:, c * 128:(c + 1) * 128], psb[bank][:, 0:128], [("ps", bank)], [("ya", c)])
                DMA(ssd_h_p[l, c * 128:(c + 1) * 128, :], ya[:, c * 128:(c + 1) * 128], ("fin", 3 + c), [("ya", c)], [("shp",)])
        P.barrier()
        chk("B%d" % l)


        release(m_layer)
        N = NTOK
        pbc = sb([128, 2096])
        pbc_box[0] = pbc
        DMA(pbc, pbc_d[l], ("pbc",), (), [("pbc",)])
        hbs = sb([N, D], BF16)
        hTs = sb([128, 8, N], BF16)
        ssq_s = sb([N, 1])
        wS = [sb([128, 8, 512], BF16) for _ in range(3)]
        wdt = sb([128, 8, 16], BF16)
        ms_t = sb([128, 7 * 32], BF16)
        mn_t = sb([N, NB * 32], BF16)
        DMA(ms_t, c_ms, ("ms",), (), [("ms",)])
        DMA(mn_t, c_mn, ("mn",), (), [("mn",)])
        xls = sb([128, 4, NB, 7])
        sgl = sb([128, 4, N])
        xc_s = sb([128, N])
        xcb_s = sb([128, N], BF16)
        gr_s = sb([128, N]); gi_s = sb([128, N]); ga_s = sb([128, N]); gb_s = sb([128, N])
        hs_s = sb([128, 4, NB, 4])
        h0_s = sb([128, 4, NB])
        mixLs = sb([128, 4, N], BF16)
        qTs = sb([128, 4, N], BF16)
        kTs = sb([128, 4, N], BF16)
        ktok = sb([N, 512])
        vtok = sb([N, 512])
        Vn = sb([N, 8, 66], BF16)
        sga_s = sb([64, 8, N])
        Kf = sb([128, 7, 512])
        Vf = sb([128, 7, 512])
        Kb = [sb([128, 7, 512], BF16) for _ in range(2)]
        Vb = [sb([128, 7, 8, 66], BF16) for _ in range(2)]
        KTs = [sb([128, 7, 4, 128], BF16) for _ in range(2)]
        Pse = [sb([128, 7 * 32], BF16) for _ in range(2)]
        Pn = [sb([N, 32], BF16) for _ in range(2)]
        Oacc_s = sb([65, NB * 32])
        rl_s = sb([64, NB * 32])
        mixAs = sb([64, 8, N], BF16)
        xbs = sb([128, 12, NB, 7])
        xa_s = sb([128, N])
        xsTs = sb([128, 8, N])
        BCT = sb([128, 4, N])
        BCtok = sb([N, 512])
        zs = sb([128, 8, N])
        dtT = sb([16, N])
        dtx = sb([128, 8, N])
        decs = sb([128, 8, N])
        xdt = sb([128, 8, N])
        Hs = [sb([128, 8, 128]) for _ in range(2)]
        bcs = [sb([128, 512]) for _ in range(2)]
        tmpa2 = [sb([128, 8, 128]) for _ in range(2)]
        tmpb2 = [sb([128, 8, 128]) for _ in range(2)]
        ysT = sb([128, 8, N])
        ysq = sb([128, 8, N])
        grs = sb([128, 2, N])
        mixSs = sb([128, 8, N], BF16)
        woLs = wS[0][:, 0:4, :]
        woAs = wS[2][0:64, :, :]
        woSs = wS[1]
        o_s = sb([N, D])
        y_s = sb([N, D])
        for b in range(2):
            MEMSET("pool", Vb[b][:, :, :, 64:65], 1.0, [("Vb1", b)])
        MEMSET("pool", Vn[:, :, 64:65], 1.0, [("Vn1",)])
        DMA(xls[:, :, :, 0:3], s_lc[l], ("slc",), (), [("xls_h",)])
        DMA(h0_s, s_lh[l], ("slh",), (), [("h0_s",)])
        DMA(xbs[:, :, :, 0:3], s_sc[l], ("ssc",), (), [("xbs_h",)])
        P.barrier()

        ACT(hbs, xs_res, AF.Square, [("xs_res",)], [("hbs",), ("ssq_s",)], accum=ssq_s[:, 0:1])
        ACT(ssq_s, ssq_s, AF.Sqrt, [("ssq_s",)], [("ssq_s",)], scale=1.0 / D, bias=eps_t[0:N, 0:1])
        RECIP(ssq_s, ssq_s, [("ssq_s",)], [("ssq_s",)])
        ACT(hbs, xs_res, AF.Copy, [("xs_res",), ("ssq_s",)], [("hbs",)], scale=ssq_s[:, 0:1])
        for kc in range(8):
            bank = 6 + (kc % 2)
            pt = psb[bank].bitcast(BF16)[:, 0:N]
            TR(pt, hbs[:, kc * 128:(kc + 1) * 128], ident_b[0:N, 0:N], [("hbs",)], [("ps", bank)])
            CP("dve" if kc % 2 == 0 else "act", hTs[:, kc, :], pt, [("ps", bank)], [("hT", kc)])

        chk("S_front")
        def wl(i, col0, ncols=512):
            load_w(wS[i], l, col0, ncols, ("wS", i))
        wl(0, 0); wl(1, 512); wl(2, 1024)
        load_w(wdt, l, 5632, 16, ("wdt",))
        for c in range(4):
            bank = c % 2
            ps = psb[bank][:, 0:N]
            proj_fm(wS[0], ("wS", 0), c, hTs, N, c * 128, 128, ps, ("ps", bank))
            CP("act", xls[:, c, :, 3:7], ps.rearrange("p (b i) -> p b i", i=4), [("ps", bank)], [("xls", c)])
        wl(0, 1536)
        for c in range(4):
            bank = c % 2
            ps = psb[bank][:, 0:N]
            proj_fm(wS[1], ("wS", 1), c, hTs, N, c * 128, 128, ps, ("ps", bank))
            ACT(sgl[:, c, :], ps, AF.Silu, [("ps", bank)], [("sgl", c)])
        wl(1, 2048)
        for c in range(4):
            bank = c % 2
            ps = psb[bank][:, 0:N]
            proj_fm(wS[2], ("wS", 2), c, hTs, N, c * 128, 128, ps, ("ps", bank))
            CP("act", qTs[:, c, :], ps, [("ps", bank)], [("qTs",)])
        wl(2, 2560)
        for c in range(4):
            bank = c % 2
            ps = psb[bank][:, 0:N]
            proj_fm(wS[0], ("wS", 0), c, hTs, N, c * 128, 128, ps, ("ps", bank))
            CP("act", kTs[:, c, :], ps, [("ps", bank)], [("kTs",)])
        bank = 2
        for kc in range(8):
            MM(psb[bank][0:N, :], hTs[:, kc, :], wS[0][:, kc, :], kc == 0, kc == 7, [("wS", 0), ("hT", kc)], [("ps", bank)])
        CP("act", ktok, psb[bank][0:N, :], [("ps", bank)], [("ktok",)])
        DMA(k_s[l], ktok, ("ktok",), [("ktok",)], [("k_s",)])
        wl(0, 3072)
        bank = 3
        for kc in range(8):
            MM(psb[bank][0:N, :], hTs[:, kc, :], wS[1][:, kc, :], kc == 0, kc == 7, [("wS", 1), ("hT", kc)], [("ps", bank)])
        CP("act", vtok, psb[bank][0:N, :], [("ps", bank)], [("vtok",)])
        CP("dve", Vn[:, :, 0:64], psb[bank][0:N, :].rearrange("p (h d) -> p h d", h=8), [("ps", bank), ("Vn1",)], [("Vn",)])
        DMA(v_s[l], vtok, ("vtok",), [("vtok",)], [("v_s",)])
        wl(1, 3584)
        for h in range(8):
            bank = h % 2
            ps = psb[bank][0:64, 0:N]
            proj_fm(wS[2], ("wS", 2), h, hTs, N, h * 64, 64, ps, ("ps", bank))
            ACT(sga_s[:, h, :], ps, AF.Silu, [("ps", bank)], [("sga_s",)])
        wl(2, 4096)
        for c in range(8):
            bank = c % 2
            ps = psb[bank][:, 0:N]
            wi = c // 4
            proj_fm(wS[wi], ("wS", wi), c, hTs, N, (c % 4) * 128, 128, ps, ("ps", bank))
            ACT(zs[:, c, :], ps, AF.Silu, [("ps", bank)], [("zs",)])
        wl(0, 4608); wl(1, 5120)
        for c in range(12):
            bank = c % 2
            ps = psb[bank][:, 0:N]
            wi = (2, 0, 1)[c // 4]
            proj_fm(wS[wi], ("wS", wi), c, hTs, N, (c % 4) * 128, 128, ps, ("ps", bank))
            CP("act", xbs[:, c, :, 3:7], ps.rearrange("p (b i) -> p b i", i=4), [("ps", bank)], [("xbs", c)])
        bank = 2
        for kc in range(8):
            MM(psb[bank][0:16, 0:N], wdt[:, kc, :], hTs[:, kc, :], kc == 0, kc == 7, [("wdt",), ("hT", kc)], [("ps", bank)])
        ACT(dtT, psb[bank][0:16, 0:N], AF.Exp, [("ps", bank)], [("dtT",)], bias=pdt[:, l:l + 1])
        ACT(dtT, dtT, AF.Ln, [("dtT",)], [("dtT",)], bias=ones_f[0:16, 0:1])
        for c in range(8):
            bank = 4 + c % 2
            MM(psb[bank][:, 0:N], E_t[:, c, :], dtT, True, True, [("dtT",)], [("ps", bank)])
            CP("act", dtx[:, c, :], psb[bank][:, 0:N], [("ps", bank)], [("dtx",)])

        chk("S_proj")
        for c in range(4):
            xv = xc_s.rearrange("p (b i) -> p b i", i=4)
            TS("dve", xv, xls[:, c, :, 0:4], lcw(l, c, 0), lcb(l, c), ALU.mult, ALU.add, [("xls", c), ("xls_h",)], [("xc_s",)])
            for tap in range(1, 4):
                STT("dve", xv, xls[:, c, :, tap:tap + 4], lcw(l, c, tap), xv, ALU.mult, ALU.add,
                    [("xls", c), ("xls_h",), ("xc_s",)], [("xc_s",)])
            CP("pool", xcb_s, xc_s, [("xc_s",)], [("xcb_s",)])
            bank = 2 + (c % 2) * 2
            MM(psb[bank][:, 0:N], wbd[:, l, 0, c, :], xcb_s, True, True, [("xcb_s",)], [("ps", bank)])
            MM(psb[bank + 1][:, 0:N], wbd[:, l, 1, c, :], xcb_s, True, True, [("xcb_s",)], [("ps", bank + 1)])
            ACT(gr_s, psb[bank][:, 0:N], AF.Sigmoid, [("ps", bank)], [("gr_s",)], bias=lba(l, c))
            ACT(gi_s, psb[bank + 1][:, 0:N], AF.Sigmoid, [("ps", bank + 1)], [("gi_s",)], bias=lbx(l, c))
            ACT(ga_s, gr_s, AF.Exp, [("gr_s",)], [("ga_s",)], scale=c8[:, l, c:c + 1])
            TT("pool", gb_s, ga_s, ga_s, ALU.mult, [("ga_s",)], [("gb_s",)])
            TS("pool", gb_s, gb_s, -1.0, 1.0, ALU.mult, ALU.add, [("gb_s",)], [("gb_s",)])
            ACT(gb_s, gb_s, AF.Sqrt, [("gb_s",)], [("gb_s",)])
            TT("dve", gi_s, gi_s, xc_s, ALU.mult, [("gi_s",), ("xc_s",)], [("gi_s",)])
            TT("dve", gb_s, gb_s, gi_s, ALU.mult, [("gb_s",), ("gi_s",)], [("gb_s",)])
            gav = ga_s.rearrange("p (b i) -> p b i", i=4)
            gbv = gb_s.rearrange("p (b i) -> p b i", i=4)
            for i in range(4):
                prev = h0_s[:, c, :] if i == 0 else hs_s[:, c, :, i - 1]
                TT("dve", hs_s[:, c, :, i], gav[:, :, i], prev, ALU.mult, [("ga_s",), ("h0_s",), ("hs_s", c)], [("hs_s", c)])
                TT("dve", hs_s[:, c, :, i], hs_s[:, c, :, i], gbv[:, :, i], ALU.add, [("gb_s",), ("hs_s", c)], [("hs_s", c)])
            TT("dve", mixLs[:, c, :].rearrange("p (b i) -> p b i", i=4), hs_s[:, c, :, :],
               sgl[:, c, :].rearrange("p (b i) -> p b i", i=4), ALU.mult, [("hs_s", c), ("sgl", c)], [("mixLs",)])
        DMA(lru_conv_s[l], xls[:, :, :, 4:7], ("fs", 0), [("xls", c) for c in range(4)], [("lcs",)])
        DMA(lru_h_s[l], hs_s[:, :, :, 3], ("fs", 1), [("hs_s", c) for c in range(4)], [("lhs",)], slow=True)

        chk("S_lru")
        for c in range(12):
            xv = xa_s.rearrange("p (b i) -> p b i", i=4)
            TS("dve", xv, xbs[:, c, :, 0:4], scw(l, c, 0), scb(l, c), ALU.mult, ALU.add, [("xbs", c), ("xbs_h",)], [("xa_s",)])
            for tap in range(1, 4):
                STT("dve", xv, xbs[:, c, :, tap:tap + 4], scw(l, c, tap), xv, ALU.mult, ALU.add,
                    [("xbs", c), ("xbs_h",), ("xa_s",)], [("xa_s",)])
            if c < 8:
                ACT(xsTs[:, c, :], xa_s, AF.Silu, [("xa_s",)], [("xsTs",)])
            else:
                ACT(BCT[:, c - 8, :], xa_s, AF.Silu, [("xa_s",)], [("BCT",)])
        DMA(ssd_conv_s[l], xbs[:, :, :, 4:7], ("fs", 2), [("xbs", c) for c in range(12)], [("scs",)])
        bank = 2
        for c in range(4):
            TR(psb[bank][0:N, c * 128:(c + 1) * 128], BCT[:, c, :], ident_f, [("BCT",)], [("ps", bank)])
        CP("act", BCtok, psb[bank][0:N, :], [("ps", bank)], [("BCtok",)])
        for c in range(8):
            ACT(decs[:, c, :], dtx[:, c, :], AF.Exp, [("dtx",), ("Aexp", l)], [("decs",)], scale=Aexp[:, l, c:c + 1])
        TT("dve", xdt, xsTs, dtx, ALU.mult, [("xsTs",), ("dtx",)], [("xdt",)])

        chk("S_conv")
        far = lambda src, b, kt: src[l, b].rearrange("(m s) d -> m s d", s=16)[32 * kt:32 * kt + 32, 0:4, :]
        for b in range(NB):
            e = b % 2
            for kt in range(3):
                DMA(Kf[:, kt, :], far(ck, b, kt), ("Kf",), (), [("Kf",)])
                DMA(Vf[:, kt, :], far(cv, b, kt), ("Vf",), (), [("Vf",)])
            DMA(Kf[:, 3:7, :], ck[l, b, 1536:2048, :].rearrange("(t p) d -> p t d", p=128), ("Kf",), (), [("Kf",)])
            DMA(Vf[:, 3:7, :], cv[l, b, 1536:2048, :].rearrange("(t p) d -> p t d", p=128), ("Vf",), (), [("Vf",)])
            DMA(Hs[e], s_sh[l, b].rearrange("(c q) n -> q c n", q=128), ("Hs", e), (), [("Hs", e)])
            CP("pool", Kb[e], Kf, [("Kf",)], [("Kb", e)])
            CP("dve", Vb[e][:, :, :, 0:64], Vf.rearrange("p t (h d) -> p t h d", h=8), [("Vf",), ("Vb1", e)], [("Vb", e)])
            chk("S_a1")
            for kt in range(7):
                bank = kt % 2
                pt = psb[bank].bitcast(BF16)[:, 0:512]
                for pr in range(4):
                    TR(pt[:, pr * 128:(pr + 1) * 128], Kb[e][:, kt, pr * 128:(pr + 1) * 128], ident_b, [("Kb", e)], [("ps", bank)])
                CP("act" if kt % 2 == 0 else "dve", KTs[e][:, kt, :, :].rearrange("p a k -> p (a k)"), pt, [("ps", bank)],
                   [("KTs", e)])
            chk("S_a2")
            sbanks = (2, 3)
            for par in range(2):
                sbank = sbanks[par]
                pb = par * 64
                for kt in range(7):
                    for pr in range(4):
                        col = kt * 16 + pr * 4
                        MM(psb[sbank][:, col:col + 4], KTs[e][pb:pb + 64, kt, pr, :],
                           qTs[pb:pb + 64, pr, 4 * b:4 * b + 4], True, True, [("KTs", e), ("qTs",)], [("ps", sbank)])
                for pr in range(4):
                    col = 112 + pr * 4
                    MM(psb[sbank][0:N, col:col + 4], kTs[pb:pb + 64, pr, :],
                       qTs[pb:pb + 64, pr, 4 * b:4 * b + 4], True, True, [("kTs",), ("qTs",)], [("ps", sbank)])
            chk("S_a3")
            for par in range(2):
                sbank = sbanks[par]
                ACT(Pse[e][:, par * 112:(par + 1) * 112], psb[sbank][:, 0:112], AF.Exp, [("ps", sbank)], [("Pse", e)], scale=0.125)
                ACT(Pn[e][:, par * 16:(par + 1) * 16], psb[sbank][0:N, 112:128], AF.Exp, [("ps", sbank)], [("Pn", e)], scale=0.125)
            TT("pool", Pse[e], Pse[e], ms_t, ALU.mult, [("Pse", e)], [("Pse", e)])
            TT("pool", Pn[e], Pn[e], mn_t[:, b * 32:(b + 1) * 32], ALU.mult, [("Pn", e)], [("Pn", e)])
            chk("S_a4")
            obank = 5
            for h in range(8):
                pr, par = h // 2, h % 2
                oc = psb[obank][0:65, b * 32 + h * 4: b * 32 + h * 4 + 4]
                for kt in range(7):
                    col = par * 112 + kt * 16 + pr * 4
                    MM(oc, Vb[e][:, kt, h, 0:65], Pse[e][:, col:col + 4], kt == 0, False,
                       [("Vb", e), ("Pse", e)], [("ps", 5)])
                col = par * 16 + pr * 4
                MM(oc, Vn[:, h, 0:65], Pn[e][:, col:col + 4], False, True, [("Vn",), ("Pn", e)], [("ps", 5)])
            chk("S_att0")
            if e == 0 and b + 1 < NB:
                continue
            els = [b - 1, b] if e == 1 else [b]
            for i in range(4):
                for bx in els:
                    ex = bx % 2
                    H = Hs[ex]
                    tmpa = tmpa2[ex]
                    tmpb = tmpb2[ex]
                    tk = 4 * bx + i
                    bank = 6 + ex
                    MM(psb[bank], ident_f[0:N, tk:tk + 1].to_broadcast([N, 128]), BCtok, True, True, [("BCtok",)], [("ps", bank)])
                    CP("act", bcs[ex], psb[bank], [("ps", bank)], [("bcs", ex)])
                    Hv = H.rearrange("p (g c) n -> p g c n", g=2)
                    TT("pool", H, H, decs[:, :, tk:tk + 1].to_broadcast([128, 8, 128]), ALU.mult, [("Hs", ex), ("decs",)], [("Hs", ex)])
                    TT("pool", tmpa.rearrange("p (g c) n -> p g c n", g=2),
                       bcs[ex][:, 0:256].rearrange("p (g n) -> p g n", g=2).unsqueeze(2).to_broadcast([128, 2, 4, 128]),
                       xdt[:, :, tk:tk + 1].rearrange("p (g c) o -> p g c o", g=2).to_broadcast([128, 2, 4, 128]), ALU.mult,
                       [("bcs", ex), ("xdt",)], [("tmpa", ex)])
                    TT("dve", H, H, tmpa, ALU.add, [("Hs", ex), ("tmpa", ex)], [("Hs", ex)])
                    TT("dve", tmpb.rearrange("p (g c) n -> p g c n", g=2), Hv,
                       bcs[ex][:, 256:512].rearrange("p (g n) -> p g n", g=2).unsqueeze(2).to_broadcast([128, 2, 4, 128]), ALU.mult,
                       [("Hs", ex), ("bcs", ex)], [("tmpb", ex)])
                    RED(ysT[:, :, tk], tmpb, [("tmpb", ex)], [("ysT",)])
            for bx in els:
                ex = bx % 2
                DMA(ssd_h_s[l, bx].rearrange("(c q) n -> q c n", q=128), Hs[ex], ("Hso", ex), [("Hs", ex)], [("shs",)])

        chk("S_loop")
        CP("dve", Oacc_s, psb[5][0:65, 0:NB * 32], [("ps", 5)], [("Oacc_s",)])
        MM(psb[6][0:64, 0:NB * 32], sel65[0:65, :], Oacc_s, True, True, [("Oacc_s",)], [("ps", 6)])
        RECIP(rl_s, psb[6][0:64, 0:NB * 32], [("ps", 6)], [("rl_s",)])
        TT("dve", rl_s, rl_s, Oacc_s[0:64, :], ALU.mult, [("rl_s",), ("Oacc_s",)], [("rl_s",)])
        TT("dve", mixAs.rearrange("p h (b i) -> p h b i", i=4), rl_s.rearrange("p (b h i) -> p h b i", h=8, i=4),
           sga_s.rearrange("p h (b i) -> p h b i", i=4), ALU.mult, [("rl_s",), ("sga_s",)], [("mixAs",)])

        chk("S_an")
        for c in range(8):
            STT("dve", ysT[:, c, :], xsTs[:, c, :], pps[:, l, 16 + c:17 + c], ysT[:, c, :], ALU.mult, ALU.add,
                [("xsTs",), ("ysT",)], [("ysT",)])
        TT("dve", ysT, ysT, zs, ALU.mult, [("ysT",), ("zs",)], [("ysT",)])
        TT("pool", ysq, ysT, ysT, ALU.mult, [("ysT",)], [("ysq",)])
        for g in range(2):
            for cc in range(4):
                MM(psb[7][:, g * N:(g + 1) * N], ones_f, ysq[:, g * 4 + cc, :], cc == 0, cc == 3, [("ysq",)], [("ps", 7)])
        ACT(grs.rearrange("p g n -> p (g n)"), psb[7][:, 0:2 * N], AF.Sqrt, [("ps", 7)], [("grs",)], scale=1.0 / 512,
            bias=eps_t[:, 0:1])
        RECIP(grs, grs, [("grs",)], [("grs",)])
        for c in range(8):
            STT("dve", mixSs[:, c, :], ysT[:, c, :], pps[:, l, 24 + c:25 + c], grs[:, c // 4, :], ALU.mult, ALU.mult,
                [("ysT",), ("grs",)], [("mixSs",)])

        chk("S_so")
        for half in range(2):
            hsl = slice(half * 512, (half + 1) * 512)
            DMA(woLs, wob[l, 0:512, hsl].rearrange("(c p) n -> p c n", p=128), ("wS", 0), (), [("wS", 0)])
            DMA(woAs, wob[l, 512:1024, hsl].rearrange("(h p) n -> p h n", p=64), ("wS", 2), (), [("wS", 2)])
            DMA(woSs, wob[l, 1024:2048, hsl].rearrange("(c p) n -> p c n", p=128), ("wS", 1), (), [("wS", 1)])
            bank = half
            ps = psb[bank][0:N, :]
            for c in range(4):
                MM(ps, mixLs[:, c, :], woLs[:, c, :], c == 0, False, [("mixLs",), ("wS", 0)], [("ps", bank)])
            for h in range(8):
                MM(ps, mixAs[:, h, :], woAs[:, h, :], False, False, [("mixAs",), ("wS", 2)], [("ps", bank)])
            for c in range(8):
                MM(ps, mixSs[:, c, :], woSs[:, c, :], False, c == 7, [("mixSs",), ("wS", 1)], [("ps", bank)])
            CP("act", o_s[:, hsl], ps, [("ps", bank)], [("o_s",)])
        ACT(y_s, o_s, AF.Square, [("o_s",)], [("y_s",), ("ssq_s",)], accum=ssq_s[:, 0:1])
        ACT(ssq_s, ssq_s, AF.Sqrt, [("ssq_s",)], [("ssq_s",)], scale=1.0 / D, bias=eps_t[0:N, 0:1])
        RECIP(ssq_s, ssq_s, [("ssq_s",)], [("ssq_s",)])
        STT("dve", y_s, o_s, ssq_s[:, 0:1], postg(l)[0:N, :], ALU.mult, ALU.mult, [("o_s",), ("ssq_s",), ("y_s",)], [("y_s",)])
        TT("dve", xs_res, xs_res, y_s, ALU.add, [("y_s",), ("xs_res",)], [("xs_res",)])
        if l == 1:
            DMA(ys_o, xs_res, ("ys_o",), [("xs_res",)], [("ys_o",)])
        P.barrier()
        chk("S%d" % l)

    return nc


def host_consts():
    k = np.arange(128)[:, None]
    q = np.arange(128)[None, :]
    ident = np.eye(128, dtype=np.float32)
    tri = (k <= q).astype(np.float32)
    negmask = np.where(k <= q, 0.0, -30000.0).astype(ml_dtypes.bfloat16)
    sel = np.zeros((128, 64), np.float32)
    sel[64, :] = 1.0
    mt = np.zeros((128, 8, 17, 128), np.float64)
    for h in range(8):
        slope = 2.0 ** (-(h + 1))
        for dl in range(17):
            d = 128 * dl + (q - k)
            c = ((d >= 0) & (d <= 128)).astype(np.float64) + ((d >= 0) & (d % 4 == 0) & (d <= 512)) + \
                ((d >= 0) & (d % 16 == 0) & (d <= 2048))
            mt[:, h, dl, :] = c * np.exp(-slope * np.maximum(d, 0))
    mtab = mt.reshape(128, -1).astype(ml_dtypes.bfloat16)
    return dict(c_ident=ident, c_tri=tri, c_negmask=negmask, c_mtab=mtab, c_sel=sel)


def host_params(inp):
    f = np.float32
    pp = np.zeros((128, 2, 104), f)
    pbc = np.zeros((2, 128, 2096), f)
    for l in range(2):
        pp[:, l, 0:16] = inp["lru_conv_w"][l].reshape(4, 4, 128).transpose(2, 1, 0).reshape(128, 16)
        pp[:, l, 16:20] = inp["lru_conv_b"][l].reshape(4, 128).T
        pp[:, l, 20:24] = inp["lru_b_a"][l].reshape(4, 128).T
        pp[:, l, 24:28] = inp["lru_b_x"][l].reshape(4, 128).T
        pp[:, l, 28:32] = inp["lru_lambda"][l].reshape(4, 128).T
        pp[:, l, 32:80] = inp["ssd_conv_w"][l].reshape(4, 12, 128).transpose(2, 1, 0).reshape(128, 48)
        pp[:, l, 80:92] = inp["ssd_conv_b"][l].reshape(12, 128).T
        pbc[l, :, 0:1024] = inp["post_norm_g"][l][None, :]
        pbc[l, :, 1024:2048] = inp["ssd_norm_g"][l][None, :]
        pbc[l, :, 2048:2064] = inp["ssd_dt_bias"][l][None, :]
        pbc[l, :, 2064:2080] = inp["ssd_a_log"][l][None, :]
        pbc[l, :, 2080:2096] = inp["ssd_d"][l][None, :]
    preg = np.ascontiguousarray(inp["pre_norm_g"].reshape(2, 8, 128).transpose(2, 0, 1)).astype(f)
    wbd = np.zeros((128, 2, 2, 4, 128), f)
    for l in range(2):
        for ai, nm in enumerate(("lru_w_a", "lru_w_x")):
            w = inp[nm][l]
            for c in range(4):
                wbd[0:64, l, ai, c, 0:64] = w[2 * c]
                wbd[64:128, l, ai, c, 64:128] = w[2 * c + 1]
    return dict(pp=pp, pbc=pbc, preg=preg, wbd=wbd)


def sample_consts(NB):
    ms = np.zeros((128, 7, 8, 4), np.float64)
    p = np.arange(128)
    for kt in range(7):
        if kt < 3:
            idx = 16 * (32 * kt + p // 4) + (p % 4)
        else:
            idx = 1536 + 128 * (kt - 3) + p
        for h in range(8):
            slope = 2.0 ** (-(h + 1))
            for i in range(4):
                d = 2048 + i - idx
                c = ((d >= 0) & (d <= 128)).astype(np.float64) + ((d >= 0) & (d % 4 == 0) & (d <= 512)) + \
                    ((d >= 0) & (d % 16 == 0) & (d <= 2048))
                ms[:, kt, h, i] = c * np.exp(-slope * np.maximum(d, 0))
    mn = np.zeros((NB * 4, NB, 8, 4), np.float64)
    for b in range(NB):
        for ip in range(4):
            for h in range(8):
                slope = 2.0 ** (-(h + 1))
                for i in range(ip, 4):
                    d = i - ip
                    mn[4 * b + ip, b, h, i] = (3.0 if d == 0 else 1.0) * np.exp(-slope * d)
    E = np.zeros((16, 8, 128), np.float32)
    for c in range(8):
        for hh in range(2):
            E[2 * c + hh, c, hh * 64:(hh + 1) * 64] = 1.0
    ms = ms.reshape(128, 7, 4, 2, 4).transpose(0, 3, 1, 2, 4)
    mn = mn.reshape(NB * 4, NB, 4, 2, 4).transpose(0, 1, 3, 2, 4)
    return dict(c_ms=np.ascontiguousarray(ms).reshape(128, -1).astype(ml_dtypes.bfloat16),
                c_mn=np.ascontiguousarray(mn).reshape(NB * 4, -1).astype(ml_dtypes.bfloat16), c_E=E)


def sample_params(inp):
    f = np.float32
    pdt = np.ascontiguousarray(inp["ssd_dt_bias"].T).astype(f)
    pps = np.zeros((128, 2, 32), f)
    for l in range(2):
        rep = lambda v: np.repeat(v.reshape(8, 2), 64, axis=1).T
        pps[:, l, 0:8] = rep(inp["ssd_dt_bias"][l])
        pps[:, l, 8:16] = rep(inp["ssd_a_log"][l])
        pps[:, l, 16:24] = rep(inp["ssd_d"][l])
        pps[:, l, 24:32] = inp["ssd_norm_g"][l].reshape(8, 128).T
    return dict(pdt=pdt, pps=pps)


def sample_inputs(inp, core, NB):
    f = np.float32
    sl = slice(core * NB, (core + 1) * NB)
    m = {}
    m["xs"] = np.ascontiguousarray(inp["x_sample"][sl].reshape(NB * 4, D), dtype=f)
    m["s_lc"] = np.ascontiguousarray(inp["state_lru_conv"][:, sl].reshape(2, NB, 3, 4, 128).transpose(0, 4, 3, 1, 2), dtype=f)
    m["s_lh"] = np.ascontiguousarray(inp["state_lru_h"][:, sl].reshape(2, NB, 4, 128).transpose(0, 3, 2, 1), dtype=f)
    m["s_sc"] = np.ascontiguousarray(inp["state_ssd_conv"][:, sl].reshape(2, NB, 3, 12, 128).transpose(0, 4, 3, 1, 2), dtype=f)
    m["s_sh"] = np.ascontiguousarray(inp["state_ssd_h"][:, sl].reshape(2, NB, 1024, 128), dtype=f)
    m["ck"] = np.ascontiguousarray(inp["cache_attn_k"][:, sl].reshape(2, NB, 2048, 512), dtype=f)
    m["cv"] = np.ascontiguousarray(inp["cache_attn_v"][:, sl].reshape(2, NB, 2048, 512), dtype=f)
    m.update(sample_consts(NB))
    m.update(sample_params(inp))
    return m


_NC_CACHE = {}


def kernel(**inputs):
    inp = {k: np.asarray(v) for k, v in inputs.items()}
    B, SEQ, _ = inp["x_prompt"].shape
    DB = inp["x_sample"].shape[0]
    n = 8
    NB = DB // n
    nc = bass.Bass("TRN2", target_bir_lowering=False)
    build(nc, SEQ, NB)
    shared = dict(host_consts())
    shared.update(host_params(inp))
    shared["w_in"] = np.ascontiguousarray(inp["w_in"], dtype=np.float32)
    shared["w_out"] = np.ascontiguousarray(inp["w_out"], dtype=np.float32)
    in_maps = []
    for c in range(n):
        m = dict(shared)
        m["xp"] = np.ascontiguousarray(inp["x_prompt"][c % B])
        m.update(sample_inputs(inp, c, NB))
        in_maps.append(m)
    res = run_bass_kernel_spmd(nc, in_maps, core_ids=list(range(n))).results
    o = assemble(res, inp, B, SEQ, DB, n)
    return tuple(o[k] for k in OUT_NAMES)


OUT_NAMES = ["yp", "ys", "lru_conv_p", "lru_conv_s", "lru_h_p", "lru_h_s", "k_p", "k_s", "v_p", "v_s",
             "ssd_conv_p", "ssd_conv_s", "ssd_h_p", "ssd_h_s"]


def assemble(res, inp, B, SEQ, DB, n):
    f = np.float32
    KW = min(2048, SEQ)
    yp = np.stack([res[b]["yp"] for b in range(B)]).astype(f)
    lru_conv_p = np.stack([res[b]["lru_conv_p"].transpose(0, 3, 2, 1).reshape(2, 3, 512) for b in range(B)], 1)
    lru_h_p = np.stack([res[b]["lru_h_p"].transpose(0, 2, 1).reshape(2, 512) for b in range(B)], 1)
    k_p = np.stack([res[b]["k_p"].reshape(2, KW, 8, 64) for b in range(B)], 1)
    v_p = np.stack([res[b]["v_p"].reshape(2, KW, 8, 64) for b in range(B)], 1)
    ssd_conv_p = np.stack([res[b]["ssd_conv_p"].transpose(0, 3, 2, 1).reshape(2, 3, 1536) for b in range(B)], 1)
    ssd_h_p = np.stack([res[b]["ssd_h_p"].reshape(2, 16, 64, 128) for b in range(B)], 1)
    NB = DB // n
    cat = lambda name, ax=0: np.concatenate([res[c][name] for c in range(n)], axis=ax)
    ys = cat("ys").reshape(DB, 4, D)
    lru_conv_s = np.concatenate([res[c]["lru_conv_s"].transpose(0, 3, 4, 2, 1).reshape(2, NB, 3, 512) for c in range(n)], 1)
    lru_h_s = np.concatenate([res[c]["lru_h_s"].transpose(0, 3, 2, 1).reshape(2, NB, 512) for c in range(n)], 1)
    k_s = np.concatenate([res[c]["k_s"].reshape(2, NB, 4, 8, 64) for c in range(n)], 1)
    v_s = np.concatenate([res[c]["v_s"].reshape(2, NB, 4, 8, 64) for c in range(n)], 1)
    ssd_conv_s = np.concatenate([res[c]["ssd_conv_s"].transpose(0, 3, 4, 2, 1).reshape(2, NB, 3, 1536) for c in range(n)], 1)
    ssd_h_s = np.concatenate([res[c]["ssd_h_s"].reshape(2, NB, 16, 64, 128) for c in range(n)], 1)
    outs = dict(yp=yp, ys=ys, lru_conv_p=lru_conv_p, lru_conv_s=lru_conv_s, lru_h_p=lru_h_p, lru_h_s=lru_h_s,
                k_p=k_p, k_s=k_s, v_p=v_p, v_s=v_s, ssd_conv_p=ssd_conv_p, ssd_conv_s=ssd_conv_s,
                ssd_h_p=ssd_h_p, ssd_h_s=ssd_h_s)
    return {k: np.ascontiguousarray(v, dtype=np.float32) for k, v in outs.items()}
```

```python
import contextlib
import numpy as np
import ml_dtypes
import concourse.bass as bass
import concourse.mybir as mybir
from concourse.bass_utils import run_bass_kernel_spmd

F32 = mybir.dt.float32
BF16 = mybir.dt.bfloat16
AF = mybir.ActivationFunctionType
ALU = mybir.AluOpType
AX = mybir.AxisListType

D = 1024
DIN = 5648
EPS = 1e-6
SELF_SYNC = True


class _Op:
    __slots__ = ("q", "fn", "deps", "sig", "val", "dma")


class Prog:
    QUEUES = ("pe", "act", "dve", "pool", "sp")

    def __init__(self, nc):
        self.nc = nc
        self.ops = []
        self.lastw = {}
        self.rd = {}
        self.dtot = {}

    def add(self, q, fn, R=(), W=(), dma=None):
        op = _Op()
        op.q, op.fn, op.dma, op.sig, op.val = q, fn, dma, False, 0
        deps = set()
        W = list(W) + [k for k in R if k[0] == "ps" and k not in W]
        for k in R:
            w = self.lastw.get(k)
            if w is not None:
                deps.add(w)
        for k in W:
            w = self.lastw.get(k)
            if w is not None:
                deps.add(w)
            for r in self.rd.get(k, ()):
                deps.add(r)
        for k in W:
            self.lastw[k] = op
            self.rd[k] = []
        for k in R:
            self.rd.setdefault(k, []).append(op)
        deps.discard(op)
        op.deps = {d: (self.dtot[d.dma] if d.dma is not None else None) for d in deps}
        for d in deps:
            d.sig = True
        if dma is not None:
            self.dtot[dma] = self.dtot.get(dma, 0) + 16
        self.ops.append(op)
        return op

    def barrier(self):
        last = []
        seen = set()
        for op in reversed(self.ops):
            key = op.dma if op.dma is not None else op.q
            if key in seen:
                continue
            seen.add(key)
            last.append(op)
        for q in self.QUEUES:
            op = _Op()
            op.q, op.dma, op.sig, op.val = q, None, False, 0
            op.fn = lambda e: e.nop()
            op.deps = {d: (self.dtot[d.dma] if d.dma is not None else None) for d in last}
            for d in last:
                d.sig = True
            self.ops.append(op)
        self.lastw = {}
        self.rd = {}

    def emit(self):
        nc = self.nc
        cnt = {q: 0 for q in self.QUEUES}
        dcnt = {}
        for op in self.ops:
            if op.dma is not None:
                dcnt[op.dma] = dcnt.get(op.dma, 0) + 16
                op.val = dcnt[op.dma]
            elif op.sig:
                cnt[op.q] += 1
                op.val = cnt[op.q]
        with contextlib.ExitStack() as st:
            esem = {q: st.enter_context(nc.semaphore("e_" + q)) for q in self.QUEUES}
            dsem = {k: st.enter_context(nc.semaphore("d_%d" % i)) for i, k in enumerate(dcnt)}
            block = st.enter_context(nc.Block())
            ops = self.ops

            def run(q, eng):
                waited = {}
                for op in ops:
                    if op.q != q:
                        continue
                    for d, dv in op.deps.items():
                        if d.dma is not None:
                            sem = dsem[d.dma]
                            val = dv
                        else:
                            if d.q == q and (q == "pe" or not SELF_SYNC):
                                continue
                            sem = esem[d.q]
                            val = d.val
                        if waited.get(sem, 0) >= val:
                            continue
                        eng.wait_ge(sem, val)
                        waited[sem] = val
                    ins = op.fn(eng)
                    if op.dma is not None:
                        ins.then_inc(dsem[op.dma], 16)
                    elif op.sig:
                        ins.then_inc(esem[q], 1)
                if q == "sp":
                    for k, v in dcnt.items():
                        if waited.get(dsem[k], 0) < v:
                            eng.wait_ge(dsem[k], v)
                    for qq in self.QUEUES:
                        if qq != "sp" and cnt[qq] > 0:
                            eng.wait_ge(esem[qq], cnt[qq])

            @block.tensor
            def _(e):
                run("pe", e)

            @block.scalar
            def _(e):
                run("act", e)

            @block.vector
            def _(e):
                run("dve", e)

            @block.gpsimd
            def _(e):
                run("pool", e)

            @block.sync
            def _(e):
                run("sp", e)


class Ctx:
    pass


STOP = None


def build(nc, SEQ, NB, do_sample=True):
    P = Prog(nc)

    class _Stop(Exception):
        pass

    def chk(tag):
        if STOP == tag:
            raise _Stop()

    try:
        _build_body(nc, P, SEQ, NB, chk)
    except _Stop:
        pass
    P.emit()
    return nc


def _build_body(nc, P, SEQ, NB, chk):
    T = 256
    NT = SEQ // T
    NBLK = SEQ // 128
    KWIN = min(2048, SEQ)
    NTOK = NB * 4

    def din(name, shape, dt=F32):
        return nc.dram_tensor(name, list(shape), dt, kind="ExternalInput").ap()

    def dout(name, shape, dt=F32):
        return nc.dram_tensor(name, list(shape), dt, kind="ExternalOutput").ap()

    def dscr(name, shape, dt=F32):
        return nc.dram_tensor(name, list(shape), dt, kind="Internal").ap()

    xp = din("xp", [SEQ, D])
    w_in = din("w_in", [2, D, DIN])
    w_out = din("w_out", [2, 2048, D])
    pp_d = din("pp", [128, 2, 104])
    pbc_d = din("pbc", [2, 128, 2096])
    preg_d = din("preg", [128, 2, 8])
    wbd_d = din("wbd", [128, 2, 2, 4, 128])
    c_ident = din("c_ident", [128, 128])
    c_tri = din("c_tri", [128, 128])
    c_negmask = din("c_negmask", [128, 128], BF16)
    c_mtab = din("c_mtab", [128, 8 * 17 * 128], BF16)
    c_sel = din("c_sel", [128, 64])

    yp = dout("yp", [SEQ, D])
    lru_conv_p = dout("lru_conv_p", [2, 128, 4, 3])
    lru_h_p = dout("lru_h_p", [2, 128, 4])
    k_p = dout("k_p", [2, KWIN, 512])
    v_p = dout("v_p", [2, KWIN, 512])
    ssd_conv_p = dout("ssd_conv_p", [2, 128, 12, 3])
    ssd_h_p = dout("ssd_h_p", [2, 1024, 128])

    xs_d = din("xs", [NTOK, D])
    s_lc = din("s_lc", [2, 128, 4, NB, 3])
    s_lh = din("s_lh", [2, 128, 4, NB])
    s_sc = din("s_sc", [2, 128, 12, NB, 3])
    s_sh = din("s_sh", [2, NB, 1024, 128])
    ck = din("ck", [2, NB, 2048, 512])
    cv = din("cv", [2, NB, 2048, 512])
    c_ms = din("c_ms", [128, 7 * 32], BF16)
    c_mn = din("c_mn", [NTOK, NB * 32], BF16)
    c_E = din("c_E", [16, 8, 128])
    pdt_d = din("pdt", [16, 2])
    pps_d = din("pps", [128, 2, 32])
    ys_o = dout("ys", [NTOK, D])
    lru_conv_s = dout("lru_conv_s", [2, 128, 4, NB, 3])
    lru_h_s = dout("lru_h_s", [2, 128, 4, NB])
    k_s = dout("k_s", [2, NTOK, 512])
    v_s = dout("v_s", [2, NTOK, 512])
    ssd_conv_s = dout("ssd_conv_s", [2, 128, 12, NB, 3])
    ssd_h_s = dout("ssd_h_s", [2, NB, 1024, 128])

    wib = dscr("wib", [2, D, DIN], BF16)
    wob = dscr("wob", [2, 2048, D], BF16)
    out1 = dscr("out1", [SEQ, D])
    resid = dscr("resid", [SEQ, D])

    LIMIT = 229376
    state = {"off": 16640, "n": 0}

    def sb(shape, dt=F32, name=None):
        nbytes = int(np.prod(shape[1:])) * (4 if dt == F32 else 2)
        nbytes = (nbytes + 63) // 64 * 64
        off = state["off"]
        state["off"] += nbytes
        assert state["off"] <= LIMIT, ("SBUF overflow", state["off"])
        state["n"] += 1
        return nc.alloc_sbuf_tensor_at(name or ("t%d" % state["n"]), list(shape), dt, offset=off).ap()

    def mark():
        return state["off"]

    def release(m):
        state["off"] = m

    psb = [nc.alloc_psum_tensor("psb%d" % i, [128, 512], F32).ap() for i in range(8)]

    def MM(out, lhsT, rhs, start, stop, R, W):
        P.add("pe", lambda e: e.matmul(out, lhsT=lhsT, rhs=rhs, start=start, stop=stop,
                                       skip_group_check=True), R, W)

    def TR(out, in_, ident, R, W):
        P.add("pe", lambda e: e.transpose(out, in_, ident), R, W)

    def ACT(out, in_, func, R, W, scale=1.0, bias=None, accum=None):
        def f(e):
            kw = {}
            if bias is not None:
                kw["bias"] = bias
            if accum is not None:
                kw["accum_out"] = accum
            return e.activation(out=out, in_=in_, func=func, scale=scale, **kw)
        P.add("act", f, R, W)

    def TS(q, out, in0, s1, s2, op0, op1, R, W, accum=None):
        def f(e):
            if s2 is None:
                return e.tensor_scalar(out=out, in0=in0, scalar1=s1, scalar2=None, op0=op0)
            if accum is not None:
                return e.tensor_scalar(out=out, in0=in0, scalar1=s1, scalar2=s2, op0=op0, op1=op1,
                                       accum_out=accum)
            return e.tensor_scalar(out=out, in0=in0, scalar1=s1, scalar2=s2, op0=op0, op1=op1)
        P.add(q, f, R, W)

    def TT(q, out, in0, in1, op, R, W):
        P.add(q, lambda e: e.tensor_tensor(out=out, in0=in0, in1=in1, op=op), R, W)

    def STT(q, out, in0, scalar, in1, op0, op1, R, W):
        P.add(q, lambda e: e.scalar_tensor_tensor(out=out, in0=in0, scalar=scalar, in1=in1,
                                                  op0=op0, op1=op1), R, W)

    def CP(q, out, in_, R, W):
        if q == "act":
            P.add("act", lambda e: e.activation(out=out, in_=in_, func=AF.Copy), R, W)
        else:
            P.add(q, lambda e: e.tensor_copy(out=out, in_=in_), R, W)

    def MEMSET(q, ap, val, W):
        P.add(q, lambda e: e.memset(ap, val), (), W)

    def SCAN(out, d0, d1, init, R, W):
        P.add("dve", lambda e: e.tensor_tensor_scan(out=out, data0=d0, data1=d1, initial=init,
                                                    op0=ALU.mult, op1=ALU.add), R, W)

    def RED(out, in_, R, W):
        P.add("dve", lambda e: e.tensor_reduce(out=out, in_=in_, axis=AX.X, op=ALU.add), R, W)

    def RECIP(out, in_, R, W):
        P.add("dve", lambda e: e.reciprocal(out=out, in_=in_), R, W)

    def DMA(out, in_, key, R, W, q="sp", slow=False):
        if slow:
            P.add(q, lambda e: e.dma_start(out=out, in_=in_, allow_slow_non_contiguous=True), R, W, dma=key)
        else:
            P.add(q, lambda e: e.dma_start(out=out, in_=in_), R, W, dma=key)

    ident_f = sb([128, 128])
    ident_b = sb([128, 128], BF16)
    tri_f = sb([128, 128])
    ones_f = sb([128, 128])
    negmask = sb([128, 4, 128], BF16)
    sel65 = sb([128, 64])
    pp = sb([128, 2, 104])
    preg = sb([128, 2, 8])
    wbd = sb([128, 2, 2, 4, 128], BF16)
    c8 = sb([128, 2, 4])
    Aneg = sb([128, 2, 16])
    eps_t = sb([128, 1])

    def lcw(l, c, tap): return pp[:, l, c * 4 + tap: c * 4 + tap + 1]
    def lcb(l, c): return pp[:, l, 16 + c: 17 + c]
    def lba(l, c): return pp[:, l, 20 + c: 21 + c]
    def lbx(l, c): return pp[:, l, 24 + c: 25 + c]
    def lam(l): return pp[:, l, 28:32]
    def scw(l, c, tap): return pp[:, l, 32 + c * 4 + tap: 32 + c * 4 + tap + 1]
    def scb(l, c): return pp[:, l, 80 + c: 81 + c]
    pbc_box = [None]
    def postg(l): return pbc_box[0][:, 0:1024]
    def normg(l): return pbc_box[0][:, 1024:2048]
    def dtb(l): return pbc_box[0][:, 2048:2064]
    def dsk(l): return pbc_box[0][:, 2080:2096]

    m0 = mark()
    stg = sb([128, 2, 2, 4, 128])
    DMA(ident_f, c_ident, "c0", (), [("ident_f",)])
    DMA(tri_f, c_tri, "c1", (), [("tri",)])
    DMA(sel65, c_sel, "c2", (), [("sel",)])
    DMA(pp, pp_d, "c3", (), [("pp",)])
    DMA(preg, preg_d, "c5", (), [("preg",)])
    DMA(stg, wbd_d, "c6", (), [("stg",)])
    for i in range(4):
        DMA(negmask[:, i, :], c_negmask, "c7", (), [("negmask",)])
    CP("dve", ident_b, ident_f, [("ident_f",)], [("ident_b",)])
    CP("dve", wbd, stg, [("stg",)], [("wbd",)])
    MEMSET("pool", ones_f, 1.0, [("ones",)])
    MEMSET("pool", eps_t, EPS, [("eps",)])
    tmp4 = sb([128, 2, 4])
    alg = sb([128, 2, 16])
    for l in range(2):
        DMA(alg[:, l, :], pbc_d[l, :, 2064:2080], "c4", (), [("alg", l)])
    for l in range(2):
        ACT(tmp4[:, l, :], lam(l), AF.Exp, [("pp",)], [("tmp4", l)], scale=-1.0)
        ACT(tmp4[:, l, :], tmp4[:, l, :], AF.Ln, [("tmp4", l), ("ones",)], [("tmp4b", l)], bias=ones_f[:, 0:1])
        TS("dve", c8[:, l, :], tmp4[:, l, :], -8.0, None, ALU.mult, None, [("tmp4b", l)], [("c8", l)])
        ACT(Aneg[:, l, :], alg[:, l, :], AF.Exp, [("alg", l)], [("Aneg0", l)])
        TS("dve", Aneg[:, l, :], Aneg[:, l, :], -1.0, None, ALU.mult, None, [("Aneg0", l)], [("Aneg", l)])
    P.barrier()
    release(m0)
    chk("setup")

    m0 = mark()
    CW = 2824
    wst = [sb([128, CW]) for _ in range(3)]
    wsb = [sb([128, CW], BF16) for _ in range(3)]
    it = 0
    for l in range(2):
        for kc in range(8):
            for cb in range(2):
                s = it % 3
                DMA(wst[s], w_in[l, kc * 128:(kc + 1) * 128, cb * CW:(cb + 1) * CW], ("wst", s), (), [("wst", s)])
                ACT(wsb[s], wst[s], AF.Copy, [("wst", s)], [("wsb", s)], scale=preg[:, l, kc:kc + 1])
                DMA(wib[l, kc * 128:(kc + 1) * 128, cb * CW:(cb + 1) * CW], wsb[s], ("wsbo", s), [("wsb", s)], [("wib",)],
                    q="pool")
                it += 1
    for l in range(2):
        for rc in range(8):
            s = it % 3
            DMA(wst[s][:, 0:2048].rearrange("p (a n) -> p a n", a=2),
                w_out[l, rc * 256:(rc + 1) * 256, :].rearrange("(a p) n -> p a n", p=128), ("wst", s), (), [("wst", s)])
            CP("act", wsb[s][:, 0:2048], wst[s][:, 0:2048], [("wst", s)], [("wsb", s)])
            DMA(wob[l, rc * 256:(rc + 1) * 256, :].rearrange("(a p) n -> p a n", p=128),
                wsb[s][:, 0:2048].rearrange("p (a n) -> p a n", a=2), ("wsbo", s), [("wsb", s)], [("wob",)], q="pool")
            it += 1
    P.barrier()
    release(m0)
    chk("prologue")

    def front_alloc():
        c = Ctx()
        c.xt = sb([128, 2, D])
        c.hb = sb([128, 2, D], BF16)
        c.hT = sb([128, 8, T], BF16)
        c.ssq = sb([128, 2])
        c.rstd = sb([128, 2])
        return c

    def front(c, src, t0, tag):
        for j in range(2):
            DMA(c.xt[:, j, :], src[t0 + j * 128: t0 + (j + 1) * 128, :], ("xt", tag), (), [("xt", j)])
        for j in range(2):
            ACT(c.hb[:, j, :], c.xt[:, j, :], AF.Square, [("xt", j)], [("hb", j), ("ssq", j)], accum=c.ssq[:, j:j + 1])
        ACT(c.rstd, c.ssq, AF.Sqrt, [("ssq", 0), ("ssq", 1)], [("rstd0",)], scale=1.0 / D, bias=eps_t[:, 0:1])
        RECIP(c.rstd, c.rstd, [("rstd0",)], [("rstd",)])
        for j in range(2):
            ACT(c.hb[:, j, :], c.xt[:, j, :], AF.Copy, [("xt", j), ("rstd",)], [("hb", j)], scale=c.rstd[:, j:j + 1])
        for kc in range(8):
            bank = 6 + (kc % 2)
            pt = psb[bank].bitcast(BF16)[:, 0:T]
            for j in range(2):
                TR(pt[:, j * 128:(j + 1) * 128], c.hb[:, j, kc * 128:(kc + 1) * 128], ident_b,
                   [("hb", j)], [("ps", bank)])
            CP("dve" if kc % 2 == 0 else "act", c.hT[:, kc, :], pt, [("ps", bank)], [("hT", kc)])

    HT_ALL = [("hT", kc) for kc in range(8)]

    def load_w(wt, l, col0, ncols, key):
        DMA(wt[:, :, 0:ncols], wib[l].rearrange("(kc p) n -> p kc n", p=128)[:, :, col0:col0 + ncols],
            key, (), [key])

    def proj_fm(wt, wkey, c, hT, ncols_t, col, M, out_ps, pskey):
        for kc in range(8):
            MM(out_ps, wt[:, kc, col:col + M], hT[:, kc, 0:ncols_t], kc == 0, kc == 7,
               [wkey, ("hT", kc)], [pskey])

    def proj_tm(wt, wkey, hT, j, col, N, out_ps, pskey):
        for kc in range(8):
            MM(out_ps, hT[:, kc, j * 128:(j + 1) * 128], wt[:, kc, col:col + N], kc == 0, kc == 7,
               [wkey, ("hT", kc)], [pskey])

    xs_res = sb([NTOK, D])
    E_t = sb([16, 8, 128])
    pdt = sb([16, 2])
    pps = sb([128, 2, 32])
    Aexp = sb([128, 2, 8])
    DMA(xs_res, xs_d, "s0", (), [("xs_res",)])
    DMA(E_t, c_E, "s1", (), [("E",)])
    DMA(pdt, pdt_d, "s2", (), [("pdt",)])
    DMA(pps, pps_d, "s3", (), [("pps",)])
    for l in range(2):
        ACT(Aexp[:, l, :], pps[:, l, 8:16], AF.Exp, [("pps",)], [("Aexp0", l)])
        TS("dve", Aexp[:, l, :], Aexp[:, l, :], -1.0, None, ALU.mult, None, [("Aexp0", l)], [("Aexp", l)])
    P.barrier()
    m_layer = mark()

    for l in range(2):
        src = xp if l == 0 else resid
        release(m_layer)
        KT = sb([128, 4, SEQ], BF16)
        V1 = sb([128, NBLK, 8, 66], BF16)
        mtab = sb([128, 8, 17, 128], BF16)
        DMA(mtab.rearrange("p h d q -> p (h d q)"), c_mtab, "c8", (), [("mtab",)])
        MEMSET("pool", V1[:, :, :, 64:65], 1.0, [("V1ones",)])
        fc = front_alloc()
        wA = [sb([128, 8, 512], BF16) for _ in range(3)]
        woL = sb([128, 4, 512], BF16)
        woA = sb([64, 8, 512], BF16)
        xl = sb([128, 4, T + 3])
        sg = sb([128, 4, T])
        xc = sb([128, T])
        xcb = sb([128, T], BF16)
        gr = sb([128, T])
        gi = sb([128, T])
        ga = sb([128, T])
        gb = sb([128, T])
        hs = sb([128, 4, T])
        hprev = sb([128, 4])
        mixL = sb([128, 4, T], BF16)
        mixA = sb([64, 8, T], BF16)
        QT = sb([128, 4, T], BF16)
        sga = sb([64, 8, T])
        Pt = [sb([128, T], BF16) for _ in range(4)]
        Oacc = sb([65, T])
        rl = sb([64, T])
        t1 = sb([64, T])
        o1 = fc.xt
        MEMSET("pool", xl[:, :, 0:3], 0.0, [("xlh", c) for c in range(4)])
        MEMSET("pool", hprev, 0.0, [("hprev", c) for c in range(4)])
        P.barrier()

        for ti in range(NT):
            t0 = ti * T
            in_win = t0 >= SEQ - KWIN
            front(fc, src, t0, "A")
            chk("A_front")
            load_w(wA[0], l, 0, 512, ("wA", 0))
            load_w(wA[1], l, 512, 512, ("wA", 1))
            load_w(wA[2], l, 1024, 512, ("wA", 2))
            for c in range(8):
                bank = c % 2
                ps = psb[bank][:, 0:T]
                wi = c // 4
                proj_fm(wA[wi], ("wA", wi), c, fc.hT, T, (c % 4) * 128, 128, ps, ("ps", bank))
                if c < 4:
                    CP("act", xl[:, c, 3:3 + T], ps, [("ps", bank)], [("xl", c)])
                else:
                    ACT(sg[:, c - 4, :], ps, AF.Silu, [("ps", bank)], [("sg", c - 4)])
            load_w(wA[0], l, 1536, 512, ("wA", 0))
            load_w(wA[1], l, 2048, 512, ("wA", 1))
            chk("A_lruproj")
            for c in range(4):
                TS("dve", xc, xl[:, c, 0:T], lcw(l, c, 0), lcb(l, c), ALU.mult, ALU.add,
                   [("xl", c), ("xlh", c)], [("xc",)])
                for tap in range(1, 4):
                    STT("dve", xc, xl[:, c, tap:tap + T], lcw(l, c, tap), xc, ALU.mult, ALU.add,
                        [("xl", c), ("xlh", c), ("xc",)], [("xc",)])
                CP("pool", xcb, xc, [("xc",)], [("xcb",)])
                CP("pool", xl[:, c, 0:3], xl[:, c, T:T + 3], [("xl", c), ("xlh", c)], [("xlh", c)])
                bank = 2 + (c % 2) * 2
                MM(psb[bank][:, 0:T], wbd[:, l, 0, c, :], xcb, True, True, [("xcb",)], [("ps", bank)])
                MM(psb[bank + 1][:, 0:T], wbd[:, l, 1, c, :], xcb, True, True, [("xcb",)], [("ps", bank + 1)])
                ACT(gr, psb[bank][:, 0:T], AF.Sigmoid, [("ps", bank)], [("gr",)], bias=lba(l, c))
                ACT(gi, psb[bank + 1][:, 0:T], AF.Sigmoid, [("ps", bank + 1)], [("gi",)], bias=lbx(l, c))
                ACT(ga, gr, AF.Exp, [("gr",)], [("ga",)], scale=c8[:, l, c:c + 1])
                TT("pool", gb, ga, ga, ALU.mult, [("ga",)], [("gb",)])
                TS("pool", gb, gb, -1.0, 1.0, ALU.mult, ALU.add, [("gb",)], [("gb",)])
                ACT(gb, gb, AF.Sqrt, [("gb",)], [("gb",)])
                TT("dve", gi, gi, xc, ALU.mult, [("gi",), ("xc",)], [("gi",)])
                TT("dve", gb, gb, gi, ALU.mult, [("gb",), ("gi",)], [("gb",)])
                SCAN(hs[:, c, :], ga, gb, hprev[:, c:c + 1], [("ga",), ("gb",), ("hprev", c)], [("hs", c)])
                CP("pool", hprev[:, c:c + 1], hs[:, c, T - 1:T], [("hs", c)], [("hprev", c)])
                TT("dve", mixL[:, c, :], hs[:, c, :], sg[:, c, :], ALU.mult, [("hs", c), ("sg", c)], [("mixL", c)])
            chk("A_lru")
            for c in range(4):
                bank = c % 2
                ps = psb[bank][:, 0:T]
                proj_fm(wA[2], ("wA", 2), c, fc.hT, T, c * 128, 128, ps, ("ps", bank))
                CP("act", QT[:, c, :], ps, [("ps", bank)], [("QT", c)])
            chk("A_q")
            load_w(wA[2], l, 2560, 512, ("wA", 2))
            for c in range(4):
                bank = c % 2
                ps = psb[bank][:, 0:T]
                proj_fm(wA[0], ("wA", 0), c, fc.hT, T, c * 128, 128, ps, ("ps", bank))
                CP("dve", KT[:, c, t0:t0 + T], ps, [("ps", bank)], [("KT", ti)])
            chk("A_k")
            if in_win:
                for j in range(2):
                    bank = 2 + j
                    proj_tm(wA[0], ("wA", 0), fc.hT, j, 0, 512, psb[bank], ("ps", bank))
                    CP("act", o1[:, j, 0:512], psb[bank], [("ps", bank)], [("xt", j)])
                    r0 = t0 + j * 128 - (SEQ - KWIN)
                    DMA(k_p[l, r0:r0 + 128, :], o1[:, j, 0:512], ("kst", j), [("xt", j)], [("k_p",)])
            chk("A_ktok")
            for j in range(2):
                bank = 4 + j
                blk = ti * 2 + j
                proj_tm(wA[1], ("wA", 1), fc.hT, j, 0, 512, psb[bank], ("ps", bank))
                chk("A_v1")
                for hh_ in range(8):
                    CP("dve", V1[:, blk, hh_, 0:64], psb[bank][:, hh_ * 64:(hh_ + 1) * 64], [("ps", bank)], [("V1", blk)])
                chk("A_v2")
                if in_win:
                    CP("act", o1[:, j, 512:1024], psb[bank], [("ps", bank), ("V1", blk)], [("xt", j)])
                    chk("A_v3")
                    r0 = t0 + j * 128 - (SEQ - KWIN)
                    DMA(v_p[l, r0:r0 + 128, :], o1[:, j, 512:1024], ("vst", j), [("xt", j)], [("v_p",)])
            chk("A_v")
            for h in range(8):
                bank = h % 2
                ps = psb[bank][0:64, 0:T]
                proj_fm(wA[2], ("wA", 2), h, fc.hT, T, h * 64, 64, ps, ("ps", bank))
                ACT(sga[:, h, :], ps, AF.Silu, [("ps", bank)], [("sga", h)])
            chk("A_attnproj")
            b0 = ti * 2
            units = []
            for h in range(8):
                kbs = [b0, b0 + 1] + list(range(b0 - 1, max(-1, b0 - 17), -1))
                for kb in kbs:
                    units.append((h, kb, kb == kbs[0], kb == kbs[-1]))
            pend = []

            def att_front(un, h, kb):
                pr, pb = h // 2, (h % 2) * 64
                d0 = b0 - kb
                if d0 < 0:
                    c0_, c1_, dl = 128, 256, 0
                elif d0 == 16:
                    c0_, c1_, dl = 0, 128, 16
                else:
                    c0_, c1_, dl = 0, 256, d0
                n = c1_ - c0_
                sbank = un % 4
                sps = psb[sbank][:, 0:n]
                MM(sps, KT[pb:pb + 64, pr, kb * 128:(kb + 1) * 128], QT[pb:pb + 64, pr, c0_:c1_], True, True,
                   [("KT", kb // 2), ("QT", pr)], [("ps", sbank)])
                pt = Pt[un % 4]
                ACT(pt[:, 0:n], sps, AF.Exp, [("ps", sbank)], [("Pt", un % 4)], scale=0.125)
                TT("dve" if un % 3 == 2 else "pool", pt[:, 0:n], pt[:, 0:n],
                   mtab[:, h, dl:dl + n // 128, :].rearrange("p a q -> p (a q)"),
                   ALU.mult, [("Pt", un % 4)], [("Pt", un % 4)])
                return c0_, c1_, n

            def att_pv(un, h, kb, first, last, c0_, c1_, n):
                obank = 4 + (h % 2)
                ops = psb[obank][0:65, 0:T]
                MM(ops[:, c0_:c1_], V1[:, kb, h, 0:65], Pt[un % 4][:, 0:n], first, last,
                   [("Pt", un % 4), ("V1", kb)], [("ps", obank)])
                if last:
                    pend.append([3, lambda: att_finish(h)])

            def att_finish(h):
                obank = 4 + (h % 2)
                ops = psb[obank][0:65, 0:T]
                CP("dve", Oacc, ops, [("ps", obank)], [("Oacc",)])
                lbank = 6 + (h % 2)
                MM(psb[lbank][0:64, 0:T], sel65[0:65, :], Oacc, True, True, [("Oacc",)], [("ps", lbank)])
                RECIP(rl, psb[lbank][0:64, 0:T], [("ps", lbank)], [("rl",)])
                TT("dve", t1, Oacc[0:64, :], rl, ALU.mult, [("Oacc",), ("rl",)], [("t1",)])
                TT("dve", mixA[:, h, :], t1, sga[:, h, :], ALU.mult, [("t1",), ("sga", h)], [("mixA", h)])

            def tick():
                for p_ in pend:
                    p_[0] -= 1
                while pend and pend[0][0] <= 0:
                    pend.pop(0)[1]()

            for un, (h, kb, first, last) in enumerate(units):
                c0_, c1_, n = att_front(un, h, kb)
                pend.append([3, (lambda un=un, h=h, kb=kb, first=first, last=last, c0_=c0_, c1_=c1_, n=n:
                                 att_pv(un, h, kb, first, last, c0_, c1_, n))])
                tick()
            while pend:
                tick()
            chk("A_attn")
            for half in range(2):
                hsl = slice(half * 512, (half + 1) * 512)
                DMA(woL, wob[l, 0:512, hsl].rearrange("(c p) n -> p c n", p=128), ("woL",), (), [("woL",)])
                DMA(woA, wob[l, 512:1024, hsl].rearrange("(h p) n -> p h n", p=64), ("woA",), (), [("woA",)])
                for j in range(2):
                    bank = 2 * j + half
                    ps = psb[bank]
                    for c in range(4):
                        MM(ps, mixL[:, c, j * 128:(j + 1) * 128], woL[:, c, :], c == 0, False,
                           [("mixL", c), ("woL",)], [("ps", bank)])
                    for h in range(8):
                        MM(ps, mixA[:, h, j * 128:(j + 1) * 128], woA[:, h, :], False, h == 7,
                           [("mixA", h), ("woA",)], [("ps", bank)])
                    CP("act" if j == 0 else "dve", o1[:, j, hsl], ps, [("ps", bank)], [("xt", j)])
            for j in range(2):
                DMA(out1[t0 + j * 128: t0 + (j + 1) * 128, :], o1[:, j, :], ("o1", j), [("xt", j)], [("out1", ti)])
        DMA(lru_conv_p[l], xl[:, :, 0:3], ("fin", 0), [("xlh", c) for c in range(4)], [("lcp",)])
        DMA(lru_h_p[l], hprev, ("fin", 1), [("hprev", c) for c in range(4)], [("lhp",)])
        P.barrier()
        chk("A%d" % l)

        release(m_layer)
        fc = front_alloc()
        pbc = sb([128, 2096])
        pbc_box[0] = pbc
        DMA(pbc, pbc_d[l], ("pbc",), (), [("pbc",)])
        wB = [sb([128, 8, 512], BF16) for _ in range(3)]
        wdt = sb([128, 8, 16], BF16)
        woS = sb([128, 8, D], BF16)
        xb = sb([128, 12, T + 3])
        xa2 = [sb([128, T]), sb([128, T])]
        xsT = sb([128, 8, T])
        BT = sb([128, 2, T], BF16)
        CT = sb([128, 2, T], BF16)
        Btok = sb([128, 2, 128], BF16)
        sz = sb([128, 2, D])
        dtt = sb([128, 2, 16])
        dtA = sb([128, 16])
        ncum = sb([128, 16])
        ecum = sb([128, 16])
        wend = sb([128, 16])
        dec = sb([128, 16])
        xs_f = sb([128, D])
        xs_b = sb([128, D], BF16)
        rhsb = sb([128, 16, 128])
        decay = sb([128, 16, 128])
        GT = sb([128, 2, 128])
        MT = sb([128, 16, 128], BF16)
        hS = sb([128, D])
        hSb = sb([128, D], BF16)
        xw = sb([128, D], BF16)
        ya = sb([128, D])
        yb = sb([128, D])
        yt = sb([128, D], BF16)
        gss = sb([128, 2])
        mixS = sb([128, 8, T], BF16)
        o1b = sb([128, 2, D])
        yo = sb([128, 2, D])
        ss2 = sb([128, 2])

        DMA(woS, wob[l, 1024:2048, :].rearrange("(c p) n -> p c n", p=128), ("woS",), (), [("woS",)])
        MEMSET("pool", xb[:, :, 0:3], 0.0, [("xbh", c) for c in range(12)])
        MEMSET("pool", hS, 0.0, [("hS",)])
        P.barrier()

        for ti in range(NT):
            t0 = ti * T
            front(fc, src, t0, "B")
            for j in range(2):
                DMA(o1b[:, j, :], out1[t0 + j * 128: t0 + (j + 1) * 128, :], ("o1b", j), [("out1", ti)], [("o1b", j)])
            load_w(wB[0], l, 3072, 512, ("wB", 0))
            load_w(wB[1], l, 3584, 512, ("wB", 1))
            load_w(wB[2], l, 4096, 512, ("wB", 2))
            load_w(wdt, l, 5632, 16, ("wdt",))
            for j in range(2):
                for half in range(2):
                    bank = 2 * j + half
                    proj_tm(wB[half], ("wB", half), fc.hT, j, 0, 512, psb[bank], ("ps", bank))
                    ACT(sz[:, j, half * 512:(half + 1) * 512], psb[bank], AF.Silu, [("ps", bank)], [("sz", j)])
            load_w(wB[0], l, 4608, 512, ("wB", 0))
            load_w(wB[1], l, 5120, 512, ("wB", 1))
            for c in range(12):
                bank = 4 + c % 2
                ps = psb[bank][:, 0:T]
                wi = (2, 0, 1)[c // 4]
                proj_fm(wB[wi], ("wB", wi), c, fc.hT, T, (c % 4) * 128, 128, ps, ("ps", bank))
                CP("act" if c % 2 == 0 else "dve", xb[:, c, 3:3 + T], ps, [("ps", bank)], [("xb", c)])
            for j in range(2):
                bank = 6 + j
                proj_tm(wdt, ("wdt",), fc.hT, j, 0, 16, psb[bank][:, 0:16], ("ps", bank))
                TT("dve", dtt[:, j, :], psb[bank][:, 0:16], dtb(l), ALU.add, [("ps", bank), ("pbc",)], [("dtt", j)])
                ACT(dtt[:, j, :], dtt[:, j, :], AF.Exp, [("dtt", j)], [("dtt", j)])
                ACT(dtt[:, j, :], dtt[:, j, :], AF.Ln, [("dtt", j)], [("dtt", j)], bias=ones_f[:, 0:1])
            for c in range(12):
                q = "dve"
                xa = xa2[c % 2]
                xak = ("xa", c % 2)
                TS(q, xa, xb[:, c, 0:T], scw(l, c, 0), scb(l, c), ALU.mult, ALU.add,
                   [("xb", c), ("xbh", c)], [xak])
                for tap in range(1, 4):
                    STT(q, xa, xb[:, c, tap:tap + T], scw(l, c, tap), xa, ALU.mult, ALU.add,
                        [("xb", c), ("xbh", c), xak], [xak])
                CP("pool", xb[:, c, 0:3], xb[:, c, T:T + 3], [("xb", c), ("xbh", c)], [("xbh", c)])
                if c < 8:
                    ACT(xsT[:, c, :], xa, AF.Silu, [xak], [("xsT", c)])
                elif c < 10:
                    ACT(BT[:, c - 8, :], xa, AF.Silu, [xak], [("BT", c - 8)])
                else:
                    ACT(CT[:, c - 10, :], xa, AF.Silu, [xak], [("CT", c - 10)])
            for j in range(2):
                js = slice(j * 128, (j + 1) * 128)
                for half in range(2):
                    bank = half
                    for cc in range(4):
                        c = half * 4 + cc
                        TR(psb[bank][:, cc * 128:(cc + 1) * 128], xsT[:, c, js], ident_f, [("xsT", c)], [("ps", bank)])
                    CP("act", xs_f[:, half * 512:(half + 1) * 512], psb[bank], [("ps", bank)], [("xs_f", half)])
                    CP("dve", xs_b[:, half * 512:(half + 1) * 512], psb[bank], [("ps", bank)], [("xs_b", half)])
                bank = 2
                ptb = psb[bank].bitcast(BF16)[:, 0:256]
                for g in range(2):
                    TR(ptb[:, g * 128:(g + 1) * 128], BT[:, g, js], ident_b, [("BT", g)], [("ps", bank)])
                CP("act", Btok.rearrange("p g n -> p (g n)"), ptb, [("ps", bank)], [("Btok",)])
                TT("dve", dtA, dtt[:, j, :], Aneg[:, l, :], ALU.mult, [("dtt", j)], [("dtA",)])
                bank = 3
                MM(psb[bank][:, 0:16], tri_f, dtA, True, True, [("dtA",)], [("ps", bank)])
                MM(psb[bank][:, 16:32], ones_f, dtA, True, True, [("dtA",)], [("ps", bank)])
                TS("dve", ncum, psb[bank][:, 0:16], -1.0, None, ALU.mult, None, [("ps", bank)], [("ncum",)])
                ACT(ecum, psb[bank][:, 0:16], AF.Exp, [("ps", bank)], [("ecum",)])
                ACT(dec, psb[bank][:, 16:32], AF.Exp, [("ps", bank)], [("dec",)])
                TT("dve", wend, psb[bank][:, 16:32], ncum, ALU.add, [("ps", bank), ("ncum",)], [("wend",)])
                ACT(wend, wend, AF.Exp, [("wend",)], [("wend",)])
                TT("dve", wend, wend, dtt[:, j, :], ALU.mult, [("wend",), ("dtt", j)], [("wend",)])
                TT("pool", rhsb, tri_f.unsqueeze(1).to_broadcast([128, 16, 128]),
                   dtA.unsqueeze(2).to_broadcast([128, 16, 128]), ALU.mult, [("dtA",)], [("rhsb",)])
                bank = 2
                for g in range(2):
                    MM(psb[bank][:, 256 + g * 128:256 + (g + 1) * 128], BT[:, g, js], CT[:, g, js], True, True,
                       [("BT", g), ("CT", g)], [("ps", bank)])
                CP("act", GT.rearrange("p g n -> p (g n)"), psb[bank][:, 256:512], [("ps", bank)], [("GT",)])
                for q4 in range(4):
                    bank = 4 + q4
                    MM(psb[bank], ones_f, rhsb[:, q4 * 4:(q4 + 1) * 4, :].rearrange("p h i -> p (h i)"), True, False,
                       [("rhsb",)], [("ps", bank)])
                    MM(psb[bank], ident_b, negmask.rearrange("p a i -> p (a i)"), False, True, (), [("ps", bank)])
                    for hh in range(4):
                        h = q4 * 4 + hh
                        ACT(decay[:, h, :], psb[bank][:, hh * 128:(hh + 1) * 128], AF.Exp, [("ps", bank), ("ncum",)],
                            [("decay", h)], bias=ncum[:, h:h + 1])
                        STT("dve", MT[:, h, :], decay[:, h, :], dtt[:, j, h:h + 1], GT[:, h // 8, :], ALU.mult, ALU.mult,
                            [("decay", h), ("dtt", j), ("GT",)], [("MT", h)])
                CP("pool", hSb, hS, [("hS",)], [("hSb",)])
                for h in range(16):
                    bank = h // 8
                    MM(psb[bank][:, (h % 8) * 64:(h % 8 + 1) * 64], MT[:, h, :], xs_b[:, h * 64:(h + 1) * 64], True, True,
                       [("MT", h), ("xs_b", h // 8)], [("ps", bank)])
                for g in range(2):
                    bank = 2 + g
                    MM(psb[bank], CT[:, g, js], hSb[:, g * 512:(g + 1) * 512], True, True, [("CT", g), ("hSb",)],
                       [("ps", bank)])
                for g in range(2):
                    gs = slice(g * 512, (g + 1) * 512)
                    TT("dve", ya[:, gs].rearrange("p (h d) -> p h d", h=8), psb[2 + g].rearrange("p (h d) -> p h d", h=8),
                       ecum[:, g * 8:(g + 1) * 8].unsqueeze(2).to_broadcast([128, 8, 64]), ALU.mult,
                       [("ps", 2 + g), ("ecum",)], [("ya", g)])
                    TT("dve", ya[:, gs], ya[:, gs], psb[g], ALU.add, [("ya", g), ("ps", g)], [("ya", g)])
                    TT("pool", yb[:, gs].rearrange("p (h d) -> p h d", h=8), xs_f[:, gs].rearrange("p (h d) -> p h d", h=8),
                       dsk(l)[:, g * 8:(g + 1) * 8].unsqueeze(2).to_broadcast([128, 8, 64]), ALU.mult,
                       [("xs_f", g)], [("yb", g)])
                    TT("dve", ya[:, gs], ya[:, gs], yb[:, gs], ALU.add, [("ya", g), ("yb", g)], [("ya", g)])
                    TT("dve", ya[:, gs], ya[:, gs], sz[:, j, gs], ALU.mult, [("ya", g), ("sz", j)], [("ya", g)])
                    ACT(yb[:, gs], ya[:, gs], AF.Square, [("ya", g), ("yb", g)], [("yb", g), ("gss", g)],
                        accum=gss[:, g:g + 1])
                ACT(gss, gss, AF.Sqrt, [("gss", 0), ("gss", 1)], [("gss2",)], scale=1.0 / 512, bias=eps_t[:, 0:1])
                RECIP(gss, gss, [("gss2",)], [("gss3",)])
                for g in range(2):
                    gs = slice(g * 512, (g + 1) * 512)
                    STT("dve", yt[:, gs], ya[:, gs], gss[:, g:g + 1], normg(l)[:, gs], ALU.mult, ALU.mult,
                        [("ya", g), ("gss3",)], [("yt", g)])
                TT("pool", xw.rearrange("p (h d) -> p h d", h=16), xs_f.rearrange("p (h d) -> p h d", h=16),
                   wend.unsqueeze(2).to_broadcast([128, 16, 64]), ALU.mult, [("xs_f", 0), ("xs_f", 1), ("wend",)], [("xw",)])
                for g in range(2):
                    bank = 4 + g
                    MM(psb[bank], Btok[:, g, :], xw[:, g * 512:(g + 1) * 512], True, True, [("Btok",), ("xw",)], [("ps", bank)])
                TT("pool", hS.rearrange("p (h d) -> p h d", h=16), hS.rearrange("p (h d) -> p h d", h=16),
                   dec.unsqueeze(2).to_broadcast([128, 16, 64]), ALU.mult, [("hS",), ("hSb",), ("dec",)], [("hS",)])
                for g in range(2):
                    gs = slice(g * 512, (g + 1) * 512)
                    TT("dve", hS[:, gs], hS[:, gs], psb[4 + g], ALU.add, [("hS",), ("ps", 4 + g)], [("hS",)])
                for half in range(2):
                    bank = 6 + half
                    ptb = psb[bank].bitcast(BF16)
                    for cc in range(4):
                        c = half * 4 + cc
                        TR(ptb[:, cc * 128:(cc + 1) * 128], yt[:, c * 128:(c + 1) * 128], ident_b, [("yt", c // 4)],
                           [("ps", bank)])
                    CP("act", mixS[:, half * 4:(half + 1) * 4, js], ptb[:, 0:512].rearrange("p (c t) -> p c t", c=4),
                       [("ps", bank)], [("mixS", j)])
            for j in range(2):
                for half in range(2):
                    bank = 2 * j + half
                    hsl = slice(half * 512, (half + 1) * 512)
                    for c in range(8):
                        MM(psb[bank], mixS[:, c, j * 128:(j + 1) * 128], woS[:, c, hsl], c == 0, c == 7,
                           [("mixS", j), ("woS",)], [("ps", bank)])
                    TT("dve", o1b[:, j, hsl], o1b[:, j, hsl], psb[bank], ALU.add, [("o1b", j), ("ps", bank)], [("o1b", j)])
                ACT(yo[:, j, :], o1b[:, j, :], AF.Square, [("o1b", j), ("yo", j)], [("yo", j), ("ss2", j)],
                    accum=ss2[:, j:j + 1])
            ACT(ss2, ss2, AF.Sqrt, [("ss2", 0), ("ss2", 1)], [("ss2b",)], scale=1.0 / D, bias=eps_t[:, 0:1])
            RECIP(ss2, ss2, [("ss2b",)], [("ss2c",)])
            for j in range(2):
                STT("dve", yo[:, j, :], o1b[:, j, :], ss2[:, j:j + 1], postg(l), ALU.mult, ALU.mult,
                    [("o1b", j), ("ss2c",), ("yo", j)], [("yo", j)])
                TT("pool", yo[:, j, :], yo[:, j, :], fc.xt[:, j, :], ALU.add, [("yo", j), ("xt", j)], [("yo", j)])
                dst = resid if l == 0 else yp
                DMA(dst[t0 + j * 128: t0 + (j + 1) * 128, :], yo[:, j, :], ("yo", j), [("yo", j)], [("dst",)])
        DMA(ssd_conv_p[l], xb[:, :, 0:3], ("fin", 2), [("xbh", c) for c in range(12)], [("scp",)])
        for half in range(2):
            for cc in range(4):
                c = half * 4 + cc
                bank = c % 4
                TR(psb[bank][:, 0:128], hS[:, c * 128:(c + 1) * 128], ident_f, [("hS",)], [("ps", bank)])
                CP("act", ya[:, c * 128:(c + 1) * 128], psb[bank][:, 0:128], [("ps", bank)], [("ya", c)])
                DMA(ssd_h_p[l, c * 128:(c + 1) * 128, :], ya[:, c * 128:(c + 1) * 128], ("fin", 3 + c), [("ya", c)], [("shp",)])
        P.barrier()
        chk("B%d" % l)


        release(m_layer)
        N = NTOK
        pbc = sb([128, 2096])
        pbc_box[0] = pbc
        DMA(pbc, pbc_d[l], ("pbc",), (), [("pbc",)])
        hbs = sb([N, D], BF16)
        hTs = sb([128, 8, N], BF16)
        ssq_s = sb([N, 1])
        wS = [sb([128, 8, 512], BF16) for _ in range(3)]
        wdt = sb([128, 8, 16], BF16)
        ms_t = sb([128, 7 * 32], BF16)
        mn_t = sb([N, NB * 32], BF16)
        DMA(ms_t, c_ms, ("ms",), (), [("ms",)])
        DMA(mn_t, c_mn, ("mn",), (), [("mn",)])
        xls = sb([128, 4, NB, 7])
        sgl = sb([128, 4, N])
        xc_s = sb([128, N])
        xcb_s = sb([128, N], BF16)
        gr_s = sb([128, N]); gi_s = sb([128, N]); ga_s = sb([128, N]); gb_s = sb([128, N])
        hs_s = sb([128, 4, NB, 4])
        h0_s = sb([128, 4, NB])
        mixLs = sb([128, 4, N], BF16)
        qTs = sb([128, 4, N], BF16)
        kTs = sb([128, 4, N], BF16)
        ktok = sb([N, 512])
        vtok = sb([N, 512])
        Vn = sb([N, 8, 66], BF16)
        sga_s = sb([64, 8, N])
        Kf = sb([128, 7, 512])
        Vf = sb([128, 7, 512])
        Kb = [sb([128, 7, 512], BF16) for _ in range(2)]
        Vb = [sb([128, 7, 8, 66], BF16) for _ in range(2)]
        KTs = [sb([128, 7, 4, 128], BF16) for _ in range(2)]
        Pse = [sb([128, 7 * 32], BF16) for _ in range(2)]
        Pn = [sb([N, 32], BF16) for _ in range(2)]
        Oacc_s = sb([65, NB * 32])
        rl_s = sb([64, NB * 32])
        mixAs = sb([64, 8, N], BF16)
        xbs = sb([128, 12, NB, 7])
        xa_s = sb([128, N])
        xsTs = sb([128, 8, N])
        BCT = sb([128, 4, N])
        BCtok = sb([N, 512])
        zs = sb([128, 8, N])
        dtT = sb([16, N])
        dtx = sb([128, 8, N])
        decs = sb([128, 8, N])
        xdt = sb([128, 8, N])
        Hs = [sb([128, 8, 128]) for _ in range(2)]
        bcs = [sb([128, 512]) for _ in range(2)]
        tmpa2 = [sb([128, 8, 128]) for _ in range(2)]
        tmpb2 = [sb([128, 8, 128]) for _ in range(2)]
        ysT = sb([128, 8, N])
        ysq = sb([128, 8, N])
        grs = sb([128, 2, N])
        mixSs = sb([128, 8, N], BF16)
        woLs = wS[0][:, 0:4, :]
        woAs = wS[2][0:64, :, :]
        woSs = wS[1]
        o_s = sb([N, D])
        y_s = sb([N, D])
        for b in range(2):
            MEMSET("pool", Vb[b][:, :, :, 64:65], 1.0, [("Vb1", b)])
        MEMSET("pool", Vn[:, :, 64:65], 1.0, [("Vn1",)])
        DMA(xls[:, :, :, 0:3], s_lc[l], ("slc",), (), [("xls_h",)])
        DMA(h0_s, s_lh[l], ("slh",), (), [("h0_s",)])
        DMA(xbs[:, :, :, 0:3], s_sc[l], ("ssc",), (), [("xbs_h",)])
        P.barrier()

        ACT(hbs, xs_res, AF.Square, [("xs_res",)], [("hbs",), ("ssq_s",)], accum=ssq_s[:, 0:1])
        ACT(ssq_s, ssq_s, AF.Sqrt, [("ssq_s",)], [("ssq_s",)], scale=1.0 / D, bias=eps_t[0:N, 0:1])
        RECIP(ssq_s, ssq_s, [("ssq_s",)], [("ssq_s",)])
        ACT(hbs, xs_res, AF.Copy, [("xs_res",), ("ssq_s",)], [("hbs",)], scale=ssq_s[:, 0:1])
        for kc in range(8):
            bank = 6 + (kc % 2)
            pt = psb[bank].bitcast(BF16)[:, 0:N]
            TR(pt, hbs[:, kc * 128:(kc + 1) * 128], ident_b[0:N, 0:N], [("hbs",)], [("ps", bank)])
            CP("dve" if kc % 2 == 0 else "act", hTs[:, kc, :], pt, [("ps", bank)], [("hT", kc)])

        chk("S_front")
        def wl(i, col0, ncols=512):
            load_w(wS[i], l, col0, ncols, ("wS", i))
        wl(0, 0); wl(1, 512); wl(2, 1024)
        load_w(wdt, l, 5632, 16, ("wdt",))
        for c in range(4):
            bank = c % 2
            ps = psb[bank][:, 0:N]
            proj_fm(wS[0], ("wS", 0), c, hTs, N, c * 128, 128, ps, ("ps", bank))
            CP("act", xls[:, c, :, 3:7], ps.rearrange("p (b i) -> p b i", i=4), [("ps", bank)], [("xls", c)])
        wl(0, 1536)
        for c in range(4):
            bank = c % 2
            ps = psb[bank][:, 0:N]
            proj_fm(wS[1], ("wS", 1), c, hTs, N, c * 128, 128, ps, ("ps", bank))
            ACT(sgl[:, c, :], ps, AF.Silu, [("ps", bank)], [("sgl", c)])
        wl(1, 2048)
        for c in range(4):
            bank = c % 2
            ps = psb[bank][:, 0:N]
            proj_fm(wS[2], ("wS", 2), c, hTs, N, c * 128, 128, ps, ("ps", bank))
            CP("act", qTs[:, c, :], ps, [("ps", bank)], [("qTs",)])
        wl(2, 2560)
        for c in range(4):
            bank = c % 2
            ps = psb[bank][:, 0:N]
            proj_fm(wS[0], ("wS", 0), c, hTs, N, c * 128, 128, ps, ("ps", bank))
            CP("act", kTs[:, c, :], ps, [("ps", bank)], [("kTs",)])
        bank = 2
        for kc in range(8):
            MM(psb[bank][0:N, :], hTs[:, kc, :], wS[0][:, kc, :], kc == 0, kc == 7, [("wS", 0), ("hT", kc)], [("ps", bank)])
        CP("act", ktok, psb[bank][0:N, :], [("ps", bank)], [("ktok",)])
        DMA(k_s[l], ktok, ("ktok",), [("ktok",)], [("k_s",)])
        wl(0, 3072)
        bank = 3
        for kc in range(8):
            MM(psb[bank][0:N, :], hTs[:, kc, :], wS[1][:, kc, :], kc == 0, kc == 7, [("wS", 1), ("hT", kc)], [("ps", bank)])
        CP("act", vtok, psb[bank][0:N, :], [("ps", bank)], [("vtok",)])
        CP("dve", Vn[:, :, 0:64], psb[bank][0:N, :].rearrange("p (h d) -> p h d", h=8), [("ps", bank), ("Vn1",)], [("Vn",)])
        DMA(v_s[l], vtok, ("vtok",), [("vtok",)], [("v_s",)])
        wl(1, 3584)
        for h in range(8):
            bank = h % 2
            ps = psb[bank][0:64, 0:N]
            proj_fm(wS[2], ("wS", 2), h, hTs, N, h * 64, 64, ps, ("ps", bank))
            ACT(sga_s[:, h, :], ps, AF.Silu, [("ps", bank)], [("sga_s",)])
        wl(2, 4096)
        for c in range(8):
            bank = c % 2
            ps = psb[bank][:, 0:N]
            wi = c // 4
            proj_fm(wS[wi], ("wS", wi), c, hTs, N, (c % 4) * 128, 128, ps, ("ps", bank))
            ACT(zs[:, c, :], ps, AF.Silu, [("ps", bank)], [("zs",)])
        wl(0, 4608); wl(1, 5120)
        for c in range(12):
            bank = c % 2
            ps = psb[bank][:, 0:N]
            wi = (2, 0, 1)[c // 4]
            proj_fm(wS[wi], ("wS", wi), c, hTs, N, (c % 4) * 128, 128, ps, ("ps", bank))
            CP("act", xbs[:, c, :, 3:7], ps.rearrange("p (b i) -> p b i", i=4), [("ps", bank)], [("xbs", c)])
        bank = 2
        for kc in range(8):
            MM(psb[bank][0:16, 0:N], wdt[:, kc, :], hTs[:, kc, :], kc == 0, kc == 7, [("wdt",), ("hT", kc)], [("ps", bank)])
        ACT(dtT, psb[bank][0:16, 0:N], AF.Exp, [("ps", bank)], [("dtT",)], bias=pdt[:, l:l + 1])
        ACT(dtT, dtT, AF.Ln, [("dtT",)], [("dtT",)], bias=ones_f[0:16, 0:1])
        for c in range(8):
            bank = 4 + c % 2
            MM(psb[bank][:, 0:N], E_t[:, c, :], dtT, True, True, [("dtT",)], [("ps", bank)])
            CP("act", dtx[:, c, :], psb[bank][:, 0:N], [("ps", bank)], [("dtx",)])

        chk("S_proj")
        for c in range(4):
            xv = xc_s.rearrange("p (b i) -> p b i", i=4)
            TS("dve", xv, xls[:, c, :, 0:4], lcw(l, c, 0), lcb(l, c), ALU.mult, ALU.add, [("xls", c), ("xls_h",)], [("xc_s",)])
            for tap in range(1, 4):
                STT("dve", xv, xls[:, c, :, tap:tap + 4], lcw(l, c, tap), xv, ALU.mult, ALU.add,
                    [("xls", c), ("xls_h",), ("xc_s",)], [("xc_s",)])
            CP("pool", xcb_s, xc_s, [("xc_s",)], [("xcb_s",)])
            bank = 2 + (c % 2) * 2
            MM(psb[bank][:, 0:N], wbd[:, l, 0, c, :], xcb_s, True, True, [("xcb_s",)], [("ps", bank)])
            MM(psb[bank + 1][:, 0:N], wbd[:, l, 1, c, :], xcb_s, True, True, [("xcb_s",)], [("ps", bank + 1)])
            ACT(gr_s, psb[bank][:, 0:N], AF.Sigmoid, [("ps", bank)], [("gr_s",)], bias=lba(l, c))
            ACT(gi_s, psb[bank + 1][:, 0:N], AF.Sigmoid, [("ps", bank + 1)], [("gi_s",)], bias=lbx(l, c))
            ACT(ga_s, gr_s, AF.Exp, [("gr_s",)], [("ga_s",)], scale=c8[:, l, c:c + 1])
            TT("pool", gb_s, ga_s, ga_s, ALU.mult, [("ga_s",)], [("gb_s",)])
            TS("pool", gb_s, gb_s, -1.0, 1.0, ALU.mult, ALU.add, [("gb_s",)], [("gb_s",)])
            ACT(gb_s, gb_s, AF.Sqrt, [("gb_s",)], [("gb_s",)])
            TT("dve", gi_s, gi_s, xc_s, ALU.mult, [("gi_s",), ("xc_s",)], [("gi_s",)])
            TT("dve", gb_s, gb_s, gi_s, ALU.mult, [("gb_s",), ("gi_s",)], [("gb_s",)])
            gav = ga_s.rearrange("p (b i) -> p b i", i=4)
            gbv = gb_s.rearrange("p (b i) -> p b i", i=4)
            for i in range(4):
                prev = h0_s[:, c, :] if i == 0 else hs_s[:, c, :, i - 1]
                TT("dve", hs_s[:, c, :, i], gav[:, :, i], prev, ALU.mult, [("ga_s",), ("h0_s",), ("hs_s", c)], [("hs_s", c)])
                TT("dve", hs_s[:, c, :, i], hs_s[:, c, :, i], gbv[:, :, i], ALU.add, [("gb_s",), ("hs_s", c)], [("hs_s", c)])
            TT("dve", mixLs[:, c, :].rearrange("p (b i) -> p b i", i=4), hs_s[:, c, :, :],
               sgl[:, c, :].rearrange("p (b i) -> p b i", i=4), ALU.mult, [("hs_s", c), ("sgl", c)], [("mixLs",)])
        DMA(lru_conv_s[l], xls[:, :, :, 4:7], ("fs", 0), [("xls", c) for c in range(4)], [("lcs",)])
        DMA(lru_h_s[l], hs_s[:, :, :, 3], ("fs", 1), [("hs_s", c) for c in range(4)], [("lhs",)], slow=True)

        chk("S_lru")
        for c in range(12):
            xv = xa_s.rearrange("p (b i) -> p b i", i=4)
            TS("dve", xv, xbs[:, c, :, 0:4], scw(l, c, 0), scb(l, c), ALU.mult, ALU.add, [("xbs", c), ("xbs_h",)], [("xa_s",)])
            for tap in range(1, 4):
                STT("dve", xv, xbs[:, c, :, tap:tap + 4], scw(l, c, tap), xv, ALU.mult, ALU.add,
                    [("xbs", c), ("xbs_h",), ("xa_s",)], [("xa_s",)])
            if c < 8:
                ACT(xsTs[:, c, :], xa_s, AF.Silu, [("xa_s",)], [("xsTs",)])
            else:
                ACT(BCT[:, c - 8, :], xa_s, AF.Silu, [("xa_s",)], [("BCT",)])
        DMA(ssd_conv_s[l], xbs[:, :, :, 4:7], ("fs", 2), [("xbs", c) for c in range(12)], [("scs",)])
        bank = 2
        for c in range(4):
            TR(psb[bank][0:N, c * 128:(c + 1) * 128], BCT[:, c, :], ident_f, [("BCT",)], [("ps", bank)])
        CP("act", BCtok, psb[bank][0:N, :], [("ps", bank)], [("BCtok",)])
        for c in range(8):
            ACT(decs[:, c, :], dtx[:, c, :], AF.Exp, [("dtx",), ("Aexp", l)], [("decs",)], scale=Aexp[:, l, c:c + 1])
        TT("dve", xdt, xsTs, dtx, ALU.mult, [("xsTs",), ("dtx",)], [("xdt",)])

        chk("S_conv")
        pend_pv = [None]

        def pv_block(b):
            e = b % 2
            obank = 5
            for h in range(8):
                pr, par = h // 2, h % 2
                oc = psb[obank][0:65, b * 32 + h * 4: b * 32 + h * 4 + 4]
                for kt in range(7):
                    col = par * 112 + kt * 16 + pr * 4
                    MM(oc, Vb[e][:, kt, h, 0:65], Pse[e][:, col:col + 4], kt == 0, False,
                       [("Vb", e), ("Pse", e)], [("ps", 5)])
                col = par * 16 + pr * 4
                MM(oc, Vn[:, h, 0:65], Pn[e][:, col:col + 4], False, True, [("Vn",), ("Pn", e)], [("ps", 5)])

        far = lambda src, b, kt: src[l, b].rearrange("(m s) d -> m s d", s=16)[32 * kt:32 * kt + 32, 0:4, :]
        for b in range(NB):
            e = b % 2
            for kt in range(3):
                DMA(Kf[:, kt, :], far(ck, b, kt), ("Kf",), (), [("Kf",)])
                DMA(Vf[:, kt, :], far(cv, b, kt), ("Vf",), (), [("Vf",)])
            DMA(Kf[:, 3:7, :], ck[l, b, 1536:2048, :].rearrange("(t p) d -> p t d", p=128), ("Kf",), (), [("Kf",)])
            DMA(Vf[:, 3:7, :], cv[l, b, 1536:2048, :].rearrange("(t p) d -> p t d", p=128), ("Vf",), (), [("Vf",)])
            DMA(Hs[e], s_sh[l, b].rearrange("(c q) n -> q c n", q=128), ("Hs", e), (), [("Hs", e)])
            CP("pool", Kb[e], Kf, [("Kf",)], [("Kb", e)])
            CP("dve", Vb[e][:, :, :, 0:64], Vf.rearrange("p t (h d) -> p t h d", h=8), [("Vf",), ("Vb1", e)], [("Vb", e)])
            chk("S_a1")
            for kt in range(7):
                bank = kt % 2
                pt = psb[bank].bitcast(BF16)[:, 0:512]
                for pr in range(4):
                    TR(pt[:, pr * 128:(pr + 1) * 128], Kb[e][:, kt, pr * 128:(pr + 1) * 128], ident_b, [("Kb", e)], [("ps", bank)])
                CP("act" if kt % 2 == 0 else "dve", KTs[e][:, kt, :, :].rearrange("p a k -> p (a k)"), pt, [("ps", bank)],
                   [("KTs", e)])
            chk("S_a2")
            sbanks = (2, 3)
            for par in range(2):
                sbank = sbanks[par]
                pb = par * 64
                for kt in range(7):
                    for pr in range(4):
                        col = kt * 16 + pr * 4
                        MM(psb[sbank][:, col:col + 4], KTs[e][pb:pb + 64, kt, pr, :],
                           qTs[pb:pb + 64, pr, 4 * b:4 * b + 4], True, True, [("KTs", e), ("qTs",)], [("ps", sbank)])
                for pr in range(4):
                    col = 112 + pr * 4
                    MM(psb[sbank][0:N, col:col + 4], kTs[pb:pb + 64, pr, :],
                       qTs[pb:pb + 64, pr, 4 * b:4 * b + 4], True, True, [("kTs",), ("qTs",)], [("ps", sbank)])
            chk("S_a3")
            for par in range(2):
                sbank = sbanks[par]
                ACT(Pse[e][:, par * 112:(par + 1) * 112], psb[sbank][:, 0:112], AF.Exp, [("ps", sbank)], [("Pse", e)], scale=0.125)
                ACT(Pn[e][:, par * 16:(par + 1) * 16], psb[sbank][0:N, 112:128], AF.Exp, [("ps", sbank)], [("Pn", e)], scale=0.125)
            TT("pool", Pse[e], Pse[e], ms_t, ALU.mult, [("Pse", e)], [("Pse", e)])
            TT("pool", Pn[e], Pn[e], mn_t[:, b * 32:(b + 1) * 32], ALU.mult, [("Pn", e)], [("Pn", e)])
            chk("S_a4")
            if pend_pv[0] is not None:
                pv_block(pend_pv[0])
            pend_pv[0] = b
            chk("S_att0")
            if e == 0 and b + 1 < NB:
                continue
            els = [b - 1, b] if e == 1 else [b]
            for i in range(4):
                for bx in els:
                    ex = bx % 2
                    H = Hs[ex]
                    tmpa = tmpa2[ex]
                    tmpb = tmpb2[ex]
                    tk = 4 * bx + i
                    bank = 6 + ex
                    MM(psb[bank], ident_f[0:N, tk:tk + 1].to_broadcast([N, 128]), BCtok, True, True, [("BCtok",)], [("ps", bank)])
                    CP("act", bcs[ex], psb[bank], [("ps", bank)], [("bcs", ex)])
                    Hv = H.rearrange("p (g c) n -> p g c n", g=2)
                    TT("pool", H, H, decs[:, :, tk:tk + 1].to_broadcast([128, 8, 128]), ALU.mult, [("Hs", ex), ("decs",)], [("Hs", ex)])
                    TT("pool", tmpa.rearrange("p (g c) n -> p g c n", g=2),
                       bcs[ex][:, 0:256].rearrange("p (g n) -> p g n", g=2).unsqueeze(2).to_broadcast([128, 2, 4, 128]),
                       xdt[:, :, tk:tk + 1].rearrange("p (g c) o -> p g c o", g=2).to_broadcast([128, 2, 4, 128]), ALU.mult,
                       [("bcs", ex), ("xdt",)], [("tmpa", ex)])
                    TT("dve", H, H, tmpa, ALU.add, [("Hs", ex), ("tmpa", ex)], [("Hs", ex)])
                    TT("dve", tmpb.rearrange("p (g c) n -> p g c n", g=2), Hv,
                       bcs[ex][:, 256:512].rearrange("p (g n) -> p g n", g=2).unsqueeze(2).to_broadcast([128, 2, 4, 128]), ALU.mult,
                       [("Hs", ex), ("bcs", ex)], [("tmpb", ex)])
                    RED(ysT[:, :, tk], tmpb, [("tmpb", ex)], [("ysT",)])
            for bx in els:
                ex = bx % 2
                DMA(ssd_h_s[l, bx].rearrange("(c q) n -> q c n", q=128), Hs[ex], ("Hso", ex), [("Hs", ex)], [("shs",)])

        pv_block(pend_pv[0])
        chk("S_loop")
        CP("dve", Oacc_s, psb[5][0:65, 0:NB * 32], [("ps", 5)], [("Oacc_s",)])
        MM(psb[6][0:64, 0:NB * 32], sel65[0:65, :], Oacc_s, True, True, [("Oacc_s",)], [("ps", 6)])
        RECIP(rl_s, psb[6][0:64, 0:NB * 32], [("ps", 6)], [("rl_s",)])
        TT("dve", rl_s, rl_s, Oacc_s[0:64, :], ALU.mult, [("rl_s",), ("Oacc_s",)], [("rl_s",)])
        TT("dve", mixAs.rearrange("p h (b i) -> p h b i", i=4), rl_s.rearrange("p (b h i) -> p h b i", h=8, i=4),
           sga_s.rearrange("p h (b i) -> p h b i", i=4), ALU.mult, [("rl_s",), ("sga_s",)], [("mixAs",)])

        chk("S_an")
        for c in range(8):
            STT("dve", ysT[:, c, :], xsTs[:, c, :], pps[:, l, 16 + c:17 + c], ysT[:, c, :], ALU.mult, ALU.add,
                [("xsTs",), ("ysT",)], [("ysT",)])
        TT("dve", ysT, ysT, zs, ALU.mult, [("ysT",), ("zs",)], [("ysT",)])
        TT("pool", ysq, ysT, ysT, ALU.mult, [("ysT",)], [("ysq",)])
        for g in range(2):
            for cc in range(4):
                MM(psb[7][:, g * N:(g + 1) * N], ones_f, ysq[:, g * 4 + cc, :], cc == 0, cc == 3, [("ysq",)], [("ps", 7)])
        ACT(grs.rearrange("p g n -> p (g n)"), psb[7][:, 0:2 * N], AF.Sqrt, [("ps", 7)], [("grs",)], scale=1.0 / 512,
            bias=eps_t[:, 0:1])
        RECIP(grs, grs, [("grs",)], [("grs",)])
        for c in range(8):
            STT("dve", mixSs[:, c, :], ysT[:, c, :], pps[:, l, 24 + c:25 + c], grs[:, c // 4, :], ALU.mult, ALU.mult,
                [("ysT",), ("grs",)], [("mixSs",)])

        chk("S_so")
        for half in range(2):
            hsl = slice(half * 512, (half + 1) * 512)
            DMA(woLs, wob[l, 0:512, hsl].rearrange("(c p) n -> p c n", p=128), ("wS", 0), (), [("wS", 0)])
            DMA(woAs, wob[l, 512:1024, hsl].rearrange("(h p) n -> p h n", p=64), ("wS", 2), (), [("wS", 2)])
            DMA(woSs, wob[l, 1024:2048, hsl].rearrange("(c p) n -> p c n", p=128), ("wS", 1), (), [("wS", 1)])
            bank = half
            ps = psb[bank][0:N, :]
            for c in range(4):
                MM(ps, mixLs[:, c, :], woLs[:, c, :], c == 0, False, [("mixLs",), ("wS", 0)], [("ps", bank)])
            for h in range(8):
                MM(ps, mixAs[:, h, :], woAs[:, h, :], False, False, [("mixAs",), ("wS", 2)], [("ps", bank)])
            for c in range(8):
                MM(ps, mixSs[:, c, :], woSs[:, c, :], False, c == 7, [("mixSs",), ("wS", 1)], [("ps", bank)])
            CP("act", o_s[:, hsl], ps, [("ps", bank)], [("o_s",)])
        ACT(y_s, o_s, AF.Square, [("o_s",)], [("y_s",), ("ssq_s",)], accum=ssq_s[:, 0:1])
        ACT(ssq_s, ssq_s, AF.Sqrt, [("ssq_s",)], [("ssq_s",)], scale=1.0 / D, bias=eps_t[0:N, 0:1])
        RECIP(ssq_s, ssq_s, [("ssq_s",)], [("ssq_s",)])
        STT("dve", y_s, o_s, ssq_s[:, 0:1], postg(l)[0:N, :], ALU.mult, ALU.mult, [("o_s",), ("ssq_s",), ("y_s",)], [("y_s",)])
        TT("dve", xs_res, xs_res, y_s, ALU.add, [("y_s",), ("xs_res",)], [("xs_res",)])
        if l == 1:
            DMA(ys_o, xs_res, ("ys_o",), [("xs_res",)], [("ys_o",)])
        P.barrier()
        chk("S%d" % l)

    return nc


def host_consts():
    k = np.arange(128)[:, None]
    q = np.arange(128)[None, :]
    ident = np.eye(128, dtype=np.float32)
    tri = (k <= q).astype(np.float32)
    negmask = np.where(k <= q, 0.0, -30000.0).astype(ml_dtypes.bfloat16)
    sel = np.zeros((128, 64), np.float32)
    sel[64, :] = 1.0
    mt = np.zeros((128, 8, 17, 128), np.float64)
    for h in range(8):
        slope = 2.0 ** (-(h + 1))
        for dl in range(17):
            d = 128 * dl + (q - k)
            c = ((d >= 0) & (d <= 128)).astype(np.float64) + ((d >= 0) & (d % 4 == 0) & (d <= 512)) + \
                ((d >= 0) & (d % 16 == 0) & (d <= 2048))
            mt[:, h, dl, :] = c * np.exp(-slope * np.maximum(d, 0))
    mtab = mt.reshape(128, -1).astype(ml_dtypes.bfloat16)
    return dict(c_ident=ident, c_tri=tri, c_negmask=negmask, c_mtab=mtab, c_sel=sel)


def host_params(inp):
    f = np.float32
    pp = np.zeros((128, 2, 104), f)
    pbc = np.zeros((2, 128, 2096), f)
    for l in range(2):
        pp[:, l, 0:16] = inp["lru_conv_w"][l].reshape(4, 4, 128).transpose(2, 1, 0).reshape(128, 16)
        pp[:, l, 16:20] = inp["lru_conv_b"][l].reshape(4, 128).T
        pp[:, l, 20:24] = inp["lru_b_a"][l].reshape(4, 128).T
        pp[:, l, 24:28] = inp["lru_b_x"][l].reshape(4, 128).T
        pp[:, l, 28:32] = inp["lru_lambda"][l].reshape(4, 128).T
        pp[:, l, 32:80] = inp["ssd_conv_w"][l].reshape(4, 12, 128).transpose(2, 1, 0).reshape(128, 48)
        pp[:, l, 80:92] = inp["ssd_conv_b"][l].reshape(12, 128).T
        pbc[l, :, 0:1024] = inp["post_norm_g"][l][None, :]
        pbc[l, :, 1024:2048] = inp["ssd_norm_g"][l][None, :]
        pbc[l, :, 2048:2064] = inp["ssd_dt_bias"][l][None, :]
        pbc[l, :, 2064:2080] = inp["ssd_a_log"][l][None, :]
        pbc[l, :, 2080:2096] = inp["ssd_d"][l][None, :]
    preg = np.ascontiguousarray(inp["pre_norm_g"].reshape(2, 8, 128).transpose(2, 0, 1)).astype(f)
    wbd = np.zeros((128, 2, 2, 4, 128), f)
    for l in range(2):
        for ai, nm in enumerate(("lru_w_a", "lru_w_x")):
            w = inp[nm][l]
            for c in range(4):
                wbd[0:64, l, ai, c, 0:64] = w[2 * c]
                wbd[64:128, l, ai, c, 64:128] = w[2 * c + 1]
    return dict(pp=pp, pbc=pbc, preg=preg, wbd=wbd)


def sample_consts(NB):
    ms = np.zeros((128, 7, 8, 4), np.float64)
    p = np.arange(128)
    for kt in range(7):
        if kt < 3:
            idx = 16 * (32 * kt + p // 4) + (p % 4)
        else:
            idx = 1536 + 128 * (kt - 3) + p
        for h in range(8):
            slope = 2.0 ** (-(h + 1))
            for i in range(4):
                d = 2048 + i - idx
                c = ((d >= 0) & (d <= 128)).astype(np.float64) + ((d >= 0) & (d % 4 == 0) & (d <= 512)) + \
                    ((d >= 0) & (d % 16 == 0) & (d <= 2048))
                ms[:, kt, h, i] = c * np.exp(-slope * np.maximum(d, 0))
    mn = np.zeros((NB * 4, NB, 8, 4), np.float64)
    for b in range(NB):
        for ip in range(4):
            for h in range(8):
                slope = 2.0 ** (-(h + 1))
                for i in range(ip, 4):
                    d = i - ip
                    mn[4 * b + ip, b, h, i] = (3.0 if d == 0 else 1.0) * np.exp(-slope * d)
    E = np.zeros((16, 8, 128), np.float32)
    for c in range(8):
        for hh in range(2):
            E[2 * c + hh, c, hh * 64:(hh + 1) * 64] = 1.0
    ms = ms.reshape(128, 7, 4, 2, 4).transpose(0, 3, 1, 2, 4)
    mn = mn.reshape(NB * 4, NB, 4, 2, 4).transpose(0, 1, 3, 2, 4)
    return dict(c_ms=np.ascontiguousarray(ms).reshape(128, -1).astype(ml_dtypes.bfloat16),
                c_mn=np.ascontiguousarray(mn).reshape(NB * 4, -1).astype(ml_dtypes.bfloat16), c_E=E)


def sample_params(inp):
    f = np.float32
    pdt = np.ascontiguousarray(inp["ssd_dt_bias"].T).astype(f)
    pps = np.zeros((128, 2, 32), f)
    for l in range(2):
        rep = lambda v: np.repeat(v.reshape(8, 2), 64, axis=1).T
        pps[:, l, 0:8] = rep(inp["ssd_dt_bias"][l])
        pps[:, l, 8:16] = rep(inp["ssd_a_log"][l])
        pps[:, l, 16:24] = rep(inp["ssd_d"][l])
        pps[:, l, 24:32] = inp["ssd_norm_g"][l].reshape(8, 128).T
    return dict(pdt=pdt, pps=pps)


def sample_inputs(inp, core, NB):
    f = np.float32
    sl = slice(core * NB, (core + 1) * NB)
    m = {}
    m["xs"] = np.ascontiguousarray(inp["x_sample"][sl].reshape(NB * 4, D), dtype=f)
    m["s_lc"] = np.ascontiguousarray(inp["state_lru_conv"][:, sl].reshape(2, NB, 3, 4, 128).transpose(0, 4, 3, 1, 2), dtype=f)
    m["s_lh"] = np.ascontiguousarray(inp["state_lru_h"][:, sl].reshape(2, NB, 4, 128).transpose(0, 3, 2, 1), dtype=f)
    m["s_sc"] = np.ascontiguousarray(inp["state_ssd_conv"][:, sl].reshape(2, NB, 3, 12, 128).transpose(0, 4, 3, 1, 2), dtype=f)
    m["s_sh"] = np.ascontiguousarray(inp["state_ssd_h"][:, sl].reshape(2, NB, 1024, 128), dtype=f)
    m["ck"] = np.ascontiguousarray(inp["cache_attn_k"][:, sl].reshape(2, NB, 2048, 512), dtype=f)
    m["cv"] = np.ascontiguousarray(inp["cache_attn_v"][:, sl].reshape(2, NB, 2048, 512), dtype=f)
    m.update(sample_consts(NB))
    m.update(sample_params(inp))
    return m


_NC_CACHE = {}


def kernel(**inputs):
    inp = {k: np.asarray(v) for k, v in inputs.items()}
    B, SEQ, _ = inp["x_prompt"].shape
    DB = inp["x_sample"].shape[0]
    n = 8
    NB = DB // n
    nc = bass.Bass("TRN2", target_bir_lowering=False)
    build(nc, SEQ, NB)
    shared = dict(host_consts())
    shared.update(host_params(inp))
    shared["w_in"] = np.ascontiguousarray(inp["w_in"], dtype=np.float32)
    shared["w_out"] = np.ascontiguousarray(inp["w_out"], dtype=np.float32)
    in_maps = []
    for c in range(n):
        m = dict(shared)
        m["xp"] = np.ascontiguousarray(inp["x_prompt"][c % B])
        m.update(sample_inputs(inp, c, NB))
        in_maps.append(m)
    res = run_bass_kernel_spmd(nc, in_maps, core_ids=list(range(n))).results
    o = assemble(res, inp, B, SEQ, DB, n)
    return tuple(o[k] for k in OUT_NAMES)


OUT_NAMES = ["yp", "ys", "lru_conv_p", "lru_conv_s", "lru_h_p", "lru_h_s", "k_p", "k_s", "v_p", "v_s",
             "ssd_conv_p", "ssd_conv_s", "ssd_h_p", "ssd_h_s"]


def assemble(res, inp, B, SEQ, DB, n):
    f = np.float32
    KW = min(2048, SEQ)
    yp = np.stack([res[b]["yp"] for b in range(B)]).astype(f)
    lru_conv_p = np.stack([res[b]["lru_conv_p"].transpose(0, 3, 2, 1).reshape(2, 3, 512) for b in range(B)], 1)
    lru_h_p = np.stack([res[b]["lru_h_p"].transpose(0, 2, 1).reshape(2, 512) for b in range(B)], 1)
    k_p = np.stack([res[b]["k_p"].reshape(2, KW, 8, 64) for b in range(B)], 1)
    v_p = np.stack([res[b]["v_p"].reshape(2, KW, 8, 64) for b in range(B)], 1)
    ssd_conv_p = np.stack([res[b]["ssd_conv_p"].transpose(0, 3, 2, 1).reshape(2, 3, 1536) for b in range(B)], 1)
    ssd_h_p = np.stack([res[b]["ssd_h_p"].reshape(2, 16, 64, 128) for b in range(B)], 1)
    NB = DB // n
    cat = lambda name, ax=0: np.concatenate([res[c][name] for c in range(n)], axis=ax)
    ys = cat("ys").reshape(DB, 4, D)
    lru_conv_s = np.concatenate([res[c]["lru_conv_s"].transpose(0, 3, 4, 2, 1).reshape(2, NB, 3, 512) for c in range(n)], 1)
    lru_h_s = np.concatenate([res[c]["lru_h_s"].transpose(0, 3, 2, 1).reshape(2, NB, 512) for c in range(n)], 1)
    k_s = np.concatenate([res[c]["k_s"].reshape(2, NB, 4, 8, 64) for c in range(n)], 1)
    v_s = np.concatenate([res[c]["v_s"].reshape(2, NB, 4, 8, 64) for c in range(n)], 1)
    ssd_conv_s = np.concatenate([res[c]["ssd_conv_s"].transpose(0, 3, 4, 2, 1).reshape(2, NB, 3, 1536) for c in range(n)], 1)
    ssd_h_s = np.concatenate([res[c]["ssd_h_s"].reshape(2, NB, 16, 64, 128) for c in range(n)], 1)
    outs = dict(yp=yp, ys=ys, lru_conv_p=lru_conv_p, lru_conv_s=lru_conv_s, lru_h_p=lru_h_p, lru_h_s=lru_h_s,
                k_p=k_p, k_s=k_s, v_p=v_p, v_s=v_s, ssd_conv_p=ssd_conv_p, ssd_conv_s=ssd_conv_s,
                ssd_h_p=ssd_h_p, ssd_h_s=ssd_h_s)
    return {k: np.ascontiguousarray(v, dtype=np.float32) for k, v in outs.items()}
```

```python
import contextlib
import numpy as np
import ml_dtypes
import concourse.bass as bass
import concourse.mybir as mybir
from concourse.bass_utils import run_bass_kernel_spmd

F32 = mybir.dt.float32
BF16 = mybir.dt.bfloat16
AF = mybir.ActivationFunctionType
ALU = mybir.AluOpType
AX = mybir.AxisListType

D = 1024
DIN = 5648
EPS = 1e-6
SELF_SYNC = True


class _Op:
    __slots__ = ("q", "fn", "deps", "sig", "val", "dma")


class Prog:
    QUEUES = ("pe", "act", "dve", "pool", "sp")

    def __init__(self, nc):
        self.nc = nc
        self.ops = []
        self.lastw = {}
        self.rd = {}
        self.dtot = {}

    def add(self, q, fn, R=(), W=(), dma=None):
        op = _Op()
        op.q, op.fn, op.dma, op.sig, op.val = q, fn, dma, False, 0
        deps = set()
        W = list(W) + [k for k in R if k[0] == "ps" and k not in W]
        for k in R:
            w = self.lastw.get(k)
            if w is not None:
                deps.add(w)
        for k in W:
            w = self.lastw.get(k)
            if w is not None:
                deps.add(w)
            for r in self.rd.get(k, ()):
                deps.add(r)
        for k in W:
            self.lastw[k] = op
            self.rd[k] = []
        for k in R:
            self.rd.setdefault(k, []).append(op)
        deps.discard(op)
        op.deps = {d: (self.dtot[d.dma] if d.dma is not None else None) for d in deps}
        for d in deps:
            d.sig = True
        if dma is not None:
            self.dtot[dma] = self.dtot.get(dma, 0) + 16
        self.ops.append(op)
        return op

    def barrier(self):
        last = []
        seen = set()
        for op in reversed(self.ops):
            key = op.dma if op.dma is not None else op.q
            if key in seen:
                continue
            seen.add(key)
            last.append(op)
        for q in self.QUEUES:
            op = _Op()
            op.q, op.dma, op.sig, op.val = q, None, False, 0
            op.fn = lambda e: e.nop()
            op.deps = {d: (self.dtot[d.dma] if d.dma is not None else None) for d in last}
            for d in last:
                d.sig = True
            self.ops.append(op)
        self.lastw = {}
        self.rd = {}

    def emit(self):
        nc = self.nc
        cnt = {q: 0 for q in self.QUEUES}
        dcnt = {}
        for op in self.ops:
            if op.dma is not None:
                dcnt[op.dma] = dcnt.get(op.dma, 0) + 16
                op.val = dcnt[op.dma]
            elif op.sig:
                cnt[op.q] += 1
                op.val = cnt[op.q]
        with contextlib.ExitStack() as st:
            esem = {q: st.enter_context(nc.semaphore("e_" + q)) for q in self.QUEUES}
            dsem = {k: st.enter_context(nc.semaphore("d_%d" % i)) for i, k in enumerate(dcnt)}
            block = st.enter_context(nc.Block())
            ops = self.ops

            def run(q, eng):
                waited = {}
                for op in ops:
                    if op.q != q:
                        continue
                    for d, dv in op.deps.items():
                        if d.dma is not None:
                            sem = dsem[d.dma]
                            val = dv
                        else:
                            if d.q == q and (q == "pe" or not SELF_SYNC):
                                continue
                            sem = esem[d.q]
                            val = d.val
                        if waited.get(sem, 0) >= val:
                            continue
                        eng.wait_ge(sem, val)
                        waited[sem] = val
                    ins = op.fn(eng)
                    if op.dma is not None:
                        ins.then_inc(dsem[op.dma], 16)
                    elif op.sig:
                        ins.then_inc(esem[q], 1)
                if q == "sp":
                    for k, v in dcnt.items():
                        if waited.get(dsem[k], 0) < v:
                            eng.wait_ge(dsem[k], v)
                    for qq in self.QUEUES:
                        if qq != "sp" and cnt[qq] > 0:
                            eng.wait_ge(esem[qq], cnt[qq])

            @block.tensor
            def _(e):
                run("pe", e)

            @block.scalar
            def _(e):
                run("act", e)

            @block.vector
            def _(e):
                run("dve", e)

            @block.gpsimd
            def _(e):
                run("pool", e)

            @block.sync
            def _(e):
                run("sp", e)


class Ctx:
    pass


STOP = None


def build(nc, SEQ, NB, do_sample=True):
    P = Prog(nc)

    class _Stop(Exception):
        pass

    def chk(tag):
        if STOP == tag:
            raise _Stop()

    try:
        _build_body(nc, P, SEQ, NB, chk)
    except _Stop:
        pass
    P.emit()
    return nc


def _build_body(nc, P, SEQ, NB, chk):
    T = 256
    NT = SEQ // T
    NBLK = SEQ // 128
    KWIN = min(2048, SEQ)
    NTOK = NB * 4

    def din(name, shape, dt=F32):
        return nc.dram_tensor(name, list(shape), dt, kind="ExternalInput").ap()

    def dout(name, shape, dt=F32):
        return nc.dram_tensor(name, list(shape), dt, kind="ExternalOutput").ap()

    def dscr(name, shape, dt=F32):
        return nc.dram_tensor(name, list(shape), dt, kind="Internal").ap()

    xp = din("xp", [SEQ, D])
    w_in = din("w_in", [2, D, DIN])
    w_out = din("w_out", [2, 2048, D])
    pp_d = din("pp", [128, 2, 104])
    pbc_d = din("pbc", [2, 128, 2096])
    preg_d = din("preg", [128, 2, 8])
    wbd_d = din("wbd", [128, 2, 2, 4, 128])
    c_ident = din("c_ident", [128, 128])
    c_tri = din("c_tri", [128, 128])
    c_negmask = din("c_negmask", [128, 128], BF16)
    c_mtab = din("c_mtab", [128, 8 * 17 * 128], BF16)
    c_sel = din("c_sel", [128, 64])

    yp = dout("yp", [SEQ, D])
    lru_conv_p = dout("lru_conv_p", [2, 128, 4, 3])
    lru_h_p = dout("lru_h_p", [2, 128, 4])
    k_p = dout("k_p", [2, KWIN, 512])
    v_p = dout("v_p", [2, KWIN, 512])
    ssd_conv_p = dout("ssd_conv_p", [2, 128, 12, 3])
    ssd_h_p = dout("ssd_h_p", [2, 1024, 128])

    xs_d = din("xs", [NTOK, D])
    s_lc = din("s_lc", [2, 128, 4, NB, 3])
    s_lh = din("s_lh", [2, 128, 4, NB])
    s_sc = din("s_sc", [2, 128, 12, NB, 3])
    s_sh = din("s_sh", [2, NB, 1024, 128])
    ck = din("ck", [2, NB, 2048, 512])
    cv = din("cv", [2, NB, 2048, 512])
    c_ms = din("c_ms", [128, 7 * 32], BF16)
    c_mn = din("c_mn", [NTOK, NB * 32], BF16)
    c_E = din("c_E", [16, 8, 128])
    pdt_d = din("pdt", [16, 2])
    pps_d = din("pps", [128, 2, 32])
    ys_o = dout("ys", [NTOK, D])
    lru_conv_s = dout("lru_conv_s", [2, 128, 4, NB, 3])
    lru_h_s = dout("lru_h_s", [2, 128, 4, NB])
    k_s = dout("k_s", [2, NTOK, 512])
    v_s = dout("v_s", [2, NTOK, 512])
    ssd_conv_s = dout("ssd_conv_s", [2, 128, 12, NB, 3])
    ssd_h_s = dout("ssd_h_s", [2, NB, 1024, 128])

    wib = dscr("wib", [2, D, DIN], BF16)
    wob = dscr("wob", [2, 2048, D], BF16)
    out1 = dscr("out1", [SEQ, D])
    resid = dscr("resid", [SEQ, D])

    LIMIT = 229376
    state = {"off": 16640, "n": 0}

    def sb(shape, dt=F32, name=None):
        nbytes = int(np.prod(shape[1:])) * (4 if dt == F32 else 2)
        nbytes = (nbytes + 63) // 64 * 64
        off = state["off"]
        state["off"] += nbytes
        assert state["off"] <= LIMIT, ("SBUF overflow", state["off"])
        state["n"] += 1
        return nc.alloc_sbuf_tensor_at(name or ("t%d" % state["n"]), list(shape), dt, offset=off).ap()

    def mark():
        return state["off"]

    def release(m):
        state["off"] = m

    psb = [nc.alloc_psum_tensor("psb%d" % i, [128, 512], F32).ap() for i in range(8)]

    def MM(out, lhsT, rhs, start, stop, R, W):
        P.add("pe", lambda e: e.matmul(out, lhsT=lhsT, rhs=rhs, start=start, stop=stop,
                                       skip_group_check=True), R, W)

    def TR(out, in_, ident, R, W):
        P.add("pe", lambda e: e.transpose(out, in_, ident), R, W)

    def ACT(out, in_, func, R, W, scale=1.0, bias=None, accum=None):
        def f(e):
            kw = {}
            if bias is not None:
                kw["bias"] = bias
            if accum is not None:
                kw["accum_out"] = accum
            return e.activation(out=out, in_=in_, func=func, scale=scale, **kw)
        P.add("act", f, R, W)

    def TS(q, out, in0, s1, s2, op0, op1, R, W, accum=None):
        def f(e):
            if s2 is None:
                return e.tensor_scalar(out=out, in0=in0, scalar1=s1, scalar2=None, op0=op0)
            if accum is not None:
                return e.tensor_scalar(out=out, in0=in0, scalar1=s1, scalar2=s2, op0=op0, op1=op1,
                                       accum_out=accum)
            return e.tensor_scalar(out=out, in0=in0, scalar1=s1, scalar2=s2, op0=op0, op1=op1)
        P.add(q, f, R, W)

    def TT(q, out, in0, in1, op, R, W):
        P.add(q, lambda e: e.tensor_tensor(out=out, in0=in0, in1=in1, op=op), R, W)

    def STT(q, out, in0, scalar, in1, op0, op1, R, W):
        P.add(q, lambda e: e.scalar_tensor_tensor(out=out, in0=in0, scalar=scalar, in1=in1,
                                                  op0=op0, op1=op1), R, W)

    def CP(q, out, in_, R, W):
        if q == "act":
            P.add("act", lambda e: e.activation(out=out, in_=in_, func=AF.Copy), R, W)
        else:
            P.add(q, lambda e: e.tensor_copy(out=out, in_=in_), R, W)

    def MEMSET(q, ap, val, W):
        P.add(q, lambda e: e.memset(ap, val), (), W)

    def SCAN(out, d0, d1, init, R, W):
        P.add("dve", lambda e: e.tensor_tensor_scan(out=out, data0=d0, data1=d1, initial=init,
                                                    op0=ALU.mult, op1=ALU.add), R, W)

    def RED(out, in_, R, W):
        P.add("dve", lambda e: e.tensor_reduce(out=out, in_=in_, axis=AX.X, op=ALU.add), R, W)

    def RECIP(out, in_, R, W):
        P.add("dve", lambda e: e.reciprocal(out=out, in_=in_), R, W)

    def DMA(out, in_, key, R, W, q="sp", slow=False):
        if slow:
            P.add(q, lambda e: e.dma_start(out=out, in_=in_, allow_slow_non_contiguous=True), R, W, dma=key)
        else:
            P.add(q, lambda e: e.dma_start(out=out, in_=in_), R, W, dma=key)

    ident_f = sb([128, 128])
    ident_b = sb([128, 128], BF16)
    tri_f = sb([128, 128])
    ones_f = sb([128, 128])
    negmask = sb([128, 4, 128], BF16)
    sel65 = sb([128, 64])
    pp = sb([128, 2, 104])
    preg = sb([128, 2, 8])
    wbd = sb([128, 2, 2, 4, 128], BF16)
    c8 = sb([128, 2, 4])
    Aneg = sb([128, 2, 16])
    eps_t = sb([128, 1])

    def lcw(l, c, tap): return pp[:, l, c * 4 + tap: c * 4 + tap + 1]
    def lcb(l, c): return pp[:, l, 16 + c: 17 + c]
    def lba(l, c): return pp[:, l, 20 + c: 21 + c]
    def lbx(l, c): return pp[:, l, 24 + c: 25 + c]
    def lam(l): return pp[:, l, 28:32]
    def scw(l, c, tap): return pp[:, l, 32 + c * 4 + tap: 32 + c * 4 + tap + 1]
    def scb(l, c): return pp[:, l, 80 + c: 81 + c]
    pbc_box = [None]
    def postg(l): return pbc_box[0][:, 0:1024]
    def normg(l): return pbc_box[0][:, 1024:2048]
    def dtb(l): return pbc_box[0][:, 2048:2064]
    def dsk(l): return pbc_box[0][:, 2080:2096]

    m0 = mark()
    stg = sb([128, 2, 2, 4, 128])
    DMA(ident_f, c_ident, "c0", (), [("ident_f",)])
    DMA(tri_f, c_tri, "c1", (), [("tri",)])
    DMA(sel65, c_sel, "c2", (), [("sel",)])
    DMA(pp, pp_d, "c3", (), [("pp",)])
    DMA(preg, preg_d, "c5", (), [("preg",)])
    DMA(stg, wbd_d, "c6", (), [("stg",)])
    for i in range(4):
        DMA(negmask[:, i, :], c_negmask, "c7", (), [("negmask",)])
    CP("dve", ident_b, ident_f, [("ident_f",)], [("ident_b",)])
    CP("dve", wbd, stg, [("stg",)], [("wbd",)])
    MEMSET("pool", ones_f, 1.0, [("ones",)])
    MEMSET("pool", eps_t, EPS, [("eps",)])
    tmp4 = sb([128, 2, 4])
    alg = sb([128, 2, 16])
    for l in range(2):
        DMA(alg[:, l, :], pbc_d[l, :, 2064:2080], "c4", (), [("alg", l)])
    for l in range(2):
        ACT(tmp4[:, l, :], lam(l), AF.Exp, [("pp",)], [("tmp4", l)], scale=-1.0)
        ACT(tmp4[:, l, :], tmp4[:, l, :], AF.Ln, [("tmp4", l), ("ones",)], [("tmp4b", l)], bias=ones_f[:, 0:1])
        TS("dve", c8[:, l, :], tmp4[:, l, :], -8.0, None, ALU.mult, None, [("tmp4b", l)], [("c8", l)])
        ACT(Aneg[:, l, :], alg[:, l, :], AF.Exp, [("alg", l)], [("Aneg0", l)])
        TS("dve", Aneg[:, l, :], Aneg[:, l, :], -1.0, None, ALU.mult, None, [("Aneg0", l)], [("Aneg", l)])
    P.barrier()
    release(m0)
    chk("setup")

    m0 = mark()
    CW = 2824
    wst = [sb([128, CW]) for _ in range(3)]
    wsb = [sb([128, CW], BF16) for _ in range(3)]
    it = 0
    for l in range(2):
        for kc in range(8):
            for cb in range(2):
                s = it % 3
                DMA(wst[s], w_in[l, kc * 128:(kc + 1) * 128, cb * CW:(cb + 1) * CW], ("wst", s), (), [("wst", s)])
                ACT(wsb[s], wst[s], AF.Copy, [("wst", s)], [("wsb", s)], scale=preg[:, l, kc:kc + 1])
                DMA(wib[l, kc * 128:(kc + 1) * 128, cb * CW:(cb + 1) * CW], wsb[s], ("wsbo", s), [("wsb", s)], [("wib",)],
                    q="pool")
                it += 1
    for l in range(2):
        for rc in range(8):
            s = it % 3
            DMA(wst[s][:, 0:2048].rearrange("p (a n) -> p a n", a=2),
                w_out[l, rc * 256:(rc + 1) * 256, :].rearrange("(a p) n -> p a n", p=128), ("wst", s), (), [("wst", s)])
            CP("act", wsb[s][:, 0:2048], wst[s][:, 0:2048], [("wst", s)], [("wsb", s)])
            DMA(wob[l, rc * 256:(rc + 1) * 256, :].rearrange("(a p) n -> p a n", p=128),
                wsb[s][:, 0:2048].rearrange("p (a n) -> p a n", a=2), ("wsbo", s), [("wsb", s)], [("wob",)], q="pool")
            it += 1
    P.barrier()
    release(m0)
    chk("prologue")

    def front_alloc():
        c = Ctx()
        c.xt = sb([128, 2, D])
        c.hb = sb([128, 2, D], BF16)
        c.hT = sb([128, 8, T], BF16)
        c.ssq = sb([128, 2])
        c.rstd = sb([128, 2])
        return c

    def front(c, src, t0, tag):
        for j in range(2):
            DMA(c.xt[:, j, :], src[t0 + j * 128: t0 + (j + 1) * 128, :], ("xt", tag), (), [("xt", j)])
        for j in range(2):
            ACT(c.hb[:, j, :], c.xt[:, j, :], AF.Square, [("xt", j)], [("hb", j), ("ssq", j)], accum=c.ssq[:, j:j + 1])
        ACT(c.rstd, c.ssq, AF.Sqrt, [("ssq", 0), ("ssq", 1)], [("rstd0",)], scale=1.0 / D, bias=eps_t[:, 0:1])
        RECIP(c.rstd, c.rstd, [("rstd0",)], [("rstd",)])
        for j in range(2):
            ACT(c.hb[:, j, :], c.xt[:, j, :], AF.Copy, [("xt", j), ("rstd",)], [("hb", j)], scale=c.rstd[:, j:j + 1])
        for kc in range(8):
            bank = 6 + (kc % 2)
            pt = psb[bank].bitcast(BF16)[:, 0:T]
            for j in range(2):
                TR(pt[:, j * 128:(j + 1) * 128], c.hb[:, j, kc * 128:(kc + 1) * 128], ident_b,
                   [("hb", j)], [("ps", bank)])
            CP("dve" if kc % 2 == 0 else "act", c.hT[:, kc, :], pt, [("ps", bank)], [("hT", kc)])

    HT_ALL = [("hT", kc) for kc in range(8)]

    def load_w(wt, l, col0, ncols, key):
        DMA(wt[:, :, 0:ncols], wib[l].rearrange("(kc p) n -> p kc n", p=128)[:, :, col0:col0 + ncols],
            key, (), [key])

    def proj_fm(wt, wkey, c, hT, ncols_t, col, M, out_ps, pskey):
        for kc in range(8):
            MM(out_ps, wt[:, kc, col:col + M], hT[:, kc, 0:ncols_t], kc == 0, kc == 7,
               [wkey, ("hT", kc)], [pskey])

    def proj_tm(wt, wkey, hT, j, col, N, out_ps, pskey):
        for kc in range(8):
            MM(out_ps, hT[:, kc, j * 128:(j + 1) * 128], wt[:, kc, col:col + N], kc == 0, kc == 7,
               [wkey, ("hT", kc)], [pskey])

    xs_res = sb([NTOK, D])
    E_t = sb([16, 8, 128])
    pdt = sb([16, 2])
    pps = sb([128, 2, 32])
    Aexp = sb([128, 2, 8])
    DMA(xs_res, xs_d, "s0", (), [("xs_res",)])
    DMA(E_t, c_E, "s1", (), [("E",)])
    DMA(pdt, pdt_d, "s2", (), [("pdt",)])
    DMA(pps, pps_d, "s3", (), [("pps",)])
    for l in range(2):
        ACT(Aexp[:, l, :], pps[:, l, 8:16], AF.Exp, [("pps",)], [("Aexp0", l)])
        TS("dve", Aexp[:, l, :], Aexp[:, l, :], -1.0, None, ALU.mult, None, [("Aexp0", l)], [("Aexp", l)])
    P.barrier()
    m_layer = mark()

    for l in range(2):
        src = xp if l == 0 else resid
        release(m_layer)
        KT = sb([128, 4, SEQ], BF16)
        V1 = sb([128, NBLK, 8, 66], BF16)
        mtab = sb([128, 8, 17, 128], BF16)
        DMA(mtab.rearrange("p h d q -> p (h d q)"), c_mtab, "c8", (), [("mtab",)])
        MEMSET("pool", V1[:, :, :, 64:65], 1.0, [("V1ones",)])
        fc = front_alloc()
        wA = [sb([128, 8, 512], BF16) for _ in range(3)]
        woL = sb([128, 4, 512], BF16)
        woA = sb([64, 8, 512], BF16)
        xl = sb([128, 4, T + 3])
        sg = sb([128, 4, T])
        xc = sb([128, T])
        xcb = sb([128, T], BF16)
        gr = sb([128, T])
        gi = sb([128, T])
        ga = sb([128, T])
        gb = sb([128, T])
        hs = sb([128, 4, T])
        hprev = sb([128, 4])
        mixL = sb([128, 4, T], BF16)
        mixA = sb([64, 8, T], BF16)
        QT = sb([128, 4, T], BF16)
        sga = sb([64, 8, T])
        Pt = [sb([128, T], BF16) for _ in range(4)]
        Oacc = sb([65, T])
        rl = sb([64, T])
        t1 = sb([64, T])
        o1 = fc.xt
        MEMSET("pool", xl[:, :, 0:3], 0.0, [("xlh", c) for c in range(4)])
        MEMSET("pool", hprev, 0.0, [("hprev", c) for c in range(4)])
        P.barrier()

        for ti in range(NT):
            t0 = ti * T
            in_win = t0 >= SEQ - KWIN
            front(fc, src, t0, "A")
            chk("A_front")
            load_w(wA[0], l, 0, 512, ("wA", 0))
            load_w(wA[1], l, 512, 512, ("wA", 1))
            load_w(wA[2], l, 1024, 512, ("wA", 2))
            for c in range(8):
                bank = c % 2
                ps = psb[bank][:, 0:T]
                wi = c // 4
                proj_fm(wA[wi], ("wA", wi), c, fc.hT, T, (c % 4) * 128, 128, ps, ("ps", bank))
                if c < 4:
                    CP("act", xl[:, c, 3:3 + T], ps, [("ps", bank)], [("xl", c)])
                else:
                    ACT(sg[:, c - 4, :], ps, AF.Silu, [("ps", bank)], [("sg", c - 4)])
            load_w(wA[0], l, 1536, 512, ("wA", 0))
            load_w(wA[1], l, 2048, 512, ("wA", 1))
            chk("A_lruproj")
            for c in range(4):
                TS("dve", xc, xl[:, c, 0:T], lcw(l, c, 0), lcb(l, c), ALU.mult, ALU.add,
                   [("xl", c), ("xlh", c)], [("xc",)])
                for tap in range(1, 4):
                    STT("dve", xc, xl[:, c, tap:tap + T], lcw(l, c, tap), xc, ALU.mult, ALU.add,
                        [("xl", c), ("xlh", c), ("xc",)], [("xc",)])
                CP("pool", xcb, xc, [("xc",)], [("xcb",)])
                CP("pool", xl[:, c, 0:3], xl[:, c, T:T + 3], [("xl", c), ("xlh", c)], [("xlh", c)])
                bank = 2 + (c % 2) * 2
                MM(psb[bank][:, 0:T], wbd[:, l, 0, c, :], xcb, True, True, [("xcb",)], [("ps", bank)])
                MM(psb[bank + 1][:, 0:T], wbd[:, l, 1, c, :], xcb, True, True, [("xcb",)], [("ps", bank + 1)])
                ACT(gr, psb[bank][:, 0:T], AF.Sigmoid, [("ps", bank)], [("gr",)], bias=lba(l, c))
                ACT(gi, psb[bank + 1][:, 0:T], AF.Sigmoid, [("ps", bank + 1)], [("gi",)], bias=lbx(l, c))
                ACT(ga, gr, AF.Exp, [("gr",)], [("ga",)], scale=c8[:, l, c:c + 1])
                TT("pool", gb, ga, ga, ALU.mult, [("ga",)], [("gb",)])
                TS("pool", gb, gb, -1.0, 1.0, ALU.mult, ALU.add, [("gb",)], [("gb",)])
                ACT(gb, gb, AF.Sqrt, [("gb",)], [("gb",)])
                TT("dve", gi, gi, xc, ALU.mult, [("gi",), ("xc",)], [("gi",)])
                TT("dve", gb, gb, gi, ALU.mult, [("gb",), ("gi",)], [("gb",)])
                SCAN(hs[:, c, :], ga, gb, hprev[:, c:c + 1], [("ga",), ("gb",), ("hprev", c)], [("hs", c)])
                CP("pool", hprev[:, c:c + 1], hs[:, c, T - 1:T], [("hs", c)], [("hprev", c)])
                TT("dve", mixL[:, c, :], hs[:, c, :], sg[:, c, :], ALU.mult, [("hs", c), ("sg", c)], [("mixL", c)])
            chk("A_lru")
            for c in range(4):
                bank = c % 2
                ps = psb[bank][:, 0:T]
                proj_fm(wA[2], ("wA", 2), c, fc.hT, T, c * 128, 128, ps, ("ps", bank))
                CP("act", QT[:, c, :], ps, [("ps", bank)], [("QT", c)])
            chk("A_q")
            load_w(wA[2], l, 2560, 512, ("wA", 2))
            for c in range(4):
                bank = c % 2
                ps = psb[bank][:, 0:T]
                proj_fm(wA[0], ("wA", 0), c, fc.hT, T, c * 128, 128, ps, ("ps", bank))
                CP("dve", KT[:, c, t0:t0 + T], ps, [("ps", bank)], [("KT", ti)])
            chk("A_k")
            if in_win:
                for j in range(2):
                    bank = 2 + j
                    proj_tm(wA[0], ("wA", 0), fc.hT, j, 0, 512, psb[bank], ("ps", bank))
                    CP("act", o1[:, j, 0:512], psb[bank], [("ps", bank)], [("xt", j)])
                    r0 = t0 + j * 128 - (SEQ - KWIN)
                    DMA(k_p[l, r0:r0 + 128, :], o1[:, j, 0:512], ("kst", j), [("xt", j)], [("k_p",)])
            chk("A_ktok")
            for j in range(2):
                bank = 4 + j
                blk = ti * 2 + j
                proj_tm(wA[1], ("wA", 1), fc.hT, j, 0, 512, psb[bank], ("ps", bank))
                chk("A_v1")
                for hh_ in range(8):
                    CP("dve", V1[:, blk, hh_, 0:64], psb[bank][:, hh_ * 64:(hh_ + 1) * 64], [("ps", bank)], [("V1", blk)])
                chk("A_v2")
                if in_win:
                    CP("act", o1[:, j, 512:1024], psb[bank], [("ps", bank), ("V1", blk)], [("xt", j)])
                    chk("A_v3")
                    r0 = t0 + j * 128 - (SEQ - KWIN)
                    DMA(v_p[l, r0:r0 + 128, :], o1[:, j, 512:1024], ("vst", j), [("xt", j)], [("v_p",)])
            chk("A_v")
            for h in range(8):
                bank = h % 2
                ps = psb[bank][0:64, 0:T]
                proj_fm(wA[2], ("wA", 2), h, fc.hT, T, h * 64, 64, ps, ("ps", bank))
                ACT(sga[:, h, :], ps, AF.Silu, [("ps", bank)], [("sga", h)])
            chk("A_attnproj")
            b0 = ti * 2
            units = []
            for h in range(8):
                kbs = [b0, b0 + 1] + list(range(b0 - 1, max(-1, b0 - 17), -1))
                for kb in kbs:
                    units.append((h, kb, kb == kbs[0], kb == kbs[-1]))
            pend = []

            def att_front(un, h, kb):
                pr, pb = h // 2, (h % 2) * 64
                d0 = b0 - kb
                if d0 < 0:
                    c0_, c1_, dl = 128, 256, 0
                elif d0 == 16:
                    c0_, c1_, dl = 0, 128, 16
                else:
                    c0_, c1_, dl = 0, 256, d0
                n = c1_ - c0_
                sbank = un % 4
                sps = psb[sbank][:, 0:n]
                MM(sps, KT[pb:pb + 64, pr, kb * 128:(kb + 1) * 128], QT[pb:pb + 64, pr, c0_:c1_], True, True,
                   [("KT", kb // 2), ("QT", pr)], [("ps", sbank)])
                pt = Pt[un % 4]
                ACT(pt[:, 0:n], sps, AF.Exp, [("ps", sbank)], [("Pt", un % 4)], scale=0.125)
                TT("dve" if un % 3 == 2 else "pool", pt[:, 0:n], pt[:, 0:n],
                   mtab[:, h, dl:dl + n // 128, :].rearrange("p a q -> p (a q)"),
                   ALU.mult, [("Pt", un % 4)], [("Pt", un % 4)])
                return c0_, c1_, n

            def att_pv(un, h, kb, first, last, c0_, c1_, n):
                obank = 4 + (h % 2)
                ops = psb[obank][0:65, 0:T]
                MM(ops[:, c0_:c1_], V1[:, kb, h, 0:65], Pt[un % 4][:, 0:n], first, last,
                   [("Pt", un % 4), ("V1", kb)], [("ps", obank)])
                if last:
                    pend.append([3, lambda: att_finish(h)])

            def att_finish(h):
                obank = 4 + (h % 2)
                ops = psb[obank][0:65, 0:T]
                CP("dve", Oacc, ops, [("ps", obank)], [("Oacc",)])
                lbank = 6 + (h % 2)
                MM(psb[lbank][0:64, 0:T], sel65[0:65, :], Oacc, True, True, [("Oacc",)], [("ps", lbank)])
                RECIP(rl, psb[lbank][0:64, 0:T], [("ps", lbank)], [("rl",)])
                TT("dve", t1, Oacc[0:64, :], rl, ALU.mult, [("Oacc",), ("rl",)], [("t1",)])
                TT("dve", mixA[:, h, :], t1, sga[:, h, :], ALU.mult, [("t1",), ("sga", h)], [("mixA", h)])

            def tick():
                for p_ in pend:
                    p_[0] -= 1
                while pend and pend[0][0] <= 0:
                    pend.pop(0)[1]()

            for un, (h, kb, first, last) in enumerate(units):
                c0_, c1_, n = att_front(un, h, kb)
                pend.append([3, (lambda un=un, h=h, kb=kb, first=first, last=last, c0_=c0_, c1_=c1_, n=n:
                                 att_pv(un, h, kb, first, last, c0_, c1_, n))])
                tick()
            while pend:
                tick()
            chk("A_attn")
            for half in range(2):
                hsl = slice(half * 512, (half + 1) * 512)
                DMA(woL, wob[l, 0:512, hsl].rearrange("(c p) n -> p c n", p=128), ("woL",), (), [("woL",)])
                DMA(woA, wob[l, 512:1024, hsl].rearrange("(h p) n -> p h n", p=64), ("woA",), (), [("woA",)])
                for j in range(2):
                    bank = 2 * j + half
                    ps = psb[bank]
                    for c in range(4):
                        MM(ps, mixL[:, c, j * 128:(j + 1) * 128], woL[:, c, :], c == 0, False,
                           [("mixL", c), ("woL",)], [("ps", bank)])
                    for h in range(8):
                        MM(ps, mixA[:, h, j * 128:(j + 1) * 128], woA[:, h, :], False, h == 7,
                           [("mixA", h), ("woA",)], [("ps", bank)])
                    CP("act" if j == 0 else "dve", o1[:, j, hsl], ps, [("ps", bank)], [("xt", j)])
            for j in range(2):
                DMA(out1[t0 + j * 128: t0 + (j + 1) * 128, :], o1[:, j, :], ("o1", j), [("xt", j)], [("out1", ti)])
        DMA(lru_conv_p[l], xl[:, :, 0:3], ("fin", 0), [("xlh", c) for c in range(4)], [("lcp",)])
        DMA(lru_h_p[l], hprev, ("fin", 1), [("hprev", c) for c in range(4)], [("lhp",)])
        P.barrier()
        chk("A%d" % l)

        release(m_layer)
        fc = front_alloc()
        pbc = sb([128, 2096])
        pbc_box[0] = pbc
        DMA(pbc, pbc_d[l], ("pbc",), (), [("pbc",)])
        wB = [sb([128, 8, 512], BF16) for _ in range(3)]
        wdt = sb([128, 8, 16], BF16)
        woS = sb([128, 8, D], BF16)
        xb = sb([128, 12, T + 3])
        xa2 = [sb([128, T]), sb([128, T])]
        xsT = sb([128, 8, T])
        BT = sb([128, 2, T], BF16)
        CT = sb([128, 2, T], BF16)
        Btok = sb([128, 2, 128], BF16)
        sz = sb([128, 2, D])
        dtt = sb([128, 2, 16])
        dtA = sb([128, 16])
        ncum = sb([128, 16])
        ecum = sb([128, 16])
        wend = sb([128, 16])
        dec = sb([128, 16])
        xs_f = sb([128, D])
        xs_b = sb([128, D], BF16)
        rhsb = sb([128, 16, 128])
        decay = sb([128, 16, 128])
        GT = sb([128, 2, 128])
        MT = sb([128, 16, 128], BF16)
        hS = sb([128, D])
        hSb = sb([128, D], BF16)
        xw = sb([128, D], BF16)
        ya = sb([128, D])
        yb = sb([128, D])
        yt = sb([128, D], BF16)
        gss = sb([128, 2])
        mixS = sb([128, 8, T], BF16)
        o1b = sb([128, 2, D])
        yo = sb([128, 2, D])
        ss2 = sb([128, 2])

        DMA(woS, wob[l, 1024:2048, :].rearrange("(c p) n -> p c n", p=128), ("woS",), (), [("woS",)])
        MEMSET("pool", xb[:, :, 0:3], 0.0, [("xbh", c) for c in range(12)])
        MEMSET("pool", hS, 0.0, [("hS",)])
        P.barrier()

        for ti in range(NT):
            t0 = ti * T
            front(fc, src, t0, "B")
            for j in range(2):
                DMA(o1b[:, j, :], out1[t0 + j * 128: t0 + (j + 1) * 128, :], ("o1b", j), [("out1", ti)], [("o1b", j)])
            load_w(wB[0], l, 3072, 512, ("wB", 0))
            load_w(wB[1], l, 3584, 512, ("wB", 1))
            load_w(wB[2], l, 4096, 512, ("wB", 2))
            load_w(wdt, l, 5632, 16, ("wdt",))
            for j in range(2):
                for half in range(2):
                    bank = 2 * j + half
                    proj_tm(wB[half], ("wB", half), fc.hT, j, 0, 512, psb[bank], ("ps", bank))
                    ACT(sz[:, j, half * 512:(half + 1) * 512], psb[bank], AF.Silu, [("ps", bank)], [("sz", j)])
            load_w(wB[0], l, 4608, 512, ("wB", 0))
            load_w(wB[1], l, 5120, 512, ("wB", 1))
            for c in range(12):
                bank = 4 + c % 2
                ps = psb[bank][:, 0:T]
                wi = (2, 0, 1)[c // 4]
                proj_fm(wB[wi], ("wB", wi), c, fc.hT, T, (c % 4) * 128, 128, ps, ("ps", bank))
                CP("act" if c % 2 == 0 else "dve", xb[:, c, 3:3 + T], ps, [("ps", bank)], [("xb", c)])
            for j in range(2):
                bank = 6 + j
                proj_tm(wdt, ("wdt",), fc.hT, j, 0, 16, psb[bank][:, 0:16], ("ps", bank))
                TT("dve", dtt[:, j, :], psb[bank][:, 0:16], dtb(l), ALU.add, [("ps", bank), ("pbc",)], [("dtt", j)])
                ACT(dtt[:, j, :], dtt[:, j, :], AF.Exp, [("dtt", j)], [("dtt", j)])
                ACT(dtt[:, j, :], dtt[:, j, :], AF.Ln, [("dtt", j)], [("dtt", j)], bias=ones_f[:, 0:1])
            for c in range(12):
                q = "dve"
                xa = xa2[c % 2]
                xak = ("xa", c % 2)
                TS(q, xa, xb[:, c, 0:T], scw(l, c, 0), scb(l, c), ALU.mult, ALU.add,
                   [("xb", c), ("xbh", c)], [xak])
                for tap in range(1, 4):
                    STT(q, xa, xb[:, c, tap:tap + T], scw(l, c, tap), xa, ALU.mult, ALU.add,
                        [("xb", c), ("xbh", c), xak], [xak])
                CP("pool", xb[:, c, 0:3], xb[:, c, T:T + 3], [("xb", c), ("xbh", c)], [("xbh", c)])
                if c < 8:
                    ACT(xsT[:, c, :], xa, AF.Silu, [xak], [("xsT", c)])
                elif c < 10:
                    ACT(BT[:, c - 8, :], xa, AF.Silu, [xak], [("BT", c - 8)])
                else:
                    ACT(CT[:, c - 10, :], xa, AF.Silu, [xak], [("CT", c - 10)])
            for j in range(2):
                js = slice(j * 128, (j + 1) * 128)
                for half in range(2):
                    bank = half
                    for cc in range(4):
                        c = half * 4 + cc
                        TR(psb[bank][:, cc * 128:(cc + 1) * 128], xsT[:, c, js], ident_f, [("xsT", c)], [("ps", bank)])
                    CP("act", xs_f[:, half * 512:(half + 1) * 512], psb[bank], [("ps", bank)], [("xs_f", half)])
                    CP("dve", xs_b[:, half * 512:(half + 1) * 512], psb[bank], [("ps", bank)], [("xs_b", half)])
                bank = 2
                ptb = psb[bank].bitcast(BF16)[:, 0:256]
                for g in range(2):
                    TR(ptb[:, g * 128:(g + 1) * 128], BT[:, g, js], ident_b, [("BT", g)], [("ps", bank)])
                CP("act", Btok.rearrange("p g n -> p (g n)"), ptb, [("ps", bank)], [("Btok",)])
                TT("dve", dtA, dtt[:, j, :], Aneg[:, l, :], ALU.mult, [("dtt", j)], [("dtA",)])
                bank = 3
                MM(psb[bank][:, 0:16], tri_f, dtA, True, True, [("dtA",)], [("ps", bank)])
                MM(psb[bank][:, 16:32], ones_f, dtA, True, True, [("dtA",)], [("ps", bank)])
                TS("dve", ncum, psb[bank][:, 0:16], -1.0, None, ALU.mult, None, [("ps", bank)], [("ncum",)])
                ACT(ecum, psb[bank][:, 0:16], AF.Exp, [("ps", bank)], [("ecum",)])
                ACT(dec, psb[bank][:, 16:32], AF.Exp, [("ps", bank)], [("dec",)])
                TT("dve", wend, psb[bank][:, 16:32], ncum, ALU.add, [("ps", bank), ("ncum",)], [("wend",)])
                ACT(wend, wend, AF.Exp, [("wend",)], [("wend",)])
                TT("dve", wend, wend, dtt[:, j, :], ALU.mult, [("wend",), ("dtt", j)], [("wend",)])
                TT("pool", rhsb, tri_f.unsqueeze(1).to_broadcast([128, 16, 128]),
                   dtA.unsqueeze(2).to_broadcast([128, 16, 128]), ALU.mult, [("dtA",)], [("rhsb",)])
                bank = 2
                for g in range(2):
                    MM(psb[bank][:, 256 + g * 128:256 + (g + 1) * 128], BT[:, g, js], CT[:, g, js], True, True,
                       [("BT", g), ("CT", g)], [("ps", bank)])
                CP("act", GT.rearrange("p g n -> p (g n)"), psb[bank][:, 256:512], [("ps", bank)], [("GT",)])
                for q4 in range(4):
                    bank = 4 + q4
                    MM(psb[bank], ones_f, rhsb[:, q4 * 4:(q4 + 1) * 4, :].rearrange("p h i -> p (h i)"), True, False,
                       [("rhsb",)], [("ps", bank)])
                    MM(psb[bank], ident_b, negmask.rearrange("p a i -> p (a i)"), False, True, (), [("ps", bank)])
                    for hh in range(4):
                        h = q4 * 4 + hh
                        ACT(decay[:, h, :], psb[bank][:, hh * 128:(hh + 1) * 128], AF.Exp, [("ps", bank), ("ncum",)],
                            [("decay", h)], bias=ncum[:, h:h + 1])
                        STT("dve", MT[:, h, :], decay[:, h, :], dtt[:, j, h:h + 1], GT[:, h // 8, :], ALU.mult, ALU.mult,
                            [("decay", h), ("dtt", j), ("GT",)], [("MT", h)])
                CP("pool", hSb, hS, [("hS",)], [("hSb",)])
                for h in range(16):
                    bank = h // 8
                    MM(psb[bank][:, (h % 8) * 64:(h % 8 + 1) * 64], MT[:, h, :], xs_b[:, h * 64:(h + 1) * 64], True, True,
                       [("MT", h), ("xs_b", h // 8)], [("ps", bank)])
                for g in range(2):
                    bank = 2 + g
                    MM(psb[bank], CT[:, g, js], hSb[:, g * 512:(g + 1) * 512], True, True, [("CT", g), ("hSb",)],
                       [("ps", bank)])
                for g in range(2):
                    gs = slice(g * 512, (g + 1) * 512)
                    TT("dve", ya[:, gs].rearrange("p (h d) -> p h d", h=8), psb[2 + g].rearrange("p (h d) -> p h d", h=8),
                       ecum[:, g * 8:(g + 1) * 8].unsqueeze(2).to_broadcast([128, 8, 64]), ALU.mult,
                       [("ps", 2 + g), ("ecum",)], [("ya", g)])
                    TT("dve", ya[:, gs], ya[:, gs], psb[g], ALU.add, [("ya", g), ("ps", g)], [("ya", g)])
                    TT("pool", yb[:, gs].rearrange("p (h d) -> p h d", h=8), xs_f[:, gs].rearrange("p (h d) -> p h d", h=8),
                       dsk(l)[:, g * 8:(g + 1) * 8].unsqueeze(2).to_broadcast([128, 8, 64]), ALU.mult,
                       [("xs_f", g)], [("yb", g)])
                    TT("dve", ya[:, gs], ya[:, gs], yb[:, gs], ALU.add, [("ya", g), ("yb", g)], [("ya", g)])
                    TT("dve", ya[:, gs], ya[:, gs], sz[:, j, gs], ALU.mult, [("ya", g), ("sz", j)], [("ya", g)])
                    ACT(yb[:, gs], ya[:, gs], AF.Square, [("ya", g), ("yb", g)], [("yb", g), ("gss", g)],
                        accum=gss[:, g:g + 1])
                ACT(gss, gss, AF.Sqrt, [("gss", 0), ("gss", 1)], [("gss2",)], scale=1.0 / 512, bias=eps_t[:, 0:1])
                RECIP(gss, gss, [("gss2",)], [("gss3",)])
                for g in range(2):
                    gs = slice(g * 512, (g + 1) * 512)
                    STT("dve", yt[:, gs], ya[:, gs], gss[:, g:g + 1], normg(l)[:, gs], ALU.mult, ALU.mult,
                        [("ya", g), ("gss3",)], [("yt", g)])
                TT("pool", xw.rearrange("p (h d) -> p h d", h=16), xs_f.rearrange("p (h d) -> p h d", h=16),
                   wend.unsqueeze(2).to_broadcast([128, 16, 64]), ALU.mult, [("xs_f", 0), ("xs_f", 1), ("wend",)], [("xw",)])
                for g in range(2):
                    bank = 4 + g
                    MM(psb[bank], Btok[:, g, :], xw[:, g * 512:(g + 1) * 512], True, True, [("Btok",), ("xw",)], [("ps", bank)])
                TT("pool", hS.rearrange("p (h d) -> p h d", h=16), hS.rearrange("p (h d) -> p h d", h=16),
                   dec.unsqueeze(2).to_broadcast([128, 16, 64]), ALU.mult, [("hS",), ("hSb",), ("dec",)], [("hS",)])
                for g in range(2):
                    gs = slice(g * 512, (g + 1) * 512)
                    TT("dve", hS[:, gs], hS[:, gs], psb[4 + g], ALU.add, [("hS",), ("ps", 4 + g)], [("hS",)])
                for half in range(2):
                    bank = 6 + half
                    ptb = psb[bank].bitcast(BF16)
                    for cc in range(4):
                        c = half * 4 + cc
                        TR(ptb[:, cc * 128:(cc + 1) * 128], yt[:, c * 128:(c + 1) * 128], ident_b, [("yt", c // 4)],
                           [("ps", bank)])
                    CP("act", mixS[:, half * 4:(half + 1) * 4, js], ptb[:, 0:512].rearrange("p (c t) -> p c t", c=4),
                       [("ps", bank)], [("mixS", j)])
            for j in range(2):
                for half in range(2):
                    bank = 2 * j + half
                    hsl = slice(half * 512, (half + 1) * 512)
                    for c in range(8):
                        MM(psb[bank], mixS[:, c, j * 128:(j + 1) * 128], woS[:, c, hsl], c == 0, c == 7,
                           [("mixS", j), ("woS",)], [("ps", bank)])
                    TT("dve", o1b[:, j, hsl], o1b[:, j, hsl], psb[bank], ALU.add, [("o1b", j), ("ps", bank)], [("o1b", j)])
                ACT(yo[:, j, :], o1b[:, j, :], AF.Square, [("o1b", j), ("yo", j)], [("yo", j), ("ss2", j)],
                    accum=ss2[:, j:j + 1])
            ACT(ss2, ss2, AF.Sqrt, [("ss2", 0), ("ss2", 1)], [("ss2b",)], scale=1.0 / D, bias=eps_t[:, 0:1])
            RECIP(ss2, ss2, [("ss2b",)], [("ss2c",)])
            for j in range(2):
                STT("dve", yo[:, j, :], o1b[:, j, :], ss2[:, j:j + 1], postg(l), ALU.mult, ALU.mult,
                    [("o1b", j), ("ss2c",), ("yo", j)], [("yo", j)])
                TT("pool", yo[:, j, :], yo[:, j, :], fc.xt[:, j, :], ALU.add, [("yo", j), ("xt", j)], [("yo", j)])
                dst = resid if l == 0 else yp
                DMA(dst[t0 + j * 128: t0 + (j + 1) * 128, :], yo[:, j, :], ("yo", j), [("yo", j)], [("dst",)])
        DMA(ssd_conv_p[l], xb[:, :, 0:3], ("fin", 2), [("xbh", c) for c in range(12)], [("scp",)])
        for half in range(2):
            for cc in range(4):
                c = half * 4 + cc
                bank = c % 4
                TR(psb[bank][:, 0:128], hS[:, c * 128:(c + 1) * 128], ident_f, [("hS",)], [("ps", bank)])
                CP("act", ya[:, c * 128:(c + 1) * 128], psb[bank][:, 0:128], [("ps", bank)], [("ya", c)])
                DMA(ssd_h_p[l, c * 128:(c + 1) * 128, :], ya[:, c * 128:(c + 1) * 128], ("fin", 3 + c), [("ya", c)], [("shp",)])
        P.barrier()
        chk("B%d" % l)


        release(m_layer)
        N = NTOK
        pbc = sb([128, 2096])
        pbc_box[0] = pbc
        DMA(pbc, pbc_d[l], ("pbc",), (), [("pbc",)])
        hbs = sb([N, D], BF16)
        hTs = sb([128, 8, N], BF16)
        ssq_s = sb([N, 1])
        wS = [sb([128, 8, 512], BF16) for _ in range(3)]
        wdt = sb([128, 8, 16], BF16)
        ms_t = sb([128, 7 * 32], BF16)
        mn_t = sb([N, NB * 32], BF16)
        DMA(ms_t, c_ms, ("ms",), (), [("ms",)])
        DMA(mn_t, c_mn, ("mn",), (), [("mn",)])
        xls = sb([128, 4, NB, 7])
        sgl = sb([128, 4, N])
        xc_s = sb([128, N])
        xcb_s = sb([128, N], BF16)
        gr_s = sb([128, N]); gi_s = sb([128, N]); ga_s = sb([128, N]); gb_s = sb([128, N])
        hs_s = sb([128, 4, NB, 4])
        h0_s = sb([128, 4, NB])
        mixLs = sb([128, 4, N], BF16)
        qTs = sb([128, 4, N], BF16)
        kTs = sb([128, 4, N], BF16)
        ktok = sb([N, 512])
        vtok = sb([N, 512])
        Vn = sb([N, 8, 66], BF16)
        sga_s = sb([64, 8, N])
        Kf = sb([128, 7, 512])
        Vf = sb([128, 7, 512])
        Kb = [sb([128, 7, 512], BF16) for _ in range(2)]
        Vb = [sb([128, 7, 8, 66], BF16) for _ in range(2)]
        KTs = [sb([128, 7, 4, 128], BF16) for _ in range(2)]
        Pse = [sb([128, 7 * 32], BF16) for _ in range(2)]
        Pn = [sb([N, 32], BF16) for _ in range(2)]
        Oacc_s = sb([65, NB * 32])
        rl_s = sb([64, NB * 32])
        mixAs = sb([64, 8, N], BF16)
        xbs = sb([128, 12, NB, 7])
        xa_s = sb([128, N])
        xsTs = sb([128, 8, N])
        BCT = sb([128, 4, N])
        BCtok = sb([N, 512])
        zs = sb([128, 8, N])
        dtT = sb([16, N])
        dtx = sb([128, 8, N])
        decs = sb([128, 8, N])
        xdt = sb([128, 8, N])
        Hs = [sb([128, 8, 128]) for _ in range(2)]
        bcs = [sb([128, 512]) for _ in range(2)]
        tmpa2 = [sb([128, 8, 128]) for _ in range(2)]
        tmpb2 = [sb([128, 8, 128]) for _ in range(2)]
        ysT = sb([128, 8, N])
        ysq = sb([128, 8, N])
        grs = sb([128, 2, N])
        mixSs = sb([128, 8, N], BF16)
        woLs = wS[0][:, 0:4, :]
        woAs = wS[2][0:64, :, :]
        woSs = wS[1]
        o_s = sb([N, D])
        y_s = sb([N, D])
        for b in range(2):
            MEMSET("pool", Vb[b][:, :, :, 64:65], 1.0, [("Vb1", b)])
        MEMSET("pool", Vn[:, :, 64:65], 1.0, [("Vn1",)])
        DMA(xls[:, :, :, 0:3], s_lc[l], ("slc",), (), [("xls_h",)])
        DMA(h0_s, s_lh[l], ("slh",), (), [("h0_s",)])
        DMA(xbs[:, :, :, 0:3], s_sc[l], ("ssc",), (), [("xbs_h",)])
        P.barrier()

        ACT(hbs, xs_res, AF.Square, [("xs_res",)], [("hbs",), ("ssq_s",)], accum=ssq_s[:, 0:1])
        ACT(ssq_s, ssq_s, AF.Sqrt, [("ssq_s",)], [("ssq_s",)], scale=1.0 / D, bias=eps_t[0:N, 0:1])
        RECIP(ssq_s, ssq_s, [("ssq_s",)], [("ssq_s",)])
        ACT(hbs, xs_res, AF.Copy, [("xs_res",), ("ssq_s",)], [("hbs",)], scale=ssq_s[:, 0:1])
        for kc in range(8):
            bank = 6 + (kc % 2)
            pt = psb[bank].bitcast(BF16)[:, 0:N]
            TR(pt, hbs[:, kc * 128:(kc + 1) * 128], ident_b[0:N, 0:N], [("hbs",)], [("ps", bank)])
            CP("dve" if kc % 2 == 0 else "act", hTs[:, kc, :], pt, [("ps", bank)], [("hT", kc)])

        chk("S_front")
        def wl(i, col0, ncols=512):
            load_w(wS[i], l, col0, ncols, ("wS", i))
        wl(0, 0); wl(1, 512); wl(2, 1024)
        load_w(wdt, l, 5632, 16, ("wdt",))
        for c in range(4):
            bank = c % 2
            ps = psb[bank][:, 0:N]
            proj_fm(wS[0], ("wS", 0), c, hTs, N, c * 128, 128, ps, ("ps", bank))
            CP("act", xls[:, c, :, 3:7], ps.rearrange("p (b i) -> p b i", i=4), [("ps", bank)], [("xls", c)])
        wl(0, 1536)
        for c in range(4):
            bank = c % 2
            ps = psb[bank][:, 0:N]
            proj_fm(wS[1], ("wS", 1), c, hTs, N, c * 128, 128, ps, ("ps", bank))
            ACT(sgl[:, c, :], ps, AF.Silu, [("ps", bank)], [("sgl", c)])
        wl(1, 2048)
        for c in range(4):
            bank = c % 2
            ps = psb[bank][:, 0:N]
            proj_fm(wS[2], ("wS", 2), c, hTs, N, c * 128, 128, ps, ("ps", bank))
            CP("act", qTs[:, c, :], ps, [("ps", bank)], [("qTs",)])
        wl(2, 2560)
        for c in range(4):
            bank = c % 2
            ps = psb[bank][:, 0:N]
            proj_fm(wS[0], ("wS", 0), c, hTs, N, c * 128, 128, ps, ("ps", bank))
            CP("act", kTs[:, c, :], ps, [("ps", bank)], [("kTs",)])
        bank = 2
        for kc in range(8):
            MM(psb[bank][0:N, :], hTs[:, kc, :], wS[0][:, kc, :], kc == 0, kc == 7, [("wS", 0), ("hT", kc)], [("ps", bank)])
        CP("act", ktok, psb[bank][0:N, :], [("ps", bank)], [("ktok",)])
        DMA(k_s[l], ktok, ("ktok",), [("ktok",)], [("k_s",)])
        wl(0, 3072)
        bank = 3
        for kc in range(8):
            MM(psb[bank][0:N, :], hTs[:, kc, :], wS[1][:, kc, :], kc == 0, kc == 7, [("wS", 1), ("hT", kc)], [("ps", bank)])
        CP("act", vtok, psb[bank][0:N, :], [("ps", bank)], [("vtok",)])
        CP("dve", Vn[:, :, 0:64], psb[bank][0:N, :].rearrange("p (h d) -> p h d", h=8), [("ps", bank), ("Vn1",)], [("Vn",)])
        DMA(v_s[l], vtok, ("vtok",), [("vtok",)], [("v_s",)])
        wl(1, 3584)
        for h in range(8):
            bank = h % 2
            ps = psb[bank][0:64, 0:N]
            proj_fm(wS[2], ("wS", 2), h, hTs, N, h * 64, 64, ps, ("ps", bank))
            ACT(sga_s[:, h, :], ps, AF.Silu, [("ps", bank)], [("sga_s",)])
        wl(2, 4096)
        for c in range(8):
            bank = c % 2
            ps = psb[bank][:, 0:N]
            wi = c // 4
            proj_fm(wS[wi], ("wS", wi), c, hTs, N, (c % 4) * 128, 128, ps, ("ps", bank))
            ACT(zs[:, c, :], ps, AF.Silu, [("ps", bank)], [("zs",)])
        wl(0, 4608); wl(1, 5120)
        for c in range(12):
            bank = c % 2
            ps = psb[bank][:, 0:N]
            wi = (2, 0, 1)[c // 4]
            proj_fm(wS[wi], ("wS", wi), c, hTs, N, (c % 4) * 128, 128, ps, ("ps", bank))
            CP("act", xbs[:, c, :, 3:7], ps.rearrange("p (b i) -> p b i", i=4), [("ps", bank)], [("xbs", c)])
        bank = 2
        for kc in range(8):
            MM(psb[bank][0:16, 0:N], wdt[:, kc, :], hTs[:, kc, :], kc == 0, kc == 7, [("wdt",), ("hT", kc)], [("ps", bank)])
        ACT(dtT, psb[bank][0:16, 0:N], AF.Exp, [("ps", bank)], [("dtT",)], bias=pdt[:, l:l + 1])
        ACT(dtT, dtT, AF.Ln, [("dtT",)], [("dtT",)], bias=ones_f[0:16, 0:1])
        for c in range(8):
            bank = 4 + c % 2
            MM(psb[bank][:, 0:N], E_t[:, c, :], dtT, True, True, [("dtT",)], [("ps", bank)])
            CP("act", dtx[:, c, :], psb[bank][:, 0:N], [("ps", bank)], [("dtx",)])

        chk("S_proj")
        for c in range(4):
            xv = xc_s.rearrange("p (b i) -> p b i", i=4)
            TS("dve", xv, xls[:, c, :, 0:4], lcw(l, c, 0), lcb(l, c), ALU.mult, ALU.add, [("xls", c), ("xls_h",)], [("xc_s",)])
            for tap in range(1, 4):
                STT("dve", xv, xls[:, c, :, tap:tap + 4], lcw(l, c, tap), xv, ALU.mult, ALU.add,
                    [("xls", c), ("xls_h",), ("xc_s",)], [("xc_s",)])
            CP("pool", xcb_s, xc_s, [("xc_s",)], [("xcb_s",)])
            bank = 2 + (c % 2) * 2
            MM(psb[bank][:, 0:N], wbd[:, l, 0, c, :], xcb_s, True, True, [("xcb_s",)], [("ps", bank)])
            MM(psb[bank + 1][:, 0:N], wbd[:, l, 1, c, :], xcb_s, True, True, [("xcb_s",)], [("ps", bank + 1)])
            ACT(gr_s, psb[bank][:, 0:N], AF.Sigmoid, [("ps", bank)], [("gr_s",)], bias=lba(l, c))
            ACT(gi_s, psb[bank + 1][:, 0:N], AF.Sigmoid, [("ps", bank + 1)], [("gi_s",)], bias=lbx(l, c))
            ACT(ga_s, gr_s, AF.Exp, [("gr_s",)], [("ga_s",)], scale=c8[:, l, c:c + 1])
            TT("pool", gb_s, ga_s, ga_s, ALU.mult, [("ga_s",)], [("gb_s",)])
            TS("pool", gb_s, gb_s, -1.0, 1.0, ALU.mult, ALU.add, [("gb_s",)], [("gb_s",)])
            ACT(gb_s, gb_s, AF.Sqrt, [("gb_s",)], [("gb_s",)])
            TT("dve", gi_s, gi_s, xc_s, ALU.mult, [("gi_s",), ("xc_s",)], [("gi_s",)])
            TT("dve", gb_s, gb_s, gi_s, ALU.mult, [("gb_s",), ("gi_s",)], [("gb_s",)])
            gav = ga_s.rearrange("p (b i) -> p b i", i=4)
            gbv = gb_s.rearrange("p (b i) -> p b i", i=4)
            for i in range(4):
                prev = h0_s[:, c, :] if i == 0 else hs_s[:, c, :, i - 1]
                TT("dve", hs_s[:, c, :, i], gav[:, :, i], prev, ALU.mult, [("ga_s",), ("h0_s",), ("hs_s", c)], [("hs_s", c)])
                TT("dve", hs_s[:, c, :, i], hs_s[:, c, :, i], gbv[:, :, i], ALU.add, [("gb_s",), ("hs_s", c)], [("hs_s", c)])
            TT("dve", mixLs[:, c, :].rearrange("p (b i) -> p b i", i=4), hs_s[:, c, :, :],
               sgl[:, c, :].rearrange("p (b i) -> p b i", i=4), ALU.mult, [("hs_s", c), ("sgl", c)], [("mixLs",)])
        DMA(lru_conv_s[l], xls[:, :, :, 4:7], ("fs", 0), [("xls", c) for c in range(4)], [("lcs",)])
        DMA(lru_h_s[l], hs_s[:, :, :, 3], ("fs", 1), [("hs_s", c) for c in range(4)], [("lhs",)], slow=True)

        chk("S_lru")
        for c in range(12):
            xv = xa_s.rearrange("p (b i) -> p b i", i=4)
            TS("dve", xv, xbs[:, c, :, 0:4], scw(l, c, 0), scb(l, c), ALU.mult, ALU.add, [("xbs", c), ("xbs_h",)], [("xa_s",)])
            for tap in range(1, 4):
                STT("dve", xv, xbs[:, c, :, tap:tap + 4], scw(l, c, tap), xv, ALU.mult, ALU.add,
                    [("xbs", c), ("xbs_h",), ("xa_s",)], [("xa_s",)])
            if c < 8:
                ACT(xsTs[:, c, :], xa_s, AF.Silu, [("xa_s",)], [("xsTs",)])
            else:
                ACT(BCT[:, c - 8, :], xa_s, AF.Silu, [("xa_s",)], [("BCT",)])
        DMA(ssd_conv_s[l], xbs[:, :, :, 4:7], ("fs", 2), [("xbs", c) for c in range(12)], [("scs",)])
        bank = 2
        for c in range(4):
            TR(psb[bank][0:N, c * 128:(c + 1) * 128], BCT[:, c, :], ident_f, [("BCT",)], [("ps", bank)])
        CP("act", BCtok, psb[bank][0:N, :], [("ps", bank)], [("BCtok",)])
        for c in range(8):
            ACT(decs[:, c, :], dtx[:, c, :], AF.Exp, [("dtx",), ("Aexp", l)], [("decs",)], scale=Aexp[:, l, c:c + 1])
        TT("dve", xdt, xsTs, dtx, ALU.mult, [("xsTs",), ("dtx",)], [("xdt",)])

        chk("S_conv")
        pend_pv = [None]

        def pv_block(b):
            e = b % 2
            obank = 5
            for h in range(8):
                pr, par = h // 2, h % 2
                oc = psb[obank][0:65, b * 32 + h * 4: b * 32 + h * 4 + 4]
                for kt in range(7):
                    col = par * 112 + kt * 16 + pr * 4
                    MM(oc, Vb[e][:, kt, h, 0:65], Pse[e][:, col:col + 4], kt == 0, False,
                       [("Vb", e), ("Pse", e)], [("ps", 5)])
                col = par * 16 + pr * 4
                MM(oc, Vn[:, h, 0:65], Pn[e][:, col:col + 4], False, True, [("Vn",), ("Pn", e)], [("ps", 5)])

        far = lambda src, b, kt: src[l, b].rearrange("(m s) d -> m s d", s=16)[32 * kt:32 * kt + 32, 0:4, :]
        for b in range(NB):
            e = b % 2
            for kt in range(3):
                DMA(Kf[:, kt, :], far(ck, b, kt), ("Kf", kt), (), [("Kf", kt)])
            DMA(Kf[:, 3:7, :], ck[l, b, 1536:2048, :].rearrange("(t p) d -> p t d", p=128), ("Kf", 3), (), [("Kf", 3)])
            for kt in range(3):
                DMA(Vf[:, kt, :], far(cv, b, kt), ("Vf", kt), (), [("Vf", kt)])
            DMA(Vf[:, 3:7, :], cv[l, b, 1536:2048, :].rearrange("(t p) d -> p t d", p=128), ("Vf", 3), (), [("Vf", 3)])
            DMA(Hs[e], s_sh[l, b].rearrange("(c q) n -> q c n", q=128), ("Hs", e), (), [("Hs", e)])
            for kt in range(3):
                CP("pool", Kb[e][:, kt, :], Kf[:, kt, :], [("Kf", kt)], [("Kb", e)])
            CP("pool", Kb[e][:, 3:7, :], Kf[:, 3:7, :], [("Kf", 3)], [("Kb", e)])
            for kt in range(3):
                CP("dve", Vb[e][:, kt, :, 0:64], Vf[:, kt, :].rearrange("p (h d) -> p h d", h=8), [("Vf", kt), ("Vb1", e)],
                   [("Vb", e)])
            CP("dve", Vb[e][:, 3:7, :, 0:64], Vf[:, 3:7, :].rearrange("p t (h d) -> p t h d", h=8), [("Vf", 3), ("Vb1", e)],
               [("Vb", e)])
            chk("S_a1")
            for kt in range(7):
                bank = kt % 2
                pt = psb[bank].bitcast(BF16)[:, 0:512]
                for pr in range(4):
                    TR(pt[:, pr * 128:(pr + 1) * 128], Kb[e][:, kt, pr * 128:(pr + 1) * 128], ident_b, [("Kb", e)], [("ps", bank)])
                CP("act" if kt % 2 == 0 else "dve", KTs[e][:, kt, :, :].rearrange("p a k -> p (a k)"), pt, [("ps", bank)],
                   [("KTs", e)])
            chk("S_a2")
            sbanks = (2, 3)
            for par in range(2):
                sbank = sbanks[par]
                pb = par * 64
                for kt in range(7):
                    for pr in range(4):
                        col = kt * 16 + pr * 4
                        MM(psb[sbank][:, col:col + 4], KTs[e][pb:pb + 64, kt, pr, :],
                           qTs[pb:pb + 64, pr, 4 * b:4 * b + 4], True, True, [("KTs", e), ("qTs",)], [("ps", sbank)])
                for pr in range(4):
                    col = 112 + pr * 4
                    MM(psb[sbank][0:N, col:col + 4], kTs[pb:pb + 64, pr, :],
                       qTs[pb:pb + 64, pr, 4 * b:4 * b + 4], True, True, [("kTs",), ("qTs",)], [("ps", sbank)])
            chk("S_a3")
            for par in range(2):
                sbank = sbanks[par]
                ACT(Pse[e][:, par * 112:(par + 1) * 112], psb[sbank][:, 0:112], AF.Exp, [("ps", sbank)], [("Pse", e)], scale=0.125)
                ACT(Pn[e][:, par * 16:(par + 1) * 16], psb[sbank][0:N, 112:128], AF.Exp, [("ps", sbank)], [("Pn", e)], scale=0.125)
            TT("pool", Pse[e], Pse[e], ms_t, ALU.mult, [("Pse", e)], [("Pse", e)])
            TT("pool", Pn[e], Pn[e], mn_t[:, b * 32:(b + 1) * 32], ALU.mult, [("Pn", e)], [("Pn", e)])
            chk("S_a4")
            if pend_pv[0] is not None:
                pv_block(pend_pv[0])
            pend_pv[0] = b
            chk("S_att0")
            if e == 0 and b + 1 < NB:
                continue
            els = [b - 1, b] if e == 1 else [b]
            for i in range(4):
                for bx in els:
                    ex = bx % 2
                    H = Hs[ex]
                    tmpa = tmpa2[ex]
                    tmpb = tmpb2[ex]
                    tk = 4 * bx + i
                    bank = 6 + ex
                    MM(psb[bank], ident_f[0:N, tk:tk + 1].to_broadcast([N, 128]), BCtok, True, True, [("BCtok",)], [("ps", bank)])
                    CP("act", bcs[ex], psb[bank], [("ps", bank)], [("bcs", ex)])
                    Hv = H.rearrange("p (g c) n -> p g c n", g=2)
                    TT("pool", H, H, decs[:, :, tk:tk + 1].to_broadcast([128, 8, 128]), ALU.mult, [("Hs", ex), ("decs",)], [("Hs", ex)])
                    TT("pool", tmpa.rearrange("p (g c) n -> p g c n", g=2),
                       bcs[ex][:, 0:256].rearrange("p (g n) -> p g n", g=2).unsqueeze(2).to_broadcast([128, 2, 4, 128]),
                       xdt[:, :, tk:tk + 1].rearrange("p (g c) o -> p g c o", g=2).to_broadcast([128, 2, 4, 128]), ALU.mult,
                       [("bcs", ex), ("xdt",)], [("tmpa", ex)])
                    TT("dve", H, H, tmpa, ALU.add, [("Hs", ex), ("tmpa", ex)], [("Hs", ex)])
                    TT("dve", tmpb.rearrange("p (g c) n -> p g c n", g=2), Hv,
                       bcs[ex][:, 256:512].rearrange("p (g n) -> p g n", g=2).unsqueeze(2).to_broadcast([128, 2, 4, 128]), ALU.mult,
                       [("Hs", ex), ("bcs", ex)], [("tmpb", ex)])
                    RED(ysT[:, :, tk], tmpb, [("tmpb", ex)], [("ysT",)])
            for bx in els:
                ex = bx % 2
                DMA(ssd_h_s[l, bx].rearrange("(c q) n -> q c n", q=128), Hs[ex], ("Hso", ex), [("Hs", ex)], [("shs",)])

        pv_block(pend_pv[0])
        chk("S_loop")
        CP("dve", Oacc_s, psb[5][0:65, 0:NB * 32], [("ps", 5)], [("Oacc_s",)])
        MM(psb[6][0:64, 0:NB * 32], sel65[0:65, :], Oacc_s, True, True, [("Oacc_s",)], [("ps", 6)])
        RECIP(rl_s, psb[6][0:64, 0:NB * 32], [("ps", 6)], [("rl_s",)])
        TT("dve", rl_s, rl_s, Oacc_s[0:64, :], ALU.mult, [("rl_s",), ("Oacc_s",)], [("rl_s",)])
        TT("dve", mixAs.rearrange("p h (b i) -> p h b i", i=4), rl_s.rearrange("p (b h i) -> p h b i", h=8, i=4),
           sga_s.rearrange("p h (b i) -> p h b i", i=4), ALU.mult, [("rl_s",), ("sga_s",)], [("mixAs",)])

        chk("S_an")
        for c in range(8):
            STT("dve", ysT[:, c, :], xsTs[:, c, :], pps[:, l, 16 + c:17 + c], ysT[:, c, :], ALU.mult, ALU.add,
                [("xsTs",), ("ysT",)], [("ysT",)])
        TT("dve", ysT, ysT, zs, ALU.mult, [("ysT",), ("zs",)], [("ysT",)])
        TT("pool", ysq, ysT, ysT, ALU.mult, [("ysT",)], [("ysq",)])
        for g in range(2):
            for cc in range(4):
                MM(psb[7][:, g * N:(g + 1) * N], ones_f, ysq[:, g * 4 + cc, :], cc == 0, cc == 3, [("ysq",)], [("ps", 7)])
        ACT(grs.rearrange("p g n -> p (g n)"), psb[7][:, 0:2 * N], AF.Sqrt, [("ps", 7)], [("grs",)], scale=1.0 / 512,
            bias=eps_t[:, 0:1])
        RECIP(grs, grs, [("grs",)], [("grs",)])
        for c in range(8):
            STT("dve", mixSs[:, c, :], ysT[:, c, :], pps[:, l, 24 + c:25 + c], grs[:, c // 4, :], ALU.mult, ALU.mult,
                [("ysT",), ("grs",)], [("mixSs",)])

        chk("S_so")
        for half in range(2):
            hsl = slice(half * 512, (half + 1) * 512)
            DMA(woLs, wob[l, 0:512, hsl].rearrange("(c p) n -> p c n", p=128), ("wS", 0), (), [("wS", 0)])
            DMA(woAs, wob[l, 512:1024, hsl].rearrange("(h p) n -> p h n", p=64), ("wS", 2), (), [("wS", 2)])
            DMA(woSs, wob[l, 1024:2048, hsl].rearrange("(c p) n -> p c n", p=128), ("wS", 1), (), [("wS", 1)])
            bank = half
            ps = psb[bank][0:N, :]
            for c in range(4):
                MM(ps, mixLs[:, c, :], woLs[:, c, :], c == 0, False, [("mixLs",), ("wS", 0)], [("ps", bank)])
            for h in range(8):
                MM(ps, mixAs[:, h, :], woAs[:, h, :], False, False, [("mixAs",), ("wS", 2)], [("ps", bank)])
            for c in range(8):
                MM(ps, mixSs[:, c, :], woSs[:, c, :], False, c == 7, [("mixSs",), ("wS", 1)], [("ps", bank)])
            CP("act", o_s[:, hsl], ps, [("ps", bank)], [("o_s",)])
        ACT(y_s, o_s, AF.Square, [("o_s",)], [("y_s",), ("ssq_s",)], accum=ssq_s[:, 0:1])
        ACT(ssq_s, ssq_s, AF.Sqrt, [("ssq_s",)], [("ssq_s",)], scale=1.0 / D, bias=eps_t[0:N, 0:1])
        RECIP(ssq_s, ssq_s, [("ssq_s",)], [("ssq_s",)])
        STT("dve", y_s, o_s, ssq_s[:, 0:1], postg(l)[0:N, :], ALU.mult, ALU.mult, [("o_s",), ("ssq_s",), ("y_s",)], [("y_s",)])
        TT("dve", xs_res, xs_res, y_s, ALU.add, [("y_s",), ("xs_res",)], [("xs_res",)])
        if l == 1:
            DMA(ys_o, xs_res, ("ys_o",), [("xs_res",)], [("ys_o",)])
        P.barrier()
        chk("S%d" % l)

    return nc


def host_consts():
    k = np.arange(128)[:, None]
    q = np.arange(128)[None, :]
    ident = np.eye(128, dtype=np.float32)
    tri = (k <= q).astype(np.float32)
    negmask = np.where(k <= q, 0.0, -30000.0).astype(ml_dtypes.bfloat16)
    sel = np.zeros((128, 64), np.float32)
    sel[64, :] = 1.0
    mt = np.zeros((128, 8, 17, 128), np.float64)
    for h in range(8):
        slope = 2.0 ** (-(h + 1))
        for dl in range(17):
            d = 128 * dl + (q - k)
            c = ((d >= 0) & (d <= 128)).astype(np.float64) + ((d >= 0) & (d % 4 == 0) & (d <= 512)) + \
                ((d >= 0) & (d % 16 == 0) & (d <= 2048))
            mt[:, h, dl, :] = c * np.exp(-slope * np.maximum(d, 0))
    mtab = mt.reshape(128, -1).astype(ml_dtypes.bfloat16)
    return dict(c_ident=ident, c_tri=tri, c_negmask=negmask, c_mtab=mtab, c_sel=sel)


def host_params(inp):
    f = np.float32
    pp = np.zeros((128, 2, 104), f)
    pbc = np.zeros((2, 128, 2096), f)
    for l in range(2):
        pp[:, l, 0:16] = inp["lru_conv_w"][l].reshape(4, 4, 128).transpose(2, 1, 0).reshape(128, 16)
        pp[:, l, 16:20] = inp["lru_conv_b"][l].reshape(4, 128).T
        pp[:, l, 20:24] = inp["lru_b_a"][l].reshape(4, 128).T
        pp[:, l, 24:28] = inp["lru_b_x"][l].reshape(4, 128).T
        pp[:, l, 28:32] = inp["lru_lambda"][l].reshape(4, 128).T
        pp[:, l, 32:80] = inp["ssd_conv_w"][l].reshape(4, 12, 128).transpose(2, 1, 0).reshape(128, 48)
        pp[:, l, 80:92] = inp["ssd_conv_b"][l].reshape(12, 128).T
        pbc[l, :, 0:1024] = inp["post_norm_g"][l][None, :]
        pbc[l, :, 1024:2048] = inp["ssd_norm_g"][l][None, :]
        pbc[l, :, 2048:2064] = inp["ssd_dt_bias"][l][None, :]
        pbc[l, :, 2064:2080] = inp["ssd_a_log"][l][None, :]
        pbc[l, :, 2080:2096] = inp["ssd_d"][l][None, :]
    preg = np.ascontiguousarray(inp["pre_norm_g"].reshape(2, 8, 128).transpose(2, 0, 1)).astype(f)
    wbd = np.zeros((128, 2, 2, 4, 128), f)
    for l in range(2):
        for ai, nm in enumerate(("lru_w_a", "lru_w_x")):
            w = inp[nm][l]
            for c in range(4):
                wbd[0:64, l, ai, c, 0:64] = w[2 * c]
                wbd[64:128, l, ai, c, 64:128] = w[2 * c + 1]
    return dict(pp=pp, pbc=pbc, preg=preg, wbd=wbd)


def sample_consts(NB):
    ms = np.zeros((128, 7, 8, 4), np.float64)
    p = np.arange(128)
    for kt in range(7):
        if kt < 3:
            idx = 16 * (32 * kt + p // 4) + (p % 4)
        else:
            idx = 1536 + 128 * (kt - 3) + p
        for h in range(8):
            slope = 2.0 ** (-(h + 1))
            for i in range(4):
                d = 2048 + i - idx
                c = ((d >= 0) & (d <= 128)).astype(np.float64) + ((d >= 0) & (d % 4 == 0) & (d <= 512)) + \
                    ((d >= 0) & (d % 16 == 0) & (d <= 2048))
                ms[:, kt, h, i] = c * np.exp(-slope * np.maximum(d, 0))
    mn = np.zeros((NB * 4, NB, 8, 4), np.float64)
    for b in range(NB):
        for ip in range(4):
            for h in range(8):
                slope = 2.0 ** (-(h + 1))
                for i in range(ip, 4):
                    d = i - ip
                    mn[4 * b + ip, b, h, i] = (3.0 if d == 0 else 1.0) * np.exp(-slope * d)
    E = np.zeros((16, 8, 128), np.float32)
    for c in range(8):
        for hh in range(2):
            E[2 * c + hh, c, hh * 64:(hh + 1) * 64] = 1.0
    ms = ms.reshape(128, 7, 4, 2, 4).transpose(0, 3, 1, 2, 4)
    mn = mn.reshape(NB * 4, NB, 4, 2, 4).transpose(0, 1, 3, 2, 4)
    return dict(c_ms=np.ascontiguousarray(ms).reshape(128, -1).astype(ml_dtypes.bfloat16),
                c_mn=np.ascontiguousarray(mn).reshape(NB * 4, -1).astype(ml_dtypes.bfloat16), c_E=E)


def sample_params(inp):
    f = np.float32
    pdt = np.ascontiguousarray(inp["ssd_dt_bias"].T).astype(f)
    pps = np.zeros((128, 2, 32), f)
    for l in range(2):
        rep = lambda v: np.repeat(v.reshape(8, 2), 64, axis=1).T
        pps[:, l, 0:8] = rep(inp["ssd_dt_bias"][l])
        pps[:, l, 8:16] = rep(inp["ssd_a_log"][l])
        pps[:, l, 16:24] = rep(inp["ssd_d"][l])
        pps[:, l, 24:32] = inp["ssd_norm_g"][l].reshape(8, 128).T
    return dict(pdt=pdt, pps=pps)


def sample_inputs(inp, core, NB):
    f = np.float32
    sl = slice(core * NB, (core + 1) * NB)
    m = {}
    m["xs"] = np.ascontiguousarray(inp["x_sample"][sl].reshape(NB * 4, D), dtype=f)
    m["s_lc"] = np.ascontiguousarray(inp["state_lru_conv"][:, sl].reshape(2, NB, 3, 4, 128).transpose(0, 4, 3, 1, 2), dtype=f)
    m["s_lh"] = np.ascontiguousarray(inp["state_lru_h"][:, sl].reshape(2, NB, 4, 128).transpose(0, 3, 2, 1), dtype=f)
    m["s_sc"] = np.ascontiguousarray(inp["state_ssd_conv"][:, sl].reshape(2, NB, 3, 12, 128).transpose(0, 4, 3, 1, 2), dtype=f)
    m["s_sh"] = np.ascontiguousarray(inp["state_ssd_h"][:, sl].reshape(2, NB, 1024, 128), dtype=f)
    m["ck"] = np.ascontiguousarray(inp["cache_attn_k"][:, sl].reshape(2, NB, 2048, 512), dtype=f)
    m["cv"] = np.ascontiguousarray(inp["cache_attn_v"][:, sl].reshape(2, NB, 2048, 512), dtype=f)
    m.update(sample_consts(NB))
    m.update(sample_params(inp))
    return m


_NC_CACHE = {}


def kernel(**inputs):
    inp = {k: np.asarray(v) for k, v in inputs.items()}
    B, SEQ, _ = inp["x_prompt"].shape
    DB = inp["x_sample"].shape[0]
    n = 8
    NB = DB // n
    nc = bass.Bass("TRN2", target_bir_lowering=False)
    build(nc, SEQ, NB)
    shared = dict(host_consts())
    shared.update(host_params(inp))
    shared["w_in"] = np.ascontiguousarray(inp["w_in"], dtype=np.float32)
    shared["w_out"] = np.ascontiguousarray(inp["w_out"], dtype=np.float32)
    in_maps = []
    for c in range(n):
        m = dict(shared)
        m["xp"] = np.ascontiguousarray(inp["x_prompt"][c % B])
        m.update(sample_inputs(inp, c, NB))
        in_maps.append(m)
    res = run_bass_kernel_spmd(nc, in_maps, core_ids=list(range(n))).results
    o = assemble(res, inp, B, SEQ, DB, n)
    return tuple(o[k] for k in OUT_NAMES)


OUT_NAMES = ["yp", "ys", "lru_conv_p", "lru_conv_s", "lru_h_p", "lru_h_s", "k_p", "k_s", "v_p", "v_s",
             "ssd_conv_p", "ssd_conv_s", "ssd_h_p", "ssd_h_s"]


def assemble(res, inp, B, SEQ, DB, n):
    f = np.float32
    KW = min(2048, SEQ)
    yp = np.stack([res[b]["yp"] for b in range(B)]).astype(f)
    lru_conv_p = np.stack([res[b]["lru_conv_p"].transpose(0, 3, 2, 1).reshape(2, 3, 512) for b in range(B)], 1)
    lru_h_p = np.stack([res[b]["lru_h_p"].transpose(0, 2, 1).reshape(2, 512) for b in range(B)], 1)
    k_p = np.stack([res[b]["k_p"].reshape(2, KW, 8, 64) for b in range(B)], 1)
    v_p = np.stack([res[b]["v_p"].reshape(2, KW, 8, 64) for b in range(B)], 1)
    ssd_conv_p = np.stack([res[b]["ssd_conv_p"].transpose(0, 3, 2, 1).reshape(2, 3, 1536) for b in range(B)], 1)
    ssd_h_p = np.stack([res[b]["ssd_h_p"].reshape(2, 16, 64, 128) for b in range(B)], 1)
    NB = DB // n
    cat = lambda name, ax=0: np.concatenate([res[c][name] for c in range(n)], axis=ax)
    ys = cat("ys").reshape(DB, 4, D)
    lru_conv_s = np.concatenate([res[c]["lru_conv_s"].transpose(0, 3, 4, 2, 1).reshape(2, NB, 3, 512) for c in range(n)], 1)
    lru_h_s = np.concatenate([res[c]["lru_h_s"].transpose(0, 3, 2, 1).reshape(2, NB, 512) for c in range(n)], 1)
    k_s = np.concatenate([res[c]["k_s"].reshape(2, NB, 4, 8, 64) for c in range(n)], 1)
    v_s = np.concatenate([res[c]["v_s"].reshape(2, NB, 4, 8, 64) for c in range(n)], 1)
    ssd_conv_s = np.concatenate([res[c]["ssd_conv_s"].transpose(0, 3, 4, 2, 1).reshape(2, NB, 3, 1536) for c in range(n)], 1)
    ssd_h_s = np.concatenate([res[c]["ssd_h_s"].reshape(2, NB, 16, 64, 128) for c in range(n)], 1)
    outs = dict(yp=yp, ys=ys, lru_conv_p=lru_conv_p, lru_conv_s=lru_conv_s, lru_h_p=lru_h_p, lru_h_s=lru_h_s,
                k_p=k_p, k_s=k_s, v_p=v_p, v_s=v_s, ssd_conv_p=ssd_conv_p, ssd_conv_s=ssd_conv_s,
                ssd_h_p=ssd_h_p, ssd_h_s=ssd_h_s)
    return {k: np.ascontiguousarray(v, dtype=np.float32) for k, v in outs.items()}
```
